# Optimizing a Trainium2 kernel written in Bass

```python
import math
import jax, jax.numpy as jnp
from jax import lax
import numpy as np


D_MODEL = 1024
BATCH = 8
SEQ = 4096
DEPTH = 2

SSM_GROUP = 16
N_GROUPS = D_MODEL // SSM_GROUP
SSM_STATE = 64
DT_MIN = 1e-3
DT_MAX = 1e-1
N_HEADS = 8
QK_NOPE = 128
QK_ROPE = 64
V_HEAD = 128
Q_LORA = 384
KV_LORA = 256
ROPE_THETA = 10000.0
Q_BLOCK = 128
SM_SCALE = (QK_NOPE + QK_ROPE) ** -0.5
NEG_INF = -1e30
D_FF = 4 * D_MODEL
N_A_LAYERS = DEPTH // 2
N_B_LAYERS = DEPTH - N_A_LAYERS
DN_ALPHA = (2 * DEPTH) ** 0.25
DN_BETA = (8 * DEPTH) ** -0.25
LN_EPS = 1e-5
RMS_EPS = 1e-6

kernel_name = 'yoco_s5_mla_sqrelu_deepnorm'


def layer_norm(x, g, b):
    xf = x.astype(jnp.float32)
    mu = jnp.mean(xf, axis=-1, keepdims=True)
    var = jnp.mean(jnp.square(xf - mu), axis=-1, keepdims=True)
    y = (xf - mu) * lax.rsqrt(var + LN_EPS) * g.astype(jnp.float32) + b.astype(jnp.float32)
    return y.astype(x.dtype)


def rms_norm(x, g):
    xf = x.astype(jnp.float32)
    y = xf * lax.rsqrt(jnp.mean(jnp.square(xf), axis=-1, keepdims=True) + RMS_EPS) * g.astype(jnp.float32)
    return y.astype(x.dtype)


def rope_tables(positions):
    half = QK_ROPE // 2
    inv_freq = ROPE_THETA ** (-jnp.arange(half, dtype=jnp.float32) / half)
    ang = positions.astype(jnp.float32)[..., None] * inv_freq
    return jnp.cos(ang), jnp.sin(ang)


def apply_rope(x, cos, sin):
    x1, x2 = jnp.split(x.astype(jnp.float32), 2, axis=-1)
    return jnp.concatenate([x1 * cos - x2 * sin, x1 * sin + x2 * cos], axis=-1).astype(x.dtype)


def _complex_linear_combine(left, right):
    ar_l, ai_l, hr_l, hi_l = left
    ar_r, ai_r, hr_r, hi_r = right
    return (ar_r * ar_l - ai_r * ai_l,
            ar_r * ai_l + ai_r * ar_l,
            ar_r * hr_l - ai_r * hi_l + hr_r,
            ar_r * hi_l + ai_r * hr_l + hi_r)


def s5_mixer(x, lam_re, lam_im, log_dt, b_re, b_im, c_re, c_im, d_skip, w_glu, w_out):
    f32 = jnp.float32
    bsz, seq, _ = x.shape
    u = x.astype(f32).reshape(bsz, seq, N_GROUPS, SSM_GROUP)
    lr = lam_re.astype(f32)
    li = lam_im.astype(f32)
    dt = jnp.exp(log_dt.astype(f32))[:, None]
    mag = jnp.exp(lr * dt)
    a_re = mag * jnp.cos(li * dt)
    a_im = mag * jnp.sin(li * dt)
    inv_den = 1.0 / (lr * lr + li * li)
    coef_re = ((a_re - 1.0) * lr + a_im * li) * inv_den
    coef_im = (a_im * lr - (a_re - 1.0) * li) * inv_den
    br = b_re.astype(f32)
    bi = b_im.astype(f32)
    bb_re = coef_re[..., None] * br - coef_im[..., None] * bi
    bb_im = coef_re[..., None] * bi + coef_im[..., None] * br
    bu_re = jnp.einsum('bsgc,gpc->bsgp', u, bb_re)
    bu_im = jnp.einsum('bsgc,gpc->bsgp', u, bb_im)
    shape_a = (1, seq, N_GROUPS, SSM_STATE)
    a_re_t = jnp.broadcast_to(a_re, shape_a)
    a_im_t = jnp.broadcast_to(a_im, shape_a)
    _, _, h_re, h_im = lax.associative_scan(
        _complex_linear_combine, (a_re_t, a_im_t, bu_re, bu_im), axis=1)
    y = (jnp.einsum('bsgp,gcp->bsgc', h_re, c_re.astype(f32))
         - jnp.einsum('bsgp,gcp->bsgc', h_im, c_im.astype(f32)))
    y = y + d_skip.astype(f32).reshape(N_GROUPS, SSM_GROUP) * u
    y = jax.nn.gelu(y.reshape(bsz, seq, D_MODEL)).astype(x.dtype)
    val, gate = jnp.split(y @ w_glu, 2, axis=-1)
    return (val * jax.nn.sigmoid(gate)) @ w_out


def mla_shared_kv(h, kv_w_a, kv_norm_g, kv_w_b, cos, sin):
    bsz, seq, _ = h.shape
    c_kv, k_rope = jnp.split(h @ kv_w_a, [KV_LORA], axis=-1)
    c_kv = rms_norm(c_kv, kv_norm_g)
    k_rope = apply_rope(k_rope, cos, sin)
    kv = (c_kv @ kv_w_b).reshape(bsz, seq, N_HEADS, QK_NOPE + V_HEAD)
    k_nope, v = jnp.split(kv, [QK_NOPE], axis=-1)
    return k_nope, k_rope, v


def mla_mixer(h, q_w_a, q_norm_g, q_w_b, w_o, k_nope, k_rope, v, cos, sin):
    bsz, seq, _ = h.shape
    c_q = rms_norm(h @ q_w_a, q_norm_g)
    q = (c_q @ q_w_b).reshape(bsz, seq, N_HEADS, QK_NOPE + QK_ROPE)
    q_nope, q_rope = jnp.split(q, [QK_NOPE], axis=-1)
    q_rope = apply_rope(q_rope, cos[:, :, None, :], sin[:, :, None, :])
    n_blocks = seq // Q_BLOCK

    def to_blocks(t):
        return t.reshape(bsz, n_blocks, Q_BLOCK, *t.shape[2:]).swapaxes(0, 1)

    key_pos = jnp.arange(seq)

    def attend_block(args):
        blk, qn, qr = args
        s = (jnp.einsum('bqhd,bkhd->bhqk', qn, k_nope, preferred_element_type=jnp.float32)
             + jnp.einsum('bqhr,bkr->bhqk', qr, k_rope, preferred_element_type=jnp.float32)) * SM_SCALE
        q_pos = blk * Q_BLOCK + jnp.arange(Q_BLOCK)
        s = jnp.where(key_pos[None, :] <= q_pos[:, None], s, NEG_INF)
        p = jax.nn.softmax(s, axis=-1).astype(v.dtype)
        return jnp.einsum('bhqk,bkhd->bqhd', p, v)

    o = lax.map(attend_block, (jnp.arange(n_blocks), to_blocks(q_nope), to_blocks(q_rope)))
    o = o.swapaxes(0, 1).reshape(bsz, seq, N_HEADS * V_HEAD)
    return o @ w_o


def sq_relu_mlp(h, w1, w2):
    return jnp.square(jax.nn.relu(h @ w1)) @ w2


def setup_inputs(seed: int = 0) -> dict:
    key = jax.random.key(seed)
    k = jax.random.split(key, 24)
    f32 = jnp.float32

    def nrm(i, shape, scale):
        return jax.random.normal(k[i], shape, f32) * scale

    n_idx = jnp.arange(SSM_STATE, dtype=f32)
    return {
        'x': nrm(0, (BATCH, SEQ, D_MODEL), 1.0),
        'positions': jnp.broadcast_to(jnp.arange(SEQ, dtype=jnp.int32), (BATCH, SEQ)),
        'ln_mix_g': 1.0 + nrm(1, (DEPTH, D_MODEL), 0.02),
        'ln_mix_b': nrm(2, (DEPTH, D_MODEL), 0.02),
        'ln_ffn_g': 1.0 + nrm(3, (DEPTH, D_MODEL), 0.02),
        'ln_ffn_b': nrm(4, (DEPTH, D_MODEL), 0.02),
        'w_ff1': nrm(5, (DEPTH, D_MODEL, D_FF), D_MODEL ** -0.5),
        'w_ff2': nrm(6, (DEPTH, D_FF, D_MODEL), D_FF ** -0.5 * DN_BETA),
        'ssm_lam_re': -0.5 + nrm(7, (N_A_LAYERS, N_GROUPS, SSM_STATE), 0.01),
        'ssm_lam_im': math.pi * n_idx + nrm(8, (N_A_LAYERS, N_GROUPS, SSM_STATE), 0.01),
        'ssm_log_dt': jax.random.uniform(k[9], (N_A_LAYERS, N_GROUPS), f32, math.log(DT_MIN), math.log(DT_MAX)),
        'ssm_b_re': nrm(10, (N_A_LAYERS, N_GROUPS, SSM_STATE, SSM_GROUP), (2 * SSM_GROUP) ** -0.5),
        'ssm_b_im': nrm(11, (N_A_LAYERS, N_GROUPS, SSM_STATE, SSM_GROUP), (2 * SSM_GROUP) ** -0.5),
        'ssm_c_re': nrm(12, (N_A_LAYERS, N_GROUPS, SSM_GROUP, SSM_STATE), SSM_STATE ** -0.5),
        'ssm_c_im': nrm(13, (N_A_LAYERS, N_GROUPS, SSM_GROUP, SSM_STATE), SSM_STATE ** -0.5),
        'ssm_d': nrm(14, (N_A_LAYERS, D_MODEL), 1.0),
        'ssm_w_glu': nrm(15, (N_A_LAYERS, D_MODEL, 2 * D_MODEL), D_MODEL ** -0.5),
        'ssm_w_out': nrm(16, (N_A_LAYERS, D_MODEL, D_MODEL), D_MODEL ** -0.5 * DN_BETA),
        'kv_w_a': nrm(17, (D_MODEL, KV_LORA + QK_ROPE), D_MODEL ** -0.5),
        'kv_norm_g': 1.0 + nrm(18, (KV_LORA,), 0.02),
        'kv_w_b': nrm(19, (KV_LORA, N_HEADS * (QK_NOPE + V_HEAD)), KV_LORA ** -0.5),
        'q_w_a': nrm(20, (N_B_LAYERS, D_MODEL, Q_LORA), D_MODEL ** -0.5),
        'q_norm_g': 1.0 + nrm(21, (N_B_LAYERS, Q_LORA), 0.02),
        'q_w_b': nrm(22, (N_B_LAYERS, Q_LORA, N_HEADS * (QK_NOPE + QK_ROPE)), Q_LORA ** -0.5),
        'attn_w_o': nrm(23, (N_B_LAYERS, N_HEADS * V_HEAD, D_MODEL), (N_HEADS * V_HEAD) ** -0.5 * DN_BETA),
    }


def reference(x, positions, ln_mix_g, ln_mix_b, ln_ffn_g, ln_ffn_b, w_ff1, w_ff2,
              ssm_lam_re, ssm_lam_im, ssm_log_dt, ssm_b_re, ssm_b_im, ssm_c_re, ssm_c_im,
              ssm_d, ssm_w_glu, ssm_w_out, kv_w_a, kv_norm_g, kv_w_b,
              q_w_a, q_norm_g, q_w_b, attn_w_o):
    cos, sin = rope_tables(positions)
    h = x
    k_nope = k_rope = v = None
    for layer in range(DEPTH):
        if layer < N_A_LAYERS:
            i = layer
            mix = s5_mixer(h, ssm_lam_re[i], ssm_lam_im[i], ssm_log_dt[i], ssm_b_re[i], ssm_b_im[i],
                           ssm_c_re[i], ssm_c_im[i], ssm_d[i], ssm_w_glu[i], ssm_w_out[i])
        else:
            if layer == N_A_LAYERS:
                k_nope, k_rope, v = mla_shared_kv(h, kv_w_a, kv_norm_g, kv_w_b, cos, sin)
            j = layer - N_A_LAYERS
            mix = mla_mixer(h, q_w_a[j], q_norm_g[j], q_w_b[j], attn_w_o[j], k_nope, k_rope, v, cos, sin)
        h = layer_norm(DN_ALPHA * h + mix, ln_mix_g[layer], ln_mix_b[layer])
        h = layer_norm(DN_ALPHA * h + sq_relu_mlp(h, w_ff1[layer], w_ff2[layer]), ln_ffn_g[layer], ln_ffn_b[layer])
    return h
```

```python
import math
import os as _os
from contextlib import ExitStack

import numpy as np
import concourse.bass as bass
import concourse.mybir as mybir
from concourse.bass_utils import run_bass_kernel_spmd

F32 = mybir.dt.float32
BF16 = mybir.dt.bfloat16
I32 = mybir.dt.int32
AF = mybir.ActivationFunctionType
ALU = mybir.AluOpType
AX = mybir.AxisListType

S = 4096
D = 1024
NT = S // 128
DFF = 4096
PAD = 8
ALPHA = 4.0 ** 0.25
LN_EPS = 1e-5
RMS_EPS = 1e-6
SM_SCALE = 192.0 ** -0.5
TWO_PI = 2.0 * math.pi
KR_ENG = _os.environ.get("KR_ENG", "pool")
ATTACH_WAIT = _os.environ.get("ATTACH_WAIT", "1") == "1"


def sl(start, n, step=1):
    return slice(start, start + (n - 1) * step + 1, step)


import heapq


class _Op:
    __slots__ = ("eng", "fn", "deps", "odeps", "needs_inc", "is_dma", "dsem", "dval", "sem_idx", "count", "idx",
                 "cost", "nbytes", "seg", "start", "fin", "nsucc", "succ", "est", "bar_waits")

    def __init__(self, eng, fn, is_dma=False):
        self.eng = eng
        self.fn = fn
        self.deps = []
        self.odeps = []
        self.needs_inc = False
        self.is_dma = is_dma
        self.dsem = None
        self.dval = 0
        self.sem_idx = 0
        self.count = 0
        self.cost = 0.1
        self.nbytes = 0
        self.bar_waits = None


class Sched:
    ENGS = ("sp", "act", "dve", "pool", "pe")
    LIMIT = 30000
    XLAT = 2.0
    DMA_LAT = 2.0
    DMA_BW = 160e3

    def __init__(self, n_dma_sems=int(_os.environ.get("NDMA", "24")), reorder=True):
        self.segs = [[]]
        self.lw = {}
        self.rd = {}
        self.n_dma = n_dma_sems
        nsw = 8
        self.dma_pool = {"sp": list(range(0, n_dma_sems - nsw)), "act": list(range(0, n_dma_sems - nsw)),
                         "pool": list(range(n_dma_sems - nsw, n_dma_sems))}
        self.nops = 0
        self.reorder = reorder

    def _deps(self, o, reads, writes):
        extra = [k for k in reads if isinstance(k, tuple) and k and k[0] == "ps" and k not in writes]
        if extra:
            writes = list(writes) + extra
        deps = {}
        for k in reads:
            w = self.lw.get(k)
            if w is not None:
                deps[id(w)] = w
        for k in writes:
            w = self.lw.get(k)
            if w is not None:
                deps[id(w)] = w
            for r in self.rd.get(k, ()):
                deps[id(r)] = r
        for d in deps.values():
            if d is o:
                continue
            if (not d.is_dma) and d.eng == o.eng and (not o.is_dma) and o.eng == "pe":
                o.odeps.append(d)
                continue
            if not d.is_dma:
                d.needs_inc = True
            o.deps.append(d)
        for k in reads:
            self.rd.setdefault(k, []).append(o)
        for k in writes:
            self.lw[k] = o
            self.rd[k] = []

    def op(self, eng, fn, reads=(), writes=(), cost=0.1):
        o = _Op(eng, fn)
        o.cost = cost
        o.idx = self.nops
        self.nops += 1
        self._deps(o, reads, writes)
        self.segs[-1].append(o)
        return o

    def dma(self, eng, fn, reads=(), writes=(), nbytes=0):
        o = _Op(eng, fn, is_dma=True)
        o.nbytes = nbytes
        o.cost = 0.06 if eng != "pool" else 1.0
        o.idx = self.nops
        self.nops += 1
        self._deps(o, reads, writes)
        self.segs[-1].append(o)
        return o

    def barrier(self):
        if self.segs[-1]:
            self.segs.append([])
        self.lw = {}
        self.rd = {}

    def _schedule_segment(self, seg, t0):
        order = {e: [] for e in self.ENGS}
        if not seg:
            return order, t0
        inseg = set(id(o) for o in seg)
        for o in seg:
            o.succ = []
            o.nsucc = 0
            o.est = t0
        for o in seg:
            for d in o.deps + o.odeps:
                if id(d) in inseg:
                    d.succ.append(o)
                    o.nsucc += 1
        if not self.reorder:
            t = {e: t0 for e in self.ENGS}
            tend = t0
            for o in seg:
                order[o.eng].append(o)
            return order, tend
        bl = {}
        for o in reversed(seg):
            c = o.cost + (self.DMA_LAT + o.nbytes / self.DMA_BW if o.is_dma else 0.0)
            m_ = 0.0
            for s_ in o.succ:
                v = bl[id(s_)] + (0.0 if (s_.eng == o.eng and not o.is_dma) else self.XLAT)
                if v > m_:
                    m_ = v
            bl[id(o)] = c + m_
        PRI = "bl"
        for o in seg:
            o.idx = (-bl[id(o)], o.idx) if PRI == "bl" else o.idx
        hest = {e: [] for e in self.ENGS}
        hidx = {e: [] for e in self.ENGS}
        free = {e: t0 for e in self.ENGS}
        for o in seg:
            if o.nsucc == 0:
                heapq.heappush(hest[o.eng], (o.est, o.idx, o))
        dma_free = t0
        remaining = len(seg)
        tend = t0
        while remaining:
            best = None
            for e in self.ENGS:
                he, hi = hest[e], hidx[e]
                while he and he[0][0] <= free[e]:
                    _, ix, oo = heapq.heappop(he)
                    heapq.heappush(hi, (ix, oo))
                if hi:
                    st, ix = free[e], hi[0][0]
                elif he:
                    st, ix = he[0][0], he[0][1]
                else:
                    continue
                if best is None or (st, ix) < best[:2]:
                    best = (st, ix, e)
            st, ix, e = best
            if hidx[e]:
                _, o = heapq.heappop(hidx[e])
            else:
                _, _, o = heapq.heappop(hest[e])
            o.start = st
            if o.is_dma:
                issue_done = st + o.cost
                ts = max(issue_done, dma_free)
                done = ts + o.nbytes / self.DMA_BW
                dma_free = done
                o.fin = done + self.DMA_LAT
                free[e] = issue_done
            else:
                o.fin = st + o.cost
                free[e] = o.fin
            tend = max(tend, o.fin)
            order[e].append(o)
            remaining -= 1
            for s_ in o.succ:
                lat = 0.0 if (s_.eng == o.eng and not o.is_dma) else self.XLAT
                if o.fin + lat > s_.est:
                    s_.est = o.fin + lat
                s_.nsucc -= 1
                if s_.nsucc == 0:
                    heapq.heappush(hest[s_.eng], (s_.est, s_.idx, s_))
        return order, tend

    def finalize(self):
        if self.segs[-1]:
            self.segs.append([])
        t0 = 0.0
        self.eng_ops = {e: [] for e in self.ENGS}
        seg_orders = []
        for seg in self.segs:
            order, t0 = self._schedule_segment(seg, t0)
            seg_orders.append(order)
        self.est_total_us = t0
        cnt = {e: 0 for e in self.ENGS}
        si = {e: 0 for e in self.ENGS}
        dma_cnt = [0] * self.n_dma
        dma_rr = {"sp": 0, "act": 0, "pool": 0}
        for sidx, order in enumerate(seg_orders):
            if sidx > 0:
                prev = seg_orders[sidx - 1]
                lasts = []
                for e in self.ENGS:
                    for o in reversed(self.eng_ops[e]):
                        if o.fn is not None and not o.is_dma:
                            lasts.append(o)
                            break
                bw = {}
                for o in lasts:
                    if not o.needs_inc:
                        o.needs_inc = True
                        if cnt[o.eng] >= self.LIMIT:
                            si[o.eng] += 1
                            cnt[o.eng] = 0
                        cnt[o.eng] += 1
                        o.count = cnt[o.eng]
                        o.sem_idx = si[o.eng]
                    bw[(o.eng, o.sem_idx)] = o.count
                for s_ in range(self.n_dma):
                    if dma_cnt[s_]:
                        bw[("d", s_)] = dma_cnt[s_]
                for e in self.ENGS:
                    b = _Op(e, None)
                    b.bar_waits = dict(bw)
                    self.eng_ops[e].append(b)
            for e in self.ENGS:
                for o in order[e]:
                    if o.is_dma:
                        pool = self.dma_pool[e]
                        rrk = "sp" if e == "act" else e
                        s_ = pool[dma_rr[rrk] % len(pool)]
                        dma_rr[rrk] += 1
                        o.bar_waits = {("d", s_): dma_cnt[s_]} if dma_cnt[s_] else None
                        dma_cnt[s_] += 16
                        o.dsem = s_
                        o.dval = dma_cnt[s_]
                    elif o.needs_inc:
                        if cnt[e] >= self.LIMIT:
                            si[e] += 1
                            cnt[e] = 0
                        cnt[e] += 1
                        o.count = cnt[e]
                        o.sem_idx = si[e]
                    self.eng_ops[e].append(o)
        self.nsem = {e: si[e] + 1 for e in self.ENGS}

    def emit(self, nc, block, es):
        self.finalize()
        esem = {e: [es.enter_context(nc.semaphore(f"s_{e}_{i}")) for i in range(self.nsem[e])] for e in self.ENGS}
        dsem = [es.enter_context(nc.semaphore(f"s_dma_{i}")) for i in range(self.n_dma)]

        def run(e, eng):
            waited = {}
            for o in self.eng_ops[e]:
                need = {}
                if o.bar_waits:
                    need.update(o.bar_waits)
                for d in o.deps:
                    if d.is_dma:
                        key = ("d", d.dsem)
                        val = d.dval
                    else:
                        key = (d.eng, d.sem_idx)
                        val = d.count
                    if val > need.get(key, 0):
                        need[key] = val
                todo = []
                for key, val in need.items():
                    if waited.get(key, 0) >= val:
                        continue
                    waited[key] = val
                    sem = dsem[key[1]] if key[0] == "d" else esem[key[0]][key[1]]
                    todo.append((1 if key[0] == "d" else 0, sem, val))
                todo.sort(key=lambda t_: t_[0])
                attach = None
                if todo and o.fn is not None and ATTACH_WAIT:
                    attach = todo.pop()
                for _, sem, val in todo:
                    eng.wait_ge(sem, val)
                if o.fn is None:
                    continue
                ins = o.fn(eng)
                if attach is not None:
                    ins._wait_ge(attach[1], attach[2])
                if o.is_dma:
                    ins.then_inc(dsem[o.dsem], 16)
                elif o.needs_inc:
                    ins.then_inc(esem[e][o.sem_idx], 1)

        @block.sync
        def _(eng):
            run("sp", eng)

        @block.scalar
        def _(eng):
            run("act", eng)

        @block.vector
        def _(eng):
            run("dve", eng)

        @block.gpsimd
        def _(eng):
            run("pool", eng)

        @block.tensor
        def _(eng):
            run("pe", eng)


class Arena:
    def __init__(self, nc, nbytes):
        self.nc = nc
        self.words = nbytes // 4
        self.t = nc.alloc_sbuf_tensor("arena", [128, self.words], F32)
        self.off = 0

    def alloc(self, shape, dtype=F32):
        n = 1
        for s_ in shape:
            n *= s_
        bpe = 4 if dtype in (F32, I32) else 2
        nw = (n * bpe + 3) // 4
        nw = (nw + 7) // 8 * 8
        assert self.off + nw <= self.words, f"SBUF arena overflow: need {self.off + nw} words of {self.words}"
        v = self.t[:, self.off:self.off + nw]
        self.off += nw
        if dtype != F32:
            v = v.bitcast(dtype)
        v = v[:, 0:n]
        if len(shape) == 2:
            v = v.rearrange("p (a b) -> p a b", a=shape[0])
        elif len(shape) == 3:
            v = v.rearrange("p (a b c) -> p a b c", a=shape[0], b=shape[1])
        elif len(shape) == 4:
            v = v.rearrange("p (a b c d) -> p a b c d", a=shape[0], b=shape[1], c=shape[2])
        return v

    def mark(self):
        return self.off

    def reset(self, m):
        self.off = m


class Builder:
    def __init__(self, stop_after=None, debug=()):
        self.stop_after = stop_after
        self.debug = set(debug)
        self.nc = bass.Bass("TRN2", target_bir_lowering=False)
        self.sc = Sched()
        self.es = ExitStack()
        nc = self.nc
        self.ar = Arena(nc, 212480)
        self.ps = [nc.alloc_psum_tensor(f"psb{i}", [128, 512], F32) for i in range(8)]
        self.dram = {}
        self.uid = 0

    def din(self, name, shape, dtype=F32):
        t = self.nc.dram_tensor(name, list(shape), dtype, kind="ExternalInput").ap()
        self.dram[name] = t
        return t

    def dout(self, name, shape, dtype=F32):
        t = self.nc.dram_tensor(name, list(shape), dtype, kind="ExternalOutput").ap()
        self.dram[name] = t
        return t

    def dscr(self, name, shape, dtype=F32):
        t = self.nc.dram_tensor(name, list(shape), dtype, kind="Internal").ap()
        self.dram[name] = t
        return t

    def alloc(self, shape, dtype=F32, name=None):
        self.uid += 1
        return T(self.ar.alloc(shape, dtype), (name or "t", self.uid))

    def bank(self, i):
        return T(self.ps[i][:, :], ("ps", i))

    @staticmethod
    def _ap(x):
        return x.ap if isinstance(x, T) else x

    @staticmethod
    def _keys(*xs):
        return [x.key for x in xs if isinstance(x, T)]

    @staticmethod
    def _n(ap):
        n = 1
        for s_ in ap.shape[1:]:
            n *= s_
        return n

    @staticmethod
    def _bpe(ap):
        return 4 if ap.dtype in (F32, I32) else 2

    def _ecost(self, eng, n, mult=1.0):
        if eng == "act":
            return 0.2 + n / 1400.0
        if eng == "dve":
            return 0.08 + mult * n / 960.0
        return 0.25 + mult * n / 500.0

    def dma(self, out, in_, eng="sp", reads_extra=(), **kw):
        o, i = self._ap(out), self._ap(in_)
        nb = self._n(o) * o.shape[0] * self._bpe(o)
        return self.sc.dma(eng, lambda e: e.dma_start(out=o, in_=i, **kw), self._keys(in_, *reads_extra),
                           self._keys(out), nbytes=nb)

    def mm(self, out, lhsT, rhs, start=True, stop=True, **kw):
        o, l, r = out.ap, lhsT.ap, rhs.ap
        c = 0.02 + max(self._n(r), 64) / 1950.0
        if l.dtype == F32:
            c *= 4
        return self.sc.op("pe", lambda e: e.matmul(o, l, r, start=start, stop=stop, **kw),
                          self._keys(lhsT, rhs), self._keys(out), cost=c)

    def tr(self, out, in_, ident, **kw):
        o, i, d = out.ap, in_.ap, ident.ap
        return self.sc.op("pe", lambda e: e.transpose(o, i, d, **kw), self._keys(in_, ident), self._keys(out),
                          cost=0.12)

    def act(self, out, in_, func, scale=1.0, bias=0.0, accum=None):
        o, i = out.ap, in_.ap
        kw = {}
        rd = self._keys(in_, scale, bias)
        wr = self._keys(out)
        if accum is not None:
            kw["accum_out"] = accum.ap
            wr += self._keys(accum)
        sc_, bi_ = self._ap(scale), self._ap(bias)
        return self.sc.op("act", lambda e: e.activation(out=o, in_=i, func=func, scale=sc_, bias=bi_, **kw), rd, wr,
                          cost=self._ecost("act", self._n(i)))

    def tt(self, eng, out, a, b, op):
        o, x, y = out.ap, a.ap, b.ap
        return self.sc.op(eng, lambda e: e.tensor_tensor(out=o, in0=x, in1=y, op=op), self._keys(a, b), self._keys(out),
                          cost=self._ecost(eng, self._n(o)))

    def ts(self, eng, out, a, s1, op0, s2=None, op1=None):
        o, x = out.ap, a.ap
        p1, p2 = self._ap(s1), self._ap(s2)
        kw = {} if op1 is None else {"op1": op1}
        return self.sc.op(eng, lambda e: e.tensor_scalar(out=o, in0=x, scalar1=p1, scalar2=p2, op0=op0, **kw),
                          self._keys(a, s1, s2), self._keys(out), cost=self._ecost(eng, self._n(o)))

    def stt(self, eng, out, a, scalar, b, op0, op1):
        o, x, y = out.ap, a.ap, b.ap
        s_ = self._ap(scalar)
        return self.sc.op(eng, lambda e: e.scalar_tensor_tensor(out=o, in0=x, scalar=s_, in1=y, op0=op0, op1=op1),
                          self._keys(a, scalar, b), self._keys(out), cost=self._ecost(eng, self._n(o)))

    def cp(self, eng, out, a):
        o, x = out.ap, a.ap
        c = self._ecost(eng, self._n(o))
        if eng == "act":
            return self.sc.op("act", lambda e: e.activation(out=o, in_=x, func=AF.Copy), self._keys(a), self._keys(out),
                              cost=c)
        return self.sc.op(eng, lambda e: e.tensor_copy(out=o, in_=x), self._keys(a), self._keys(out), cost=c)

    def memset(self, eng, out, val):
        o = out.ap
        return self.sc.op(eng, lambda e: e.memset(o, val), [], self._keys(out), cost=self._ecost(eng, self._n(o)))

    def gen(self, eng, name, reads, writes, **kw):
        kw2 = {k: self._ap(v_) for k, v_ in kw.items()}
        n = self._n(kw2["out"]) if "out" in kw2 else 64
        mult = 2.0 if name == "tensor_tensor_scan" else 1.0
        return self.sc.op(eng, lambda e: getattr(e, name)(**kw2), self._keys(*reads), self._keys(*writes),
                          cost=self._ecost(eng, n, mult))


class T:
    __slots__ = ("ap", "key")

    def __init__(self, ap, key):
        self.ap = ap
        self.key = key

    def __getitem__(self, idx):
        return T(self.ap[idx], self.key)

    def k(self, *suffix):
        return T(self.ap, (self.key,) + tuple(suffix))

    def re(self, pattern, **kw):
        return T(self.ap.rearrange(pattern, **kw), self.key)

    def bc(self, shape):
        ap = self.ap
        while len(ap.shape) < len(shape):
            ap = ap.unsqueeze(len(ap.shape))
        return T(ap.broadcast_to(list(shape)), self.key)

    def cast(self, dt_):
        return T(self.ap.bitcast(dt_), self.key)

    def ubc(self, axis, shape):
        return T(self.ap.unsqueeze(axis).broadcast_to(list(shape)), self.key)


class Prog(Builder):
    def declare_io(self):
        self.x = self.din("x", [S, D])
        self.pos = self.din("positions", [NT, 128], I32)
        for n, shp in (("ln_mix_g", [2, D]), ("ln_mix_b", [2, D]), ("ln_ffn_g", [2, D]), ("ln_ffn_b", [2, D]),
                       ("w_ff1", [2, D, DFF]), ("w_ff2", [2, DFF, D]),
                       ("ssm_lam_re", [64, 64]), ("ssm_lam_im", [64, 64]), ("ssm_log_dt", [1, 64]),
                       ("ssm_b_re", [64, 64, 16]), ("ssm_b_im", [64, 64, 16]),
                       ("ssm_c_re", [64, 16, 64]), ("ssm_c_im", [64, 16, 64]), ("ssm_d", [D]),
                       ("ssm_w_glu", [D, 2 * D]), ("ssm_w_out", [D, D]),
                       ("kv_w_a", [D, 320]), ("kv_norm_g", [1, 256]), ("kv_w_b", [256, 2048]),
                       ("q_w_a", [D, 384]), ("q_norm_g", [1, 384]), ("q_w_b", [384, 1536]),
                       ("attn_w_o", [D, D])):
            self.din(n, shp)
        self.out = self.dout("out", [S, D])

    def consts(self):
        self.identf = self.alloc([128], F32, "identf")
        self.identb = self.alloc([128], BF16, "identb")
        self.memset("pool", self.identf, 0.0)
        self.gen("pool", "affine_select", [self.identf], [self.identf], out=self.identf, in_=self.identf,
                 pattern=[[-1, 128]], compare_op=ALU.not_equal, fill=1.0, base=0, channel_multiplier=1)
        self.cp("pool", self.identb, self.identf)

    def phase_a0(self):
        m = self.ar.mark()
        xs = [self.alloc([D], F32, "xs") for _ in range(3)]
        if _os.environ.get("DUMMY_DMA"):
            dmy = self.alloc([D], F32, "dmy")
            for _ in range(int(_os.environ["DUMMY_DMA"])):
                self.dma(dmy, self.x[0:128, :])
        for tt in range(NT):
            slot = xs[tt % 3]
            self.dma(slot, self.x[tt * 128:(tt + 1) * 128, :], reads_extra=[dmy] if _os.environ.get("DUMMY_DMA") and tt == 0 else ())
            if _os.environ.get("A0_VIA_DVE"):
                if tt == 0:
                    xs2 = [self.alloc([D], F32, "xs2") for _ in range(3)]
                self.cp("pool", xs2[tt % 3], slot)
                slot = xs2[tt % 3]
            for half in range(2):
                bank = self.bank((tt % 2) * 2 + half)
                for j in range(4):
                    kt = half * 4 + j
                    self.tr(bank[:, j * 128:(j + 1) * 128], slot[:, kt * 128:(kt + 1) * 128], self.identf)
                o = self.UT[:, half * 4:half * 4 + 4, :, tt * 16:(tt + 1) * 16].k(half, tt).re("p a s j -> p a j s")
                i = bank.re("p (a j s) -> p a j s", a=4, s=8)
                self.cp("act" if half == 0 else "dve", o, i)
        self.sc.barrier()
        self.ar.reset(m)

    def phase_a1(self):
        dr = self.dram
        A = self.alloc
        self.UT = A([8, 8, 512], BF16, "UT")
        self.ut_top = self.ar.mark()
        self.ZupW = [A([8, 8, 128], BF16, "zupw_re"), A([8, 8, 128], BF16, "zupw_im")]
        self.CarW = [A([32, 8, 32], BF16, "carw_re"), A([32, 8, 32], BF16, "carw_nim")]
        self.BD = A([8, 8, 128], BF16, "bd")
        self.Rch = A([32], F32, "rch")
        self.f8 = A([32], F32, "f8")
        self.Dsk = A([8], F32, "dsk")
        m = self.ar.mark()
        sm = lambda n: A([32], F32, n)
        CIN = [A([8, 128], F32, "cin_re"), A([8, 128], F32, "cin_im")]
        for ci, nm in enumerate(("ssm_c_re", "ssm_c_im")):
            self.memset("pool", CIN[ci], 0.0)
            v = dr[nm].rearrange("(kt qq g) c p -> qq g c kt p", qq=4, g=2)
            for qq in range(4):
                for g2 in range(2):
                    p0 = qq * 32 + g2 * 16
                    self.dma(CIN[ci][p0:p0 + 16, :, g2 * 64:(g2 + 1) * 64], v[qq, g2])
        ldt = sm("ldt")
        for g2 in range(2):
            src = dr["ssm_log_dt"][0, g2::2].partition_broadcast(64)
            self.dma(ldt[g2 * 64:(g2 + 1) * 64, :], src, allow_slow_non_contiguous=True)
        self.dma(self.Dsk, dr["ssm_d"].rearrange("(k p) -> p k", p=128), allow_slow_non_contiguous=True)
        Bbr, Bbi = A([32, 16], F32, "Bbr"), A([32, 16], F32, "Bbi")
        BbBD = [A([32, 32], F32, "bbbd_re"), A([32, 32], F32, "bbbd_im")]
        lr, li = sm("lr"), sm("li")
        m0, m1 = A([1], F32, "m0"), A([1], F32, "m1")
        self.memset("pool", m0, 0.0)
        self.memset("pool", m0[0:64], 1.0)
        self.memset("pool", m1, 0.0)
        self.memset("pool", m1[64:128], 1.0)
        bdm = A([128], F32, "bdm")
        self.memset("pool", bdm, 1.0)
        for i in range(4):
            blk = bdm[:, 32 * i:32 * i + 32]
            self.gen("pool", "affine_select", [bdm], [bdm], out=blk, in_=blk, pattern=[[0, 32]],
                     compare_op=ALU.is_ge, fill=0.0, base=-32 * i, channel_multiplier=1)
            self.gen("pool", "affine_select", [bdm], [bdm], out=blk, in_=blk, pattern=[[0, 32]],
                     compare_op=ALU.is_ge, fill=0.0, base=32 * i + 31, channel_multiplier=-1)
        cre, cim = sm("cre"), sm("cim")
        AR = [sm(f"ar{k}") for k in range(9)]
        AI = [sm(f"ai{k}") for k in range(9)]
        m_short = self.ar.mark()
        Lin = A([256], F32, "Lin")
        self.memset("pool", Lin, 0.0)
        self.dma(Lin[0:32, 0:128], dr["ssm_lam_re"].rearrange("(q g) p -> q (g p)", g=2))
        self.dma(Lin[0:32, 128:256], dr["ssm_lam_im"].rearrange("(q g) p -> q (g p)", g=2))
        Br, Bi = A([32, 16], F32, "Br"), A([32, 16], F32, "Bi")
        for g2 in range(2):
            self.dma(Br[g2 * 64:(g2 + 1) * 64], dr["ssm_b_re"].rearrange("(q g) p c -> g p q c", g=2)[g2])
            self.dma(Bi[g2 * 64:(g2 + 1) * 64], dr["ssm_b_im"].rearrange("(q g) p c -> g p q c", g=2)[g2])
        bk = self.bank(0)
        self.tr(bk[:, 0:32], Lin[0:32, 0:128], self.identf[0:32, 0:32])
        self.tr(bk[:, 32:64], Lin[0:32, 128:256], self.identf[0:32, 0:32])
        self.cp("dve", lr, bk[:, 0:32])
        self.cp("dve", li, bk[:, 32:64])
        dt, xr, mag, ang, trn, trc = sm("dt"), sm("xr"), sm("mag"), sm("ang"), sm("trn"), sm("trc")
        ti = A([32], I32, "ti")
        rs, rc, sn, cs = sm("rs"), sm("rc"), sm("sn"), sm("cs")
        self.act(dt, ldt, AF.Exp)
        self.tt("dve", xr, lr, dt, ALU.mult)
        self.act(mag, xr, AF.Exp)
        self.act(self.Rch, xr, AF.Exp, scale=8.0)
        self.tt("dve", ang, li, dt, ALU.mult)
        self.ts("dve", trn, ang, 1.0 / TWO_PI, ALU.mult)
        self.ts("dve", trc, trn, 0.25, ALU.add)
        self.cp("dve", ti, trn)
        self.tt("dve", rs, trn, ti, ALU.subtract)
        ti2 = A([32], I32, "ti2")
        self.cp("dve", ti2, trc)
        self.tt("dve", rc, trc, ti2, ALU.subtract)
        self.act(sn, rs, AF.Sin, scale=TWO_PI)
        self.act(cs, rc, AF.Sin, scale=TWO_PI)
        t8 = sm("t8")
        ti3 = A([32], I32, "ti3")
        self.ts("dve", t8, rs, 8.0, ALU.mult)
        self.cp("dve", ti3, t8)
        self.tt("dve", self.f8, t8, ti3, ALU.subtract)
        are, aim = sm("are"), sm("aim")
        self.tt("dve", are, mag, cs, ALU.mult)
        self.tt("dve", aim, mag, sn, ALU.mult)
        den, t1, t2, inv, am1 = sm("den"), sm("t1"), sm("t2"), sm("inv"), sm("am1")
        self.tt("dve", t1, lr, lr, ALU.mult)
        self.tt("dve", t2, li, li, ALU.mult)
        self.tt("dve", den, t1, t2, ALU.add)
        self.gen("dve", "reciprocal", [den], [inv], out=inv, in_=den)
        self.ts("dve", am1, are, -1.0, ALU.add)
        t3, t4 = sm("t3"), sm("t4")
        self.tt("dve", t3, am1, lr, ALU.mult)
        self.tt("dve", t4, aim, li, ALU.mult)
        self.tt("dve", t1, t3, t4, ALU.add)
        self.tt("dve", cre, t1, inv, ALU.mult)
        t5, t6 = sm("t5"), sm("t6")
        self.tt("dve", t5, aim, lr, ALU.mult)
        self.tt("dve", t6, am1, li, ALU.mult)
        self.tt("dve", t2, t5, t6, ALU.subtract)
        self.tt("dve", cim, t2, inv, ALU.mult)
        self.memset("pool", AR[0], 1.0)
        self.memset("pool", AI[0], 0.0)
        self.cp("dve", AR[1], are)
        self.cp("dve", AI[1], aim)
        u1, u2, u3, u4 = sm("u1"), sm("u2"), sm("u3"), sm("u4")
        for k in range(1, 8):
            self.tt("dve", u1, AR[k], are, ALU.mult)
            self.tt("dve", u2, AI[k], aim, ALU.mult)
            self.tt("dve", AR[k + 1], u1, u2, ALU.subtract)
            self.tt("dve", u3, AR[k], aim, ALU.mult)
            self.tt("dve", u4, AI[k], are, ALU.mult)
            self.tt("dve", AI[k + 1], u3, u4, ALU.add)
        w1, w2 = A([32, 16], F32, "w1"), A([32, 16], F32, "w2")
        s16 = [128, 32, 16]
        self.tt("dve", w1, Br, cre.bc(s16), ALU.mult)
        self.tt("dve", w2, Bi, cim.bc(s16), ALU.mult)
        self.tt("dve", Bbr, w1, w2, ALU.subtract)
        w3, w4 = A([32, 16], F32, "w3"), A([32, 16], F32, "w4")
        self.tt("dve", w3, Bi, cre.bc(s16), ALU.mult)
        self.tt("dve", w4, Br, cim.bc(s16), ALU.mult)
        self.tt("dve", Bbi, w3, w4, ALU.add)
        for src, dst in ((Bbr, BbBD[0]), (Bbi, BbBD[1])):
            self.ts("pool", dst[:, :, 0:16], src, m0, ALU.mult)
            self.ts("pool", dst[:, :, 16:32], src, m1, ALU.mult)
        self.sc.barrier()
        self.ar.reset(m_short)
        s32 = [128, 32, 32]
        TP = [A([32, 32], F32, f"tp{i}") for i in range(4)]
        bi_ = 0
        for s_ in range(8):
            k = 7 - s_
            Are, Aim, e2, e4 = TP
            self.tt("dve", Are, BbBD[0], AR[k].bc(s32), ALU.mult)
            self.tt("pool", e2, BbBD[1], AI[k].bc(s32), ALU.mult)
            self.tt("dve", Are, Are, e2, ALU.subtract)
            self.tt("dve", Aim, BbBD[1], AR[k].bc(s32), ALU.mult)
            self.tt("pool", e4, BbBD[0], AI[k].bc(s32), ALU.mult)
            self.tt("dve", Aim, Aim, e4, ALU.add)
            for ri, src_ in enumerate((Are, Aim)):
                for half in range(2):
                    bk = self.bank(bi_ % 4)
                    bi_ += 1
                    for j in range(4):
                        kt = half * 4 + j
                        self.tr(bk[:, j * 128:(j + 1) * 128],
                                src_[:, kt * 4:(kt + 1) * 4, :].re("p a b -> p (a b)"), self.identf)
                    self.cp("act", self.ZupW[ri][:, half * 4:half * 4 + 4, s_, :], bk.re("p (a b) -> p a b", a=4))
        CBD = [A([32, 32], F32, "cbd_re"), A([32, 32], F32, "cbd_im")]
        for ci in range(2):
            for half in range(2):
                bk = self.bank(4 + (ci * 2 + half) % 4)
                for j in range(4):
                    kt = half * 4 + j
                    self.tr(bk[:, j * 128:(j + 1) * 128], CIN[ci][:, kt, :], self.identf)
                self.cp("dve", CBD[ci][:, half * 16:half * 16 + 16, :], bk.re("p (a b) -> p a b", a=16))
        for k in range(9):
            CAre, nCAim, f2, f4 = TP
            self.tt("dve", CAre, CBD[0], AR[k].bc(s32), ALU.mult)
            self.tt("pool", f2, CBD[1], AI[k].bc(s32), ALU.mult)
            self.tt("dve", CAre, CAre, f2, ALU.subtract)
            self.tt("dve", nCAim, CBD[0], AI[k].bc(s32), ALU.mult)
            self.tt("pool", f4, CBD[1], AR[k].bc(s32), ALU.mult)
            self.stt("dve", nCAim, nCAim, -1.0, f4, ALU.mult, ALU.subtract)
            if k >= 1:
                self.cp("act", self.CarW[0][:, :, k - 1, :], CAre)
                self.cp("act", self.CarW[1][:, :, k - 1, :], nCAim)
            if k <= 7:
                for half in range(2):
                    bk = self.bank(half * 2 + (k % 2))
                    for j in range(4):
                        kt = half * 4 + j
                        o = bk[:, j * 128:(j + 1) * 128]
                        fl = lambda t_: t_[:, kt * 4:(kt + 1) * 4, :].re("p a b -> p (a b)")
                        self.mm(o, fl(BbBD[0]), fl(CAre), True, False)
                        self.mm(o, fl(BbBD[1]), fl(nCAim), False, True)
                    self.tt("dve", self.BD[:, half * 4:half * 4 + 4, k, :], bk.re("p (a b) -> p a b", a=4),
                            T(bdm.ap.unsqueeze(1).broadcast_to([128, 4, 128]), bdm.key), ALU.mult)
        self.sc.barrier()
        self.ar.reset(m)

    def phase_a2(self):
        A = self.alloc
        UT = self.UT
        m = self.ar.mark()
        iota_f = A([512], F32, "iota_f")
        mi = self.ar.mark()
        iota_i = A([512], I32, "iota_i")
        self.gen("pool", "iota", [], [iota_i], out=iota_i, pattern=[[1, 512]], base=0, channel_multiplier=0)
        self.cp("pool", iota_f, iota_i)
        self.ar.reset(mi)
        ph = A([512], F32, "ph")
        pi = A([512], I32, "pi")
        ph2 = A([512], F32, "ph2")
        pi2 = A([512], I32, "pi2")
        halfpi = A([1], F32, "halfpi")
        self.memset("pool", halfpi, math.pi / 2.0)
        NSL = 2
        cos_t = [A([512], F32, "cos") for _ in range(NSL)]
        sin_t = [A([512], F32, "sin") for _ in range(NSL)]
        T1 = [A([512], F32, "t1") for _ in range(NSL)]
        T2 = [A([512], F32, "t2") for _ in range(NSL)]
        T3 = [A([512], F32, "t3") for _ in range(NSL)]
        GR = [A([512], F32, "gr") for _ in range(NSL)]
        GI = [A([512], F32, "gi") for _ in range(NSL)]
        NH = 8
        Hre = [A([512], BF16, "hre") for _ in range(NH)]
        Him = [A([512], BF16, "him") for _ in range(NH)]
        evt = [A([512], F32, "evt") for _ in range(2)]
        for h in Hre + Him:
            self.memset("pool", h[:, 0:1], 0.0)
        obi = 0
        n1 = 511
        for kt in range(8):
            for pl in range(4):
                q = kt * 4 + pl
                sl_ = q % NSL
                hs = q % NH
                zr, zi = self.bank(sl_), self.bank(2 + sl_)
                for ri, zb in ((0, zr), (1, zi)):
                    for s_ in range(8):
                        self.mm(zb, self.ZupW[ri][32 * pl:32 * pl + 32, kt, s_, :],
                                UT[32 * pl:32 * pl + 32, kt, s_, :].k(kt, s_),
                                s_ == 0, s_ == 7, tile_position=(32 * pl, 0))
                f8q = self.f8[:, q:q + 1]
                c_, s__ = cos_t[sl_], sin_t[sl_]
                self.act(ph, iota_f, AF.Identity, scale=f8q)
                self.cp("dve", pi, ph)
                self.tt("dve", ph, ph, pi, ALU.subtract)
                self.act(s__, ph, AF.Sin, scale=TWO_PI)
                self.act(ph2, ph, AF.Abs)
                self.act(c_, ph2, AF.Sin, scale=-TWO_PI, bias=halfpi)
                t1, t2, t3, gr, gi = T1[sl_], T2[sl_], T3[sl_], GR[sl_], GI[sl_]
                self.tt("dve", t1, zr, c_, ALU.mult)
                self.tt("dve", t2, zi, s__, ALU.mult)
                self.tt("pool", t1, t1, t2, ALU.add)
                self.tt("dve", t2, zi, c_, ALU.mult)
                self.tt("dve", t3, zr, s__, ALU.mult)
                self.tt("pool", t2, t2, t3, ALU.subtract)
                Rb = self.Rch[:, q:q + 1].bc([128, 512])
                self.gen("dve", "tensor_tensor_scan", [Rb, t1], [gr], out=gr, data0=Rb, data1=t1, initial=0.0,
                         op0=ALU.mult, op1=ALU.add)
                self.gen("dve", "tensor_tensor_scan", [Rb, t2], [gi], out=gi, data0=Rb, data1=t2, initial=0.0,
                         op0=ALU.mult, op1=ALU.add)
                self.tt("dve", t1[:, 0:n1], gr[:, 0:n1], c_[:, 0:n1], ALU.mult)
                self.tt("pool", t3[:, 0:n1], gi[:, 0:n1], s__[:, 0:n1], ALU.mult)
                self.tt("pool", Hre[hs][:, 1:512], t1[:, 0:n1], t3[:, 0:n1], ALU.subtract)
                self.tt("dve", t2[:, 0:n1], gi[:, 0:n1], c_[:, 0:n1], ALU.mult)
                self.tt("pool", t3[:, 0:n1], gr[:, 0:n1], s__[:, 0:n1], ALU.mult)
                self.tt("pool", Him[hs][:, 1:512], t2[:, 0:n1], t3[:, 0:n1], ALU.add)
            for t in range(7, -1, -1):
                ob = self.bank(4 + (obi % 4))
                obi += 1
                for s_ in range(t + 1):
                    self.mm(ob, self.BD[:, kt, t - s_, :], UT[:, kt, s_, :].k(kt, s_), s_ == 0, False)
                for pl in range(4):
                    q = kt * 4 + pl
                    hs = q % NH
                    o_ = ob[32 * pl:32 * pl + 32, :]
                    self.mm(o_, self.CarW[0][:, q, t, :], Hre[hs], False, False, tile_position=(0, 32 * pl))
                    self.mm(o_, self.CarW[1][:, q, t, :], Him[hs], False, True, tile_position=(0, 32 * pl))
                ev = evt[t % 2]
                ut = UT[:, kt, t, :].k(kt, t)
                self.stt("dve", ev, ut, self.Dsk[:, kt:kt + 1], ob, ALU.mult, ALU.add)
                self.act(ut, ev, AF.Gelu)
        self.sc.barrier()
        self.ar.reset(m)

    def load_w(self, dst, src, nkt, split=None):
        v = src.rearrange("(k p) n -> p k n", p=128)
        n = dst.ap.shape[2]
        cw = min(n, 2048)
        if split is None:
            stg = [self.alloc([cw], F32, "wstage") for _ in range(2)]
        else:
            stg = [s_[:, 0:cw] for s_ in split]
        engs = ("pool", "act", "dve")
        i = 0
        for kt in range(nkt):
            for c0 in range(0, n, cw):
                s_ = stg[i % 2]
                self.dma(s_, v[:, kt, c0:c0 + cw])
                self.cp(engs[i % 3], dst[:, kt, c0:c0 + cw].k("w", kt), s_)
                i += 1
    def load_bcast(self, dst, row_ap):
        self.dma(dst, row_ap.partition_broadcast(128))

    def layernorm(self, r, out, gb, bb, scr):
        st, mv, sd = scr["st"], scr["mv"], scr["sd"]
        for c in range(2):
            self.gen("dve", "bn_stats", [r], [st], out=st[:, c, :], in_=r[:, c * 512:(c + 1) * 512])
        self.gen("dve", "bn_aggr", [st], [mv], out=mv, in_=st.re("p a b -> p (a b)"))
        self.act(sd, mv[:, 1:2], AF.Sqrt, scale=1.0, bias=scr["eps"])
        self.gen("dve", "reciprocal", [sd], [sd], out=sd, in_=sd)
        self.ts("dve", out, r, mv[:, 0:1], ALU.subtract, sd, ALU.mult)
        self.tt("pool", out, out, gb, ALU.mult)
        self.tt("pool", out, out, bb, ALU.add)

    def ln_scratch(self):
        A = self.alloc
        eps = A([1], F32, "eps")
        self.memset("pool", eps, LN_EPS)
        return [{"st": A([2, 6], F32, "st"), "mv": A([2], F32, "mv"), "sd": A([1], F32, "sd"), "eps": eps}
                for _ in range(2)]

    def to_feature_major(self, h, hT, banks):
        for half in range(2):
            bk = banks[half]
            for j in range(4):
                kt = half * 4 + j
                self.tr(bk[:, j * 128:(j + 1) * 128], h[:, kt * 128:(kt + 1) * 128], self.identf)
            self.cp("act", hT[:, half * 4:half * 4 + 4, :], bk.re("p (a b) -> p a b", a=4))

    def phase_b1(self):
        A = self.alloc
        dr = self.dram
        UT = self.UT
        m = self.ar.mark()
        self.H1 = self.dscr("H1", [S, D], F32)
        self.H1T = self.dscr("H1T", [128, 8, S], BF16)
        Wg = A([8, 2 * D], BF16, "wglu")
        Wo = A([8, D], BF16, "wout")
        self.load_w(Wg, dr["ssm_w_glu"], 8)
        self.load_w(Wo, dr["ssm_w_out"], 8)
        gb, bb = A([D], F32, "gb"), A([D], F32, "bb")
        self.load_bcast(gb, dr["ln_mix_g"][0])
        self.load_bcast(bb, dr["ln_mix_b"][0])
        lsc = self.ln_scratch()
        zT = [A([8, 512], BF16, "zT") for _ in range(2)]
        sg = [A([512], F32, "sg") for _ in range(2)]
        xt = [A([D], F32, "xt") for _ in range(2)]
        rr = [A([D], F32, "rr") for _ in range(2)]
        h1 = [A([D], F32, "h1") for _ in range(2)]
        h1T = [A([8, 512], BF16, "h1T") for _ in range(2)]
        xperm = self.x.rearrange("(c j t) d -> c t j d", j=64, t=8)
        h1perm = self.H1.rearrange("(c j t) d -> c t j d", j=64, t=8)
        n = 0
        for tb in range(S // 512):
            z = zT[tb % 2]
            hTb = h1T[tb % 2]
            for mt in range(8):
                bv, bg = self.bank(2 * (mt % 2)), self.bank(2 * (mt % 2) + 1)
                for which, bk in ((0, bv), (1, bg)):
                    c0 = which * D + mt * 128
                    for kt in range(8):
                        self.mm(bk, Wg[:, kt, c0:c0 + 128].k("w", kt),
                                UT[:, kt, :, tb * 64:(tb + 1) * 64], kt == 0, kt == 7)
                s_ = sg[mt % 2]
                self.act(s_, bg, AF.Sigmoid)
                self.tt("dve", z[:, mt, :], bv, s_, ALU.mult)
            for sub in range(4):
                tt_ = tb * 4 + sub
                x_, r_, h_ = xt[n % 2], rr[n % 2], h1[n % 2]
                sc_ = lsc[n % 2]
                n += 1
                for tl in range(2):
                    self.dma(x_[tl * 64:(tl + 1) * 64, :], xperm[tb, 2 * sub + tl])
                for nh in range(2):
                    bk = self.bank(4 + nh)
                    for kt in range(8):
                        self.mm(bk, z[:, kt, sub * 128:(sub + 1) * 128], Wo[:, kt, nh * 512:(nh + 1) * 512].k("w", kt),
                                kt == 0, kt == 7)
                    self.stt("dve", r_[:, nh * 512:(nh + 1) * 512], x_[:, nh * 512:(nh + 1) * 512], ALPHA, bk,
                             ALU.mult, ALU.add)
                self.layernorm(r_, h_, gb, bb, sc_)
                for tl in range(2):
                    self.dma(h1perm[tb, 2 * sub + tl], h_[tl * 64:(tl + 1) * 64, :])
                for half in range(2):
                    bk = self.bank(6 + half)
                    for j in range(4):
                        kt = half * 4 + j
                        self.tr(bk[:, j * 128:(j + 1) * 128], h_[:, kt * 128:(kt + 1) * 128], self.identf)
                    o = hTb[:, half * 4:half * 4 + 4, :].re("p a (j t) -> p a t j", t=8)[:, :, 2 * sub:2 * sub + 2, :]
                    self.cp("act", o, bk.re("p (a t j) -> p a t j", a=4, t=2))
            self.dma(self.H1T[:, :, tb * 512:(tb + 1) * 512], hTb)
        self.sc.barrier()
        self.ar.reset(m)

    def phase_ffn(self, layer, Hin, HinT, Hout, HoutT):
        A = self.alloc
        dr = self.dram
        m = self.ar.mark()
        W1 = A([8, DFF], BF16, "w1")
        W2 = A([32, D], BF16, "w2")
        stg = [A([1024], F32, "wstage") for _ in range(4)]
        v1 = dr["w_ff1"][layer].rearrange("(k p) n -> p k n", p=128)
        v2 = dr["w_ff2"][layer].rearrange("(k p) n -> p k n", p=128)
        engs = ("pool", "act", "dve")
        li = 0
        for cb in range(8):
            for kg in range(4):
                s_ = stg[li % 4]
                s2 = s_.re("p (a b) -> p a b", a=2)
                self.dma(s2, v1[:, kg * 2:(kg + 1) * 2, cb * 512:(cb + 1) * 512])
                self.cp(engs[li % 3], W1[:, kg * 2:(kg + 1) * 2, cb * 512:(cb + 1) * 512].k("w1", cb), s2)
                li += 1
            for r2 in range(4):
                s_ = stg[li % 4]
                f0 = cb * 4 + r2
                self.dma(s_, v2[:, f0, :])
                self.cp(engs[li % 3], W2[:, f0, :].k("w2", f0), s_)
                li += 1
        gb, bb = A([D], F32, "gb"), A([D], F32, "bb")
        self.load_bcast(gb, dr["ln_ffn_g"][layer])
        self.load_bcast(bb, dr["ln_ffn_b"][layer])
        lsc = self.ln_scratch()
        hT = [A([8, 256], BF16, "hT") for _ in range(2)]
        hin = [A([D], F32, "hin") for _ in range(4)]
        rr = [A([D], F32, "rr") for _ in range(2)]
        ho = [A([D], F32, "ho") for _ in range(2)]
        hoT = [A([8, 128], BF16, "hoT") for _ in range(2)]
        rl = [A([256], F32, "rl") for _ in range(3)]
        aT = [A([256], BF16, "aT") for _ in range(4)]
        n = 0
        fb = 0
        for blk in range(S // 256):
            t0 = blk * 256
            h_T = hT[blk % 2]
            hs_ = [hin[(2 * blk) % 4], hin[(2 * blk + 1) % 4]]
            self.dma(h_T, HinT[:, :, t0:t0 + 256])
            for sub in range(2):
                self.dma(hs_[sub], Hin[t0 + sub * 128:t0 + (sub + 1) * 128, :])
            acc = [[self.bank(0), self.bank(1)], [self.bank(2), self.bank(3)]]

            def ff2(ft, a_):
                for sub in range(2):
                    for nh in range(2):
                        self.mm(acc[sub][nh], a_[:, sub * 128:(sub + 1) * 128],
                                W2[:, ft, nh * 512:(nh + 1) * 512].k("w2", ft), ft == 0, ft == 31)

            prev = None
            for ft in range(32):
                bk = self.bank(4 + fb % 3)
                r_ = rl[fb % 3]
                a_ = aT[fb % 4]
                fb += 1
                for kt in range(8):
                    self.mm(bk[:, 0:256], W1[:, kt, ft * 128:(ft + 1) * 128].k("w1", ft // 4), h_T[:, kt, :], kt == 0, kt == 7)
                self.act(r_, bk[:, 0:256], AF.Relu)
                self.tt("pool", a_, r_, r_, ALU.mult)
                if prev is not None:
                    ff2(*prev)
                prev = (ft, a_)
            ff2(*prev)
            for sub in range(2):
                tt_ = blk * 2 + sub
                r2, h_o, h_oT = rr[n % 2], ho[n % 2], hoT[n % 2]
                sc_ = lsc[n % 2]
                n += 1
                for nh in range(2):
                    self.stt("dve", r2[:, nh * 512:(nh + 1) * 512], hs_[sub][:, nh * 512:(nh + 1) * 512], ALPHA,
                             acc[sub][nh], ALU.mult, ALU.add)
                self.layernorm(r2, h_o, gb, bb, sc_)
                self.dma(Hout[tt_ * 128:(tt_ + 1) * 128, :], h_o)
                if HoutT is not None:
                    self.to_feature_major(h_o, h_oT, (self.bank(6), self.bank(7)))
                    self.dma(HoutT[:, :, tt_ * 128:(tt_ + 1) * 128], h_oT)
        self.sc.barrier()
        self.ar.reset(m)

    def rmsnorm_tm(self, out_bf, bank_ap, n, g_b, st, mv, rs, eps):
        self.gen("dve", "bn_stats", [bank_ap], [st], out=st, in_=bank_ap)
        self.gen("dve", "bn_aggr", [st], [mv], out=mv, in_=st)
        self.stt("dve", rs, mv[:, 0:1], mv[:, 0:1], mv[:, 1:2], ALU.mult, ALU.add)
        self.act(rs, rs, AF.Sqrt, scale=1.0, bias=eps)
        self.gen("dve", "reciprocal", [rs], [rs], out=rs, in_=rs)
        self.stt("dve", out_bf, bank_ap, rs, g_b, ALU.mult, ALU.mult)

    def phase_c0(self):
        A = self.alloc
        dr = self.dram
        self.Wkvb = A([2, 2048], BF16, "wkvb")
        self.Wqb = A([3, 1536], BF16, "wqb")
        self.load_w(self.Wkvb, dr["kv_w_b"], 2)
        self.load_w(self.Wqb, dr["q_w_b"], 3)
        import os
        if os.environ.get("C0_PAD"):
            self.alloc([int(os.environ["C0_PAD"])], F32, "pad")
        self.cosT = A([NT, 32], F32, "cosT")
        self.sinT = A([NT, 32], F32, "sinT")
        self.ckvT = A([2, S], BF16, "ckvT")
        self.kropeT = A([S], BF16, "kropeT")
        self.cqT = A([3, S], BF16, "cqT")
        self.c_top = self.ar.mark()
        Wkva = A([8, 320], BF16, "wkva")
        Wqa = A([8, 384], BF16, "wqa")
        self.load_w(Wkva, dr["kv_w_a"], 8)
        self.load_w(Wqa, dr["q_w_a"], 8)
        gkv, gq = A([256], F32, "gkv"), A([384], F32, "gq")
        self.load_bcast(gkv, dr["kv_norm_g"][0])
        self.load_bcast(gq, dr["q_norm_g"][0])
        eps = A([1], F32, "rmseps")
        self.memset("pool", eps, RMS_EPS)
        import os
        SK = os.environ.get("C0_SKIP", "").split(",")
        C0NT = int(os.environ.get("C0_NT", str(NT)))
        m2 = self.ar.mark()
        pin = A([128], I32, "pin")
        pinf = A([128], F32, "pinf")
        posf = A([NT], F32, "posf")
        invf = A([32], F32, "invf")
        self.memset("pool", pinf, 0.0)
        self.dma(pin[0:NT, :], self.pos)
        self.cp("dve", pinf[0:NT, :], pin[0:NT, :])
        bk = self.bank(0)
        if "ptr" not in SK:
            self.tr(bk[:, 0:NT], pinf[0:NT, :], self.identf[0:NT, 0:NT])
            self.cp("dve", posf, bk[:, 0:NT])
        iv = (np.float32(10000.0) ** (-(np.arange(32, dtype=np.float32) / np.float32(32.0)))).astype(np.float32)
        for i_ in range(32):
            self.memset("pool", invf[:, i_:i_ + 1], float(iv[i_]))
        s3 = [128, NT, 32]
        if "tab" in SK:
            self.sc.barrier()
            self.ar.reset(m2)
            return
        ang = A([NT, 32], F32, "ang")
        angc = A([NT, 32], F32, "angc")
        ai = A([NT, 32], I32, "ai")
        self.tt("dve", ang, posf.bc(s3), invf.ubc(1, s3), ALU.mult)
        self.ts("dve", ang, ang, 1.0 / TWO_PI, ALU.mult)
        self.ts("dve", angc, ang, 0.25, ALU.add)
        self.cp("dve", ai, ang)
        self.tt("dve", ang, ang, ai, ALU.subtract)
        self.act(self.sinT, ang, AF.Sin, scale=TWO_PI)
        self.cp("dve", ai, angc)
        self.tt("dve", angc, angc, ai, ALU.subtract)
        self.act(self.cosT, angc, AF.Sin, scale=TWO_PI)
        self.sc.barrier()
        if not _os.environ.get("NO_M2_RESET"):
            self.ar.reset(m2)
        h2T = [A([8, 128], BF16, "h2T") for _ in range(2)]
        stk = [A([6], F32, "stk") for _ in range(2)]
        stq = [A([6], F32, "stq") for _ in range(2)]
        mvk = [A([2], F32, "mvk") for _ in range(2)]
        mvq = [A([2], F32, "mvq") for _ in range(2)]
        rk = [A([1], F32, "rk") for _ in range(2)]
        rq = [A([1], F32, "rq") for _ in range(2)]
        ckv_tm = [A([256], F32, "ckv_tm") for _ in range(2)]
        kr_tm = [A([128], F32, "kr_tm") for _ in range(2)]
        cq_tm = [A([384], F32, "cq_tm") for _ in range(2)]
        ra = [[A([32], F32, "ra") for _ in range(4)] for _ in range(2)]
        for tt_ in range(C0NT):
            sl_ = tt_ % 2
            tok = slice(tt_ * 128, (tt_ + 1) * 128)
            hT = h2T[sl_]
            self.dma(hT, self.H2T[:, :, tok])
            bA, bB, bC = self.bank(sl_), self.bank(2 + sl_), self.bank(4 + sl_)
            for kt in range(8):
                self.mm(bA[:, 0:320], hT[:, kt, :], Wkva[:, kt, :].k("w", kt), kt == 0, kt == 7)
            for kt in range(8):
                self.mm(bB[:, 0:384], hT[:, kt, :], Wqa[:, kt, :].k("w", kt), kt == 0, kt == 7)
            if "rms" not in SK:
                self.rmsnorm_tm(ckv_tm[sl_], bA[:, 0:256], 256, gkv, stk[sl_], mvk[sl_], rk[sl_], eps)
                self.rmsnorm_tm(cq_tm[sl_], bB[:, 0:384], 384, gq, stq[sl_], mvq[sl_], rq[sl_], eps)
            x1, x2 = bA[:, 256:288], bA[:, 288:320]
            c_, s_ = self.cosT[:, tt_, :], self.sinT[:, tt_, :]
            a1, a2, a3, a4 = ra[sl_]
            if "rope" not in SK:
                self.tt("dve", a1, x1, c_, ALU.mult)
                self.tt("dve", a2, x2, s_, ALU.mult)
                self.tt("dve", a3, x1, s_, ALU.mult)
                self.tt("dve", a4, x2, c_, ALU.mult)
            kr3 = kr_tm[sl_].re("p (r c) -> p r c", r=2)
            s2 = [128, 2, 32]
            for r_ in range(2):
                self.tt(KR_ENG, kr3[:, r_, 0:32], a1, a2, ALU.subtract)
                self.tt(KR_ENG, kr3[:, r_, 32:64], a3, a4, ALU.add)
            bC2 = self.bank(6 + sl_)
            srcs = [ckv_tm[sl_][:, 0:128], ckv_tm[sl_][:, 128:256], kr_tm[sl_], cq_tm[sl_][:, 0:128],
                    cq_tm[sl_][:, 128:256], cq_tm[sl_][:, 256:384]]
            for j, s__ in enumerate(srcs):
                dst = bC[:, j * 128:(j + 1) * 128] if j < 4 else bC2[:, (j - 4) * 128:(j - 3) * 128]
                self.tr(dst, s__, self.identf)
            if "ev1" not in SK:
                self.cp("act", self.ckvT[:, :, tok], bC[:, 0:256].re("p (a b) -> p a b", a=2))
            if "ev2" not in SK:
                self.cp("act", self.kropeT[:, tok], bC[:, 256:384])
            if "ev3" not in SK:
                self.cp("dve", self.cqT[:, 0, tok], bC[:, 384:512])
            if "ev4" not in SK:
                self.cp("dve", self.cqT[:, 1:3, tok], bC2[:, 0:256].re("p (a b) -> p a b", a=2))
        self.sc.barrier()
        self.ar.reset(self.c_top)

    def phase_c(self):
        A = self.alloc
        self.OT = self.dscr("OT", [128, 8, S], BF16)
        onesf = A([128], F32, "onesf")
        self.memset("pool", onesf, 1.0)
        accD = [A([512], F32, "accD") for _ in range(2)]
        accP = [A([512], F32, "accP") for _ in range(2)]
        knT = A([4, S], BF16, "knT")
        vtm = A([NT, 4, 128], BF16, "vtm")
        QN = [A([4, 512], BF16, "QN") for _ in range(2)]
        QR = [A([2, 512], BF16, "QR") for _ in range(2)]
        qr_tm = [A([256], F32, "qr_tm") for _ in range(2)]
        ra = [[A([4, 32], F32, "qra") for _ in range(4)] for _ in range(2)]
        PT = [A([512], BF16, "PT") for _ in range(4)]
        rd = [A([512], F32, "rd") for _ in range(1)]
        oTb = [A([4, 512], BF16, "oTb") for _ in range(2)]
        Wkvb4 = self.Wkvb.re("p k (h two d) -> p k h two d", two=2, d=128)
        Wqb3 = self.Wqb.re("p k (h e) -> p k h e", e=192)
        ev = 0
        for hh in range(2):
            for tb in range(S // 512):
                for hl in range(4):
                    h = 4 * hh + hl
                    bk = self.bank(ev % 3)
                    for j in range(2):
                        self.mm(bk, self.Wkvb[:, j, h * 256:h * 256 + 128].k("w", j),
                                self.ckvT[:, j, tb * 512:(tb + 1) * 512], j == 0, j == 1)
                    self.cp("act" if ev % 2 == 0 else "dve", knT[:, hl, tb * 512:(tb + 1) * 512], bk)
                    ev += 1
            for tt_ in range(NT):
                bk = self.bank(ev % 3)
                for j in range(2):
                    self.mm(bk, self.ckvT[:, j, tt_ * 128:(tt_ + 1) * 128],
                            Wkvb4[:, j, 4 * hh:4 * hh + 4, 1, :].k("w", j), j == 0, j == 1)
                self.cp("act" if ev % 2 == 0 else "dve", vtm[:, tt_, :, :], bk.re("p (a b) -> p a b", a=4))
                ev += 1
            sbi = 0
            pti = 0
            for qb in range(S // 512):
                qn, qr = QN[qb % 2], QR[qb % 2]
                qs = slice(qb * 512, (qb + 1) * 512)
                for hl in range(4):
                    h = 4 * hh + hl
                    bk = self.bank(7)
                    for j in range(3):
                        self.mm(bk, self.Wqb[:, j, h * 192:h * 192 + 128].k("w", j), self.cqT[:, j, qs], j == 0, j == 2)
                    self.cp("dve", qn[:, hl, :], bk)
                for sub in range(4):
                    tt_ = qb * 4 + sub
                    tok = slice(tt_ * 128, (tt_ + 1) * 128)
                    bk = self.bank(7)
                    for j in range(3):
                        self.mm(bk[:, 0:256], self.cqT[:, j, tok], Wqb3[:, j, 4 * hh:4 * hh + 4, 128:192].k("w", j),
                                j == 0, j == 2)
                    b4 = bk[:, 0:256].re("p (h r c) -> p h r c", h=4, r=2)
                    x1, x2 = b4[:, :, 0, :], b4[:, :, 1, :]
                    s4 = [128, 4, 32]
                    c_, s_ = self.cosT[:, tt_, :].ubc(1, s4), self.sinT[:, tt_, :].ubc(1, s4)
                    a1, a2, a3, a4 = ra[sub % 2]
                    self.tt("dve", a1, x1, c_, ALU.mult)
                    self.tt("dve", a2, x2, s_, ALU.mult)
                    self.tt("dve", a3, x1, s_, ALU.mult)
                    self.tt("dve", a4, x2, c_, ALU.mult)
                    q3 = qr_tm[sub % 2].re("p (h e) -> p h e", h=4)
                    self.tt("pool", q3[:, :, 0:32], a1, a2, ALU.subtract)
                    self.tt("pool", q3[:, :, 32:64], a3, a4, ALU.add)
                    for pr in range(2):
                        self.tr(bk[:, 256 + pr * 128:256 + (pr + 1) * 128], qr_tm[sub % 2][:, pr * 128:(pr + 1) * 128],
                                self.identf)
                    self.cp("act", qr[:, :, sub * 128:(sub + 1) * 128],
                            bk[:, 256:512].re("p (a b) -> p a b", a=2))
                ot = oTb[qb % 2]
                for hl in range(4):
                    pr, e = hl // 2, hl % 2
                    bO, bD = self.bank(3 + 2 * (hl % 2)), self.bank(4 + 2 * (hl % 2))
                    nk = 4 * qb + 4

                    def c0_of(kt):
                        i = kt - 4 * qb
                        return 128 * i if i > 0 else 0

                    def scores(kt):
                        sb = self.bank(kt % 3)
                        c0 = c0_of(kt)
                        ks = slice(kt * 128, (kt + 1) * 128)
                        self.mm(sb[:, c0:512], knT[:, hl, ks], qn[:, hl, c0:512], True, False)
                        self.mm(sb[:, c0:512], self.kropeT[64 * e:64 * e + 64, ks], qr[64 * e:64 * e + 64, pr, c0:512],
                                False, True, tile_position=(64 * e, 0))

                    aD, aP = accD[hl % 2], accP[hl % 2]
                    self.memset("pool", aD, 0.0)
                    self.memset("pool", aP, 0.0)
                    scores(0)
                    for kt in range(nk):
                        if kt + 1 < nk:
                            scores(kt + 1)
                        sb = self.bank(kt % 3)
                        c0 = c0_of(kt)
                        pt = PT[pti % 4]
                        pti += 1
                        self.act(pt[:, c0:512], sb[:, c0:512], AF.Exp, scale=SM_SCALE)
                        if kt >= 4 * qb:
                            dg = pt[:, c0:c0 + 128]
                            self.gen("pool", "affine_select", [pt], [pt], out=dg, in_=dg, pattern=[[1, 128]],
                                     compare_op=ALU.is_ge, fill=0.0, base=0, channel_multiplier=-1)
                        self.mm(bO[:, c0:512], vtm[:, kt, hl, :], pt[:, c0:512], kt == 0, kt == nk - 1)
                        if kt % 3 == 2:
                            self.tt("pool", aP[:, c0:512], aP[:, c0:512], pt[:, c0:512], ALU.add)
                        else:
                            self.tt("dve", aD[:, c0:512], aD[:, c0:512], pt[:, c0:512], ALU.add)
                    self.mm(bD, onesf, aD, True, False)
                    self.mm(bD, onesf, aP, False, True)
                    r_ = rd[0]
                    self.gen("dve", "reciprocal", [bD], [r_], out=r_, in_=bD)
                    self.tt("dve", ot[:, hl, :], bO, r_, ALU.mult)
                self.dma(self.OT[:, 4 * hh:4 * hh + 4, qs], ot)
        self.sc.barrier()
        if self.stop_after in ("c", "c2"):
            self.dbg_dump("dbg_ckvT", self.ckvT, [128, 2, S], BF16)
            self.dbg_dump("dbg_kropeT", self.kropeT, [128, S], BF16)
            self.dbg_dump("dbg_cqT", self.cqT, [128, 3, S], BF16)
            self.dbg_dump("dbg_knT", knT, [128, 4, S], BF16)
            self.dbg_dump("dbg_vtm", vtm, [128, NT, 4, 128], BF16)
            self.dbg_dump("dbg_cosT", self.cosT, [128, NT, 32], F32)
            self.sc.barrier()
        self.ar.off = 0
        self.consts()
        self.sc.barrier()

    def phase_c2(self):
        A = self.alloc
        dr = self.dram
        m = self.ar.mark()
        self.H3 = self.H1
        self.H3T = self.H1T
        Wo = A([8, D], BF16, "wo")
        self.load_w(Wo, dr["attn_w_o"], 8)
        gb, bb = A([D], F32, "gb"), A([D], F32, "bb")
        self.load_bcast(gb, dr["ln_mix_g"][1])
        self.load_bcast(bb, dr["ln_mix_b"][1])
        lsc = self.ln_scratch()
        oT = [A([8, 128], BF16, "oT") for _ in range(3)]
        h2 = [A([D], F32, "h2") for _ in range(3)]
        h3 = [A([D], F32, "h3") for _ in range(2)]
        h3T = [A([8, 128], BF16, "h3T") for _ in range(2)]
        for tt_ in range(NT):
            tok = slice(tt_ * 128, (tt_ + 1) * 128)
            o_, r_, h_, hT_ = oT[tt_ % 3], h2[tt_ % 3], h3[tt_ % 2], h3T[tt_ % 2]
            self.dma(o_, self.OT[:, :, tok])
            self.dma(r_, self.H2[tok, :])
            for nh in range(2):
                bk = self.bank(2 * (tt_ % 2) + nh)
                for kt in range(8):
                    self.mm(bk, o_[:, kt, :], Wo[:, kt, nh * 512:(nh + 1) * 512].k("w", kt), kt == 0, kt == 7)
                self.stt("dve", r_[:, nh * 512:(nh + 1) * 512], r_[:, nh * 512:(nh + 1) * 512], ALPHA, bk,
                         ALU.mult, ALU.add)
            self.layernorm(r_, h_, gb, bb, lsc[tt_ % 2])
            self.dma(self.H3[tok, :], h_)
            self.to_feature_major(h_, hT_, (self.bank(4 + 2 * (tt_ % 2)), self.bank(5 + 2 * (tt_ % 2))))
            self.dma(self.H3T[:, :, tok], hT_)
        self.sc.barrier()
        self.ar.reset(m)

    def finish(self):
        nc = self.nc
        with self.es:
            with nc.Block() as block:
                self.sc.emit(nc, block, self.es)
        return nc

    def dbg_dump(self, name, src, shape, dtype):
        t = self.dout(name, shape, dtype)
        self.dma(t, src)
        return t

    def dbg_copyT(self, name, src_dram):
        t = self.dout(name, [128, 8, S], BF16)
        buf = [self.alloc([8, 512], BF16, "dbgbufT") for _ in range(2)]
        for i in range(S // 512):
            self.dma(buf[i % 2], src_dram[:, :, i * 512:(i + 1) * 512])
            self.dma(t[:, :, i * 512:(i + 1) * 512], buf[i % 2])
        return t

    def dbg_copy(self, name, src_dram, shape, dtype):
        t = self.dout(name, shape, dtype)
        buf = [self.alloc([D], F32, "dbgbuf") for _ in range(2)]
        for i in range(shape[0] // 128):
            self.dma(buf[i % 2], src_dram[i * 128:(i + 1) * 128, :])
            self.dma(t[i * 128:(i + 1) * 128, :], buf[i % 2])
        return t


def build(stop_after=None, debug=()):
    b = Prog(stop_after, debug)
    b.declare_io()
    b.consts()
    b.phase_a1()
    if stop_after == "a1":
        b.dbg_dump("dbg_zupw_re", b.ZupW[0], [128, 8, 8, 128], BF16)
        b.dbg_dump("dbg_zupw_im", b.ZupW[1], [128, 8, 8, 128], BF16)
        b.dbg_dump("dbg_carw_re", b.CarW[0], [128, 32, 8, 32], BF16)
        b.dbg_dump("dbg_carw_nim", b.CarW[1], [128, 32, 8, 32], BF16)
        b.dbg_dump("dbg_bd", b.BD, [128, 8, 8, 128], BF16)
        b.dbg_dump("dbg_rch", b.Rch, [128, 32], F32)
        b.dbg_dump("dbg_f8", b.f8, [128, 32], F32)
        b.sc.barrier()
        return b.finish()
    b.phase_a0()
    if stop_after == "a0":
        b.dbg_dump("dbg_UT", b.UT, [128, 8, 8, 512], BF16)
        b.sc.barrier()
        return b.finish()
    b.phase_a2()
    if stop_after == "a2":
        b.dbg_dump("dbg_UT", b.UT, [128, 8, 8, 512], BF16)
        b.sc.barrier()
        return b.finish()
    b.ar.reset(b.ut_top)
    b.phase_b1()
    b.ar.off = 0
    b.consts()
    b.sc.barrier()
    if stop_after == "b1":
        b.dbg_copy("dbg_H1", b.H1, [S, D], F32)
        b.sc.barrier()
        return b.finish()
    b.H2 = b.dscr("H2", [S, D], F32)
    b.H2T = b.dscr("H2T", [128, 8, S], BF16)
    b.phase_ffn(0, b.H1, b.H1T, b.H2, b.H2T)
    if stop_after == "b2":
        b.dbg_copy("dbg_H2", b.H2, [S, D], F32)
        b.sc.barrier()
        return b.finish()
    b.phase_c0()
    if stop_after == "c0":
        b.dbg_dump("dbg_ckvT", b.ckvT, [128, 2, S], BF16)
        b.dbg_dump("dbg_kropeT", b.kropeT, [128, S], BF16)
        b.dbg_dump("dbg_cqT", b.cqT, [128, 3, S], BF16)
        b.dbg_dump("dbg_cosT", b.cosT, [128, NT, 32], F32)
        b.dbg_dump("dbg_sinT", b.sinT, [128, NT, 32], F32)
        b.sc.barrier()
        return b.finish()
    b.phase_c()
    if stop_after == "c":
        b.dbg_copyT("dbg_OT", b.OT)
        b.sc.barrier()
        return b.finish()
    b.phase_c2()
    if stop_after == "c2":
        b.dbg_copy("dbg_H3", b.H3, [S, D], F32)
        b.sc.barrier()
        return b.finish()
    b.phase_ffn(1, b.H3, b.H3T, b.out, None)
    return b.finish()


WEIGHT_NAMES = ["ln_mix_g", "ln_mix_b", "ln_ffn_g", "ln_ffn_b", "w_ff1", "w_ff2", "ssm_lam_re", "ssm_lam_im",
                "ssm_log_dt", "ssm_b_re", "ssm_b_im", "ssm_c_re", "ssm_c_im", "ssm_d", "ssm_w_glu", "ssm_w_out",
                "kv_w_a", "kv_norm_g", "kv_w_b", "q_w_a", "q_norm_g", "q_w_b", "attn_w_o"]


def make_in_maps(inputs, n_cores=8):
    f = lambda a: np.ascontiguousarray(np.asarray(a))
    shared = {
        "ln_mix_g": f(inputs["ln_mix_g"]), "ln_mix_b": f(inputs["ln_mix_b"]),
        "ln_ffn_g": f(inputs["ln_ffn_g"]), "ln_ffn_b": f(inputs["ln_ffn_b"]),
        "w_ff1": f(inputs["w_ff1"]), "w_ff2": f(inputs["w_ff2"]),
        "ssm_lam_re": f(inputs["ssm_lam_re"])[0], "ssm_lam_im": f(inputs["ssm_lam_im"])[0],
        "ssm_log_dt": f(inputs["ssm_log_dt"]).reshape(1, 64),
        "ssm_b_re": f(inputs["ssm_b_re"])[0], "ssm_b_im": f(inputs["ssm_b_im"])[0],
        "ssm_c_re": f(inputs["ssm_c_re"])[0], "ssm_c_im": f(inputs["ssm_c_im"])[0],
        "ssm_d": f(inputs["ssm_d"])[0], "ssm_w_glu": f(inputs["ssm_w_glu"])[0],
        "ssm_w_out": f(inputs["ssm_w_out"])[0],
        "kv_w_a": f(inputs["kv_w_a"]), "kv_norm_g": f(inputs["kv_norm_g"]).reshape(1, 256),
        "kv_w_b": f(inputs["kv_w_b"]),
        "q_w_a": f(inputs["q_w_a"])[0], "q_norm_g": f(inputs["q_norm_g"]).reshape(1, 384),
        "q_w_b": f(inputs["q_w_b"])[0], "attn_w_o": f(inputs["attn_w_o"])[0],
    }
    x = f(inputs["x"])
    pos = f(inputs["positions"]).astype(np.int32)
    maps = []
    for c in range(n_cores):
        m = dict(shared)
        m["x"] = x[c]
        m["positions"] = pos[c].reshape(NT, 128)
        maps.append(m)
    return maps


def kernel(**inputs):
    nc = build()
    maps = make_in_maps(inputs)
    res = run_bass_kernel_spmd(nc, maps, core_ids=list(range(8)))
    return np.stack([np.asarray(r["out"]) for r in res.results], axis=0).astype(np.float32)
```

```python
import math
import os as _os
from contextlib import ExitStack

import numpy as np
import concourse.bass as bass
import concourse.mybir as mybir
from concourse.bass_utils import run_bass_kernel_spmd

F32 = mybir.dt.float32
BF16 = mybir.dt.bfloat16
I32 = mybir.dt.int32
AF = mybir.ActivationFunctionType
ALU = mybir.AluOpType
AX = mybir.AxisListType

S = 4096
D = 1024
NT = S // 128
DFF = 4096
PAD = 8
ALPHA = 4.0 ** 0.25
LN_EPS = 1e-5
RMS_EPS = 1e-6
SM_SCALE = 192.0 ** -0.5
TWO_PI = 2.0 * math.pi
KR_ENG = _os.environ.get("KR_ENG", "pool")
ATTACH_WAIT = _os.environ.get("ATTACH_WAIT", "1") == "1"


def sl(start, n, step=1):
    return slice(start, start + (n - 1) * step + 1, step)


import heapq


class _Op:
    __slots__ = ("eng", "fn", "deps", "odeps", "needs_inc", "is_dma", "dsem", "dval", "sem_idx", "count", "idx",
                 "cost", "nbytes", "seg", "start", "fin", "nsucc", "succ", "est", "bar_waits")

    def __init__(self, eng, fn, is_dma=False):
        self.eng = eng
        self.fn = fn
        self.deps = []
        self.odeps = []
        self.needs_inc = False
        self.is_dma = is_dma
        self.dsem = None
        self.dval = 0
        self.sem_idx = 0
        self.count = 0
        self.cost = 0.1
        self.nbytes = 0
        self.bar_waits = None


class Sched:
    ENGS = ("sp", "act", "dve", "pool", "pe")
    LIMIT = 30000
    XLAT = 1.3
    DMA_LAT = 2.0
    DMA_BW = 160e3

    def __init__(self, n_dma_sems=int(_os.environ.get("NDMA", "24")), reorder=True):
        self.segs = [[]]
        self.lw = {}
        self.rd = {}
        self.n_dma = n_dma_sems
        nsw = 8
        self.dma_pool = {"sp": list(range(0, n_dma_sems - nsw)), "act": list(range(0, n_dma_sems - nsw)),
                         "pool": list(range(n_dma_sems - nsw, n_dma_sems))}
        self.nops = 0
        self.reorder = reorder

    def _deps(self, o, reads, writes):
        extra = [k for k in reads if isinstance(k, tuple) and k and k[0] == "ps" and k not in writes]
        if extra:
            writes = list(writes) + extra
        deps = {}
        for k in reads:
            w = self.lw.get(k)
            if w is not None:
                deps[id(w)] = w
        for k in writes:
            w = self.lw.get(k)
            if w is not None:
                deps[id(w)] = w
            for r in self.rd.get(k, ()):
                deps[id(r)] = r
        for d in deps.values():
            if d is o:
                continue
            if (not d.is_dma) and d.eng == o.eng and (not o.is_dma) and o.eng == "pe":
                o.odeps.append(d)
                continue
            if not d.is_dma:
                d.needs_inc = True
            o.deps.append(d)
        for k in reads:
            self.rd.setdefault(k, []).append(o)
        for k in writes:
            self.lw[k] = o
            self.rd[k] = []

    def op(self, eng, fn, reads=(), writes=(), cost=0.1):
        o = _Op(eng, fn)
        o.cost = cost
        o.idx = self.nops
        self.nops += 1
        self._deps(o, reads, writes)
        self.segs[-1].append(o)
        return o

    def dma(self, eng, fn, reads=(), writes=(), nbytes=0):
        o = _Op(eng, fn, is_dma=True)
        o.nbytes = nbytes
        o.cost = 0.06 if eng != "pool" else 1.0
        o.idx = self.nops
        self.nops += 1
        self._deps(o, reads, writes)
        self.segs[-1].append(o)
        return o

    def barrier(self):
        if self.segs[-1]:
            self.segs.append([])
        self.lw = {}
        self.rd = {}

    def _schedule_segment(self, seg, t0):
        order = {e: [] for e in self.ENGS}
        if not seg:
            return order, t0
        inseg = set(id(o) for o in seg)
        for o in seg:
            o.succ = []
            o.nsucc = 0
            o.est = t0
        for o in seg:
            for d in o.deps + o.odeps:
                if id(d) in inseg:
                    d.succ.append(o)
                    o.nsucc += 1
        if not self.reorder:
            t = {e: t0 for e in self.ENGS}
            tend = t0
            for o in seg:
                order[o.eng].append(o)
            return order, tend
        bl = {}
        for o in reversed(seg):
            c = o.cost + (self.DMA_LAT + o.nbytes / self.DMA_BW if o.is_dma else 0.0)
            m_ = 0.0
            for s_ in o.succ:
                v = bl[id(s_)] + (0.0 if (s_.eng == o.eng and not o.is_dma) else self.XLAT)
                if v > m_:
                    m_ = v
            bl[id(o)] = c + m_
        PRI = "bl"
        for o in seg:
            o.idx = (-bl[id(o)], o.idx) if PRI == "bl" else o.idx
        hest = {e: [] for e in self.ENGS}
        hidx = {e: [] for e in self.ENGS}
        free = {e: t0 for e in self.ENGS}
        for o in seg:
            if o.nsucc == 0:
                heapq.heappush(hest[o.eng], (o.est, o.idx, o))
        dma_free = t0
        remaining = len(seg)
        tend = t0
        while remaining:
            best = None
            for e in self.ENGS:
                he, hi = hest[e], hidx[e]
                while he and he[0][0] <= free[e]:
                    _, ix, oo = heapq.heappop(he)
                    heapq.heappush(hi, (ix, oo))
                if hi:
                    st, ix = free[e], hi[0][0]
                elif he:
                    st, ix = he[0][0], he[0][1]
                else:
                    continue
                if best is None or (st, ix) < best[:2]:
                    best = (st, ix, e)
            st, ix, e = best
            if hidx[e]:
                _, o = heapq.heappop(hidx[e])
            else:
                _, _, o = heapq.heappop(hest[e])
            o.start = st
            if o.is_dma:
                issue_done = st + o.cost
                ts = max(issue_done, dma_free)
                done = ts + o.nbytes / self.DMA_BW
                dma_free = done
                o.fin = done + self.DMA_LAT
                free[e] = issue_done
            else:
                o.fin = st + o.cost
                free[e] = o.fin
            tend = max(tend, o.fin)
            order[e].append(o)
            remaining -= 1
            for s_ in o.succ:
                lat = 0.0 if (s_.eng == o.eng and not o.is_dma) else self.XLAT
                if o.fin + lat > s_.est:
                    s_.est = o.fin + lat
                s_.nsucc -= 1
                if s_.nsucc == 0:
                    heapq.heappush(hest[s_.eng], (s_.est, s_.idx, s_))
        return order, tend

    def finalize(self):
        if self.segs[-1]:
            self.segs.append([])
        t0 = 0.0
        self.eng_ops = {e: [] for e in self.ENGS}
        seg_orders = []
        for seg in self.segs:
            order, t0 = self._schedule_segment(seg, t0)
            seg_orders.append(order)
        self.est_total_us = t0
        cnt = {e: 0 for e in self.ENGS}
        si = {e: 0 for e in self.ENGS}
        dma_cnt = [0] * self.n_dma
        dma_rr = {"sp": 0, "act": 0, "pool": 0}
        for sidx, order in enumerate(seg_orders):
            if sidx > 0:
                prev = seg_orders[sidx - 1]
                lasts = []
                for e in self.ENGS:
                    for o in reversed(self.eng_ops[e]):
                        if o.fn is not None and not o.is_dma:
                            lasts.append(o)
                            break
                bw = {}
                for o in lasts:
                    if not o.needs_inc:
                        o.needs_inc = True
                        if cnt[o.eng] >= self.LIMIT:
                            si[o.eng] += 1
                            cnt[o.eng] = 0
                        cnt[o.eng] += 1
                        o.count = cnt[o.eng]
                        o.sem_idx = si[o.eng]
                    bw[(o.eng, o.sem_idx)] = o.count
                for s_ in range(self.n_dma):
                    if dma_cnt[s_]:
                        bw[("d", s_)] = dma_cnt[s_]
                for e in self.ENGS:
                    b = _Op(e, None)
                    b.bar_waits = dict(bw)
                    self.eng_ops[e].append(b)
            for e in self.ENGS:
                for o in order[e]:
                    if o.is_dma:
                        pool = self.dma_pool[e]
                        rrk = "sp" if e == "act" else e
                        s_ = pool[dma_rr[rrk] % len(pool)]
                        dma_rr[rrk] += 1
                        o.bar_waits = {("d", s_): dma_cnt[s_]} if dma_cnt[s_] else None
                        dma_cnt[s_] += 16
                        o.dsem = s_
                        o.dval = dma_cnt[s_]
                    elif o.needs_inc:
                        if cnt[e] >= self.LIMIT:
                            si[e] += 1
                            cnt[e] = 0
                        cnt[e] += 1
                        o.count = cnt[e]
                        o.sem_idx = si[e]
                    self.eng_ops[e].append(o)
        self.nsem = {e: si[e] + 1 for e in self.ENGS}

    def emit(self, nc, block, es):
        self.finalize()
        esem = {e: [es.enter_context(nc.semaphore(f"s_{e}_{i}")) for i in range(self.nsem[e])] for e in self.ENGS}
        dsem = [es.enter_context(nc.semaphore(f"s_dma_{i}")) for i in range(self.n_dma)]

        def run(e, eng):
            waited = {}
            for o in self.eng_ops[e]:
                need = {}
                if o.bar_waits:
                    need.update(o.bar_waits)
                for d in o.deps:
                    if d.is_dma:
                        key = ("d", d.dsem)
                        val = d.dval
                    else:
                        key = (d.eng, d.sem_idx)
                        val = d.count
                    if val > need.get(key, 0):
                        need[key] = val
                todo = []
                for key, val in need.items():
                    if waited.get(key, 0) >= val:
                        continue
                    waited[key] = val
                    sem = dsem[key[1]] if key[0] == "d" else esem[key[0]][key[1]]
                    todo.append((1 if key[0] == "d" else 0, sem, val))
                todo.sort(key=lambda t_: t_[0])
                attach = None
                if todo and o.fn is not None and ATTACH_WAIT:
                    attach = todo.pop()
                for _, sem, val in todo:
                    eng.wait_ge(sem, val)
                if o.fn is None:
                    continue
                ins = o.fn(eng)
                if attach is not None:
                    ins._wait_ge(attach[1], attach[2])
                if o.is_dma:
                    ins.then_inc(dsem[o.dsem], 16)
                elif o.needs_inc:
                    ins.then_inc(esem[e][o.sem_idx], 1)

        @block.sync
        def _(eng):
            run("sp", eng)

        @block.scalar
        def _(eng):
            run("act", eng)

        @block.vector
        def _(eng):
            run("dve", eng)

        @block.gpsimd
        def _(eng):
            run("pool", eng)

        @block.tensor
        def _(eng):
            run("pe", eng)


class Arena:
    def __init__(self, nc, nbytes):
        self.nc = nc
        self.words = nbytes // 4
        self.t = nc.alloc_sbuf_tensor("arena", [128, self.words], F32)
        self.off = 0

    def alloc(self, shape, dtype=F32):
        n = 1
        for s_ in shape:
            n *= s_
        bpe = 4 if dtype in (F32, I32) else 2
        nw = (n * bpe + 3) // 4
        nw = (nw + 7) // 8 * 8
        assert self.off + nw <= self.words, f"SBUF arena overflow: need {self.off + nw} words of {self.words}"
        v = self.t[:, self.off:self.off + nw]
        self.off += nw
        if dtype != F32:
            v = v.bitcast(dtype)
        v = v[:, 0:n]
        if len(shape) == 2:
            v = v.rearrange("p (a b) -> p a b", a=shape[0])
        elif len(shape) == 3:
            v = v.rearrange("p (a b c) -> p a b c", a=shape[0], b=shape[1])
        elif len(shape) == 4:
            v = v.rearrange("p (a b c d) -> p a b c d", a=shape[0], b=shape[1], c=shape[2])
        return v

    def mark(self):
        return self.off

    def reset(self, m):
        self.off = m


class Builder:
    def __init__(self, stop_after=None, debug=()):
        self.stop_after = stop_after
        self.debug = set(debug)
        self.nc = bass.Bass("TRN2", target_bir_lowering=False)
        self.sc = Sched()
        self.es = ExitStack()
        nc = self.nc
        self.ar = Arena(nc, 212480)
        self.ps = [nc.alloc_psum_tensor(f"psb{i}", [128, 512], F32) for i in range(8)]
        self.dram = {}
        self.uid = 0

    def din(self, name, shape, dtype=F32):
        t = self.nc.dram_tensor(name, list(shape), dtype, kind="ExternalInput").ap()
        self.dram[name] = t
        return t

    def dout(self, name, shape, dtype=F32):
        t = self.nc.dram_tensor(name, list(shape), dtype, kind="ExternalOutput").ap()
        self.dram[name] = t
        return t

    def dscr(self, name, shape, dtype=F32):
        t = self.nc.dram_tensor(name, list(shape), dtype, kind="Internal").ap()
        self.dram[name] = t
        return t

    def alloc(self, shape, dtype=F32, name=None):
        self.uid += 1
        return T(self.ar.alloc(shape, dtype), (name or "t", self.uid))

    def bank(self, i):
        return T(self.ps[i][:, :], ("ps", i))

    @staticmethod
    def _ap(x):
        return x.ap if isinstance(x, T) else x

    @staticmethod
    def _keys(*xs):
        return [x.key for x in xs if isinstance(x, T)]

    @staticmethod
    def _n(ap):
        n = 1
        for s_ in ap.shape[1:]:
            n *= s_
        return n

    @staticmethod
    def _bpe(ap):
        return 4 if ap.dtype in (F32, I32) else 2

    def _ecost(self, eng, n, mult=1.0):
        if eng == "act":
            return 0.2 + n / 1400.0
        if eng == "dve":
            return 0.08 + mult * n / 960.0
        return 0.25 + mult * n / 500.0

    def dma(self, out, in_, eng="sp", reads_extra=(), **kw):
        o, i = self._ap(out), self._ap(in_)
        nb = self._n(o) * o.shape[0] * self._bpe(o)
        return self.sc.dma(eng, lambda e: e.dma_start(out=o, in_=i, **kw), self._keys(in_, *reads_extra),
                           self._keys(out), nbytes=nb)

    def mm(self, out, lhsT, rhs, start=True, stop=True, **kw):
        o, l, r = out.ap, lhsT.ap, rhs.ap
        c = 0.02 + max(self._n(r), 64) / 1950.0
        if l.dtype == F32:
            c *= 4
        return self.sc.op("pe", lambda e: e.matmul(o, l, r, start=start, stop=stop, **kw),
                          self._keys(lhsT, rhs), self._keys(out), cost=c)

    def tr(self, out, in_, ident, **kw):
        o, i, d = out.ap, in_.ap, ident.ap
        return self.sc.op("pe", lambda e: e.transpose(o, i, d, **kw), self._keys(in_, ident), self._keys(out),
                          cost=0.12)

    def act(self, out, in_, func, scale=1.0, bias=0.0, accum=None):
        o, i = out.ap, in_.ap
        kw = {}
        rd = self._keys(in_, scale, bias)
        wr = self._keys(out)
        if accum is not None:
            kw["accum_out"] = accum.ap
            wr += self._keys(accum)
        sc_, bi_ = self._ap(scale), self._ap(bias)
        return self.sc.op("act", lambda e: e.activation(out=o, in_=i, func=func, scale=sc_, bias=bi_, **kw), rd, wr,
                          cost=self._ecost("act", self._n(i)))

    def tt(self, eng, out, a, b, op):
        o, x, y = out.ap, a.ap, b.ap
        return self.sc.op(eng, lambda e: e.tensor_tensor(out=o, in0=x, in1=y, op=op), self._keys(a, b), self._keys(out),
                          cost=self._ecost(eng, self._n(o)))

    def ts(self, eng, out, a, s1, op0, s2=None, op1=None):
        o, x = out.ap, a.ap
        p1, p2 = self._ap(s1), self._ap(s2)
        kw = {} if op1 is None else {"op1": op1}
        return self.sc.op(eng, lambda e: e.tensor_scalar(out=o, in0=x, scalar1=p1, scalar2=p2, op0=op0, **kw),
                          self._keys(a, s1, s2), self._keys(out), cost=self._ecost(eng, self._n(o)))

    def stt(self, eng, out, a, scalar, b, op0, op1):
        o, x, y = out.ap, a.ap, b.ap
        s_ = self._ap(scalar)
        return self.sc.op(eng, lambda e: e.scalar_tensor_tensor(out=o, in0=x, scalar=s_, in1=y, op0=op0, op1=op1),
                          self._keys(a, scalar, b), self._keys(out), cost=self._ecost(eng, self._n(o)))

    def cp(self, eng, out, a):
        o, x = out.ap, a.ap
        c = self._ecost(eng, self._n(o))
        if eng == "act":
            return self.sc.op("act", lambda e: e.activation(out=o, in_=x, func=AF.Copy), self._keys(a), self._keys(out),
                              cost=c)
        return self.sc.op(eng, lambda e: e.tensor_copy(out=o, in_=x), self._keys(a), self._keys(out), cost=c)

    def memset(self, eng, out, val):
        o = out.ap
        return self.sc.op(eng, lambda e: e.memset(o, val), [], self._keys(out), cost=self._ecost(eng, self._n(o)))

    def gen(self, eng, name, reads, writes, **kw):
        kw2 = {k: self._ap(v_) for k, v_ in kw.items()}
        n = self._n(kw2["out"]) if "out" in kw2 else 64
        mult = 2.0 if name == "tensor_tensor_scan" else 1.0
        return self.sc.op(eng, lambda e: getattr(e, name)(**kw2), self._keys(*reads), self._keys(*writes),
                          cost=self._ecost(eng, n, mult))


class T:
    __slots__ = ("ap", "key")

    def __init__(self, ap, key):
        self.ap = ap
        self.key = key

    def __getitem__(self, idx):
        return T(self.ap[idx], self.key)

    def k(self, *suffix):
        return T(self.ap, (self.key,) + tuple(suffix))

    def re(self, pattern, **kw):
        return T(self.ap.rearrange(pattern, **kw), self.key)

    def bc(self, shape):
        ap = self.ap
        while len(ap.shape) < len(shape):
            ap = ap.unsqueeze(len(ap.shape))
        return T(ap.broadcast_to(list(shape)), self.key)

    def cast(self, dt_):
        return T(self.ap.bitcast(dt_), self.key)

    def ubc(self, axis, shape):
        return T(self.ap.unsqueeze(axis).broadcast_to(list(shape)), self.key)


class Prog(Builder):
    def declare_io(self):
        self.x = self.din("x", [S, D])
        self.pos = self.din("positions", [NT, 128], I32)
        for n, shp in (("ln_mix_g", [2, D]), ("ln_mix_b", [2, D]), ("ln_ffn_g", [2, D]), ("ln_ffn_b", [2, D]),
                       ("w_ff1", [2, D, DFF]), ("w_ff2", [2, DFF, D]),
                       ("ssm_lam_re", [64, 64]), ("ssm_lam_im", [64, 64]), ("ssm_log_dt", [1, 64]),
                       ("ssm_b_re", [64, 64, 16]), ("ssm_b_im", [64, 64, 16]),
                       ("ssm_c_re", [64, 16, 64]), ("ssm_c_im", [64, 16, 64]), ("ssm_d", [D]),
                       ("ssm_w_glu", [D, 2 * D]), ("ssm_w_out", [D, D]),
                       ("kv_w_a", [D, 320]), ("kv_norm_g", [1, 256]), ("kv_w_b", [256, 2048]),
                       ("q_w_a", [D, 384]), ("q_norm_g", [1, 384]), ("q_w_b", [384, 1536]),
                       ("attn_w_o", [D, D])):
            self.din(n, shp)
        self.out = self.dout("out", [S, D])

    def consts(self):
        self.identf = self.alloc([128], F32, "identf")
        self.identb = self.alloc([128], BF16, "identb")
        self.memset("pool", self.identf, 0.0)
        self.gen("pool", "affine_select", [self.identf], [self.identf], out=self.identf, in_=self.identf,
                 pattern=[[-1, 128]], compare_op=ALU.not_equal, fill=1.0, base=0, channel_multiplier=1)
        self.cp("pool", self.identb, self.identf)

    def phase_a0(self):
        m = self.ar.mark()
        xs = [self.alloc([D], F32, "xs") for _ in range(3)]
        if _os.environ.get("DUMMY_DMA"):
            dmy = self.alloc([D], F32, "dmy")
            for _ in range(int(_os.environ["DUMMY_DMA"])):
                self.dma(dmy, self.x[0:128, :])
        for tt in range(NT):
            slot = xs[tt % 3]
            self.dma(slot, self.x[tt * 128:(tt + 1) * 128, :], reads_extra=[dmy] if _os.environ.get("DUMMY_DMA") and tt == 0 else ())
            if _os.environ.get("A0_VIA_DVE"):
                if tt == 0:
                    xs2 = [self.alloc([D], F32, "xs2") for _ in range(3)]
                self.cp("pool", xs2[tt % 3], slot)
                slot = xs2[tt % 3]
            for half in range(2):
                bank = self.bank((tt % 2) * 2 + half)
                for j in range(4):
                    kt = half * 4 + j
                    self.tr(bank[:, j * 128:(j + 1) * 128], slot[:, kt * 128:(kt + 1) * 128], self.identf)
                o = self.UT[:, half * 4:half * 4 + 4, :, tt * 16:(tt + 1) * 16].k(half, tt).re("p a s j -> p a j s")
                i = bank.re("p (a j s) -> p a j s", a=4, s=8)
                self.cp("act" if half == 0 else "dve", o, i)
        self.sc.barrier()
        self.ar.reset(m)

    def phase_a1(self):
        dr = self.dram
        A = self.alloc
        self.UT = A([8, 8, 512], BF16, "UT")
        self.ut_top = self.ar.mark()
        self.ZupW = [A([8, 8, 128], BF16, "zupw_re"), A([8, 8, 128], BF16, "zupw_im")]
        self.CarW = [A([32, 8, 32], BF16, "carw_re"), A([32, 8, 32], BF16, "carw_nim")]
        self.BD = A([8, 8, 128], BF16, "bd")
        self.Rch = A([32], F32, "rch")
        self.f8 = A([32], F32, "f8")
        self.Dsk = A([8], F32, "dsk")
        m = self.ar.mark()
        sm = lambda n: A([32], F32, n)
        CIN = [A([8, 128], F32, "cin_re"), A([8, 128], F32, "cin_im")]
        for ci, nm in enumerate(("ssm_c_re", "ssm_c_im")):
            self.memset("pool", CIN[ci], 0.0)
            v = dr[nm].rearrange("(kt qq g) c p -> qq g c kt p", qq=4, g=2)
            for qq in range(4):
                for g2 in range(2):
                    p0 = qq * 32 + g2 * 16
                    self.dma(CIN[ci][p0:p0 + 16, :, g2 * 64:(g2 + 1) * 64], v[qq, g2])
        ldt = sm("ldt")
        for g2 in range(2):
            src = dr["ssm_log_dt"][0, g2::2].partition_broadcast(64)
            self.dma(ldt[g2 * 64:(g2 + 1) * 64, :], src, allow_slow_non_contiguous=True)
        self.dma(self.Dsk, dr["ssm_d"].rearrange("(k p) -> p k", p=128), allow_slow_non_contiguous=True)
        Bbr, Bbi = A([32, 16], F32, "Bbr"), A([32, 16], F32, "Bbi")
        BbBD = [A([32, 32], F32, "bbbd_re"), A([32, 32], F32, "bbbd_im")]
        lr, li = sm("lr"), sm("li")
        m0, m1 = A([1], F32, "m0"), A([1], F32, "m1")
        self.memset("pool", m0, 0.0)
        self.memset("pool", m0[0:64], 1.0)
        self.memset("pool", m1, 0.0)
        self.memset("pool", m1[64:128], 1.0)
        bdm = A([128], F32, "bdm")
        self.memset("pool", bdm, 1.0)
        for i in range(4):
            blk = bdm[:, 32 * i:32 * i + 32]
            self.gen("pool", "affine_select", [bdm], [bdm], out=blk, in_=blk, pattern=[[0, 32]],
                     compare_op=ALU.is_ge, fill=0.0, base=-32 * i, channel_multiplier=1)
            self.gen("pool", "affine_select", [bdm], [bdm], out=blk, in_=blk, pattern=[[0, 32]],
                     compare_op=ALU.is_ge, fill=0.0, base=32 * i + 31, channel_multiplier=-1)
        cre, cim = sm("cre"), sm("cim")
        AR = [sm(f"ar{k}") for k in range(9)]
        AI = [sm(f"ai{k}") for k in range(9)]
        m_short = self.ar.mark()
        Lin = A([256], F32, "Lin")
        self.memset("pool", Lin, 0.0)
        self.dma(Lin[0:32, 0:128], dr["ssm_lam_re"].rearrange("(q g) p -> q (g p)", g=2))
        self.dma(Lin[0:32, 128:256], dr["ssm_lam_im"].rearrange("(q g) p -> q (g p)", g=2))
        Br, Bi = A([32, 16], F32, "Br"), A([32, 16], F32, "Bi")
        for g2 in range(2):
            self.dma(Br[g2 * 64:(g2 + 1) * 64], dr["ssm_b_re"].rearrange("(q g) p c -> g p q c", g=2)[g2])
            self.dma(Bi[g2 * 64:(g2 + 1) * 64], dr["ssm_b_im"].rearrange("(q g) p c -> g p q c", g=2)[g2])
        bk = self.bank(0)
        self.tr(bk[:, 0:32], Lin[0:32, 0:128], self.identf[0:32, 0:32])
        self.tr(bk[:, 32:64], Lin[0:32, 128:256], self.identf[0:32, 0:32])
        self.cp("dve", lr, bk[:, 0:32])
        self.cp("dve", li, bk[:, 32:64])
        dt, xr, mag, ang, trn, trc = sm("dt"), sm("xr"), sm("mag"), sm("ang"), sm("trn"), sm("trc")
        ti = A([32], I32, "ti")
        rs, rc, sn, cs = sm("rs"), sm("rc"), sm("sn"), sm("cs")
        self.act(dt, ldt, AF.Exp)
        self.tt("dve", xr, lr, dt, ALU.mult)
        self.act(mag, xr, AF.Exp)
        self.act(self.Rch, xr, AF.Exp, scale=8.0)
        self.tt("dve", ang, li, dt, ALU.mult)
        self.ts("dve", trn, ang, 1.0 / TWO_PI, ALU.mult)
        self.ts("dve", trc, trn, 0.25, ALU.add)
        self.cp("dve", ti, trn)
        self.tt("dve", rs, trn, ti, ALU.subtract)
        ti2 = A([32], I32, "ti2")
        self.cp("dve", ti2, trc)
        self.tt("dve", rc, trc, ti2, ALU.subtract)
        self.act(sn, rs, AF.Sin, scale=TWO_PI)
        self.act(cs, rc, AF.Sin, scale=TWO_PI)
        t8 = sm("t8")
        ti3 = A([32], I32, "ti3")
        self.ts("dve", t8, rs, 8.0, ALU.mult)
        self.cp("dve", ti3, t8)
        self.tt("dve", self.f8, t8, ti3, ALU.subtract)
        are, aim = sm("are"), sm("aim")
        self.tt("dve", are, mag, cs, ALU.mult)
        self.tt("dve", aim, mag, sn, ALU.mult)
        den, t1, t2, inv, am1 = sm("den"), sm("t1"), sm("t2"), sm("inv"), sm("am1")
        self.tt("dve", t1, lr, lr, ALU.mult)
        self.tt("dve", t2, li, li, ALU.mult)
        self.tt("dve", den, t1, t2, ALU.add)
        self.gen("dve", "reciprocal", [den], [inv], out=inv, in_=den)
        self.ts("dve", am1, are, -1.0, ALU.add)
        t3, t4 = sm("t3"), sm("t4")
        self.tt("dve", t3, am1, lr, ALU.mult)
        self.tt("dve", t4, aim, li, ALU.mult)
        self.tt("dve", t1, t3, t4, ALU.add)
        self.tt("dve", cre, t1, inv, ALU.mult)
        t5, t6 = sm("t5"), sm("t6")
        self.tt("dve", t5, aim, lr, ALU.mult)
        self.tt("dve", t6, am1, li, ALU.mult)
        self.tt("dve", t2, t5, t6, ALU.subtract)
        self.tt("dve", cim, t2, inv, ALU.mult)
        self.memset("pool", AR[0], 1.0)
        self.memset("pool", AI[0], 0.0)
        self.cp("dve", AR[1], are)
        self.cp("dve", AI[1], aim)
        u1, u2, u3, u4 = sm("u1"), sm("u2"), sm("u3"), sm("u4")
        for k in range(1, 8):
            self.tt("dve", u1, AR[k], are, ALU.mult)
            self.tt("dve", u2, AI[k], aim, ALU.mult)
            self.tt("dve", AR[k + 1], u1, u2, ALU.subtract)
            self.tt("dve", u3, AR[k], aim, ALU.mult)
            self.tt("dve", u4, AI[k], are, ALU.mult)
            self.tt("dve", AI[k + 1], u3, u4, ALU.add)
        w1, w2 = A([32, 16], F32, "w1"), A([32, 16], F32, "w2")
        s16 = [128, 32, 16]
        self.tt("dve", w1, Br, cre.bc(s16), ALU.mult)
        self.tt("dve", w2, Bi, cim.bc(s16), ALU.mult)
        self.tt("dve", Bbr, w1, w2, ALU.subtract)
        w3, w4 = A([32, 16], F32, "w3"), A([32, 16], F32, "w4")
        self.tt("dve", w3, Bi, cre.bc(s16), ALU.mult)
        self.tt("dve", w4, Br, cim.bc(s16), ALU.mult)
        self.tt("dve", Bbi, w3, w4, ALU.add)
        for src, dst in ((Bbr, BbBD[0]), (Bbi, BbBD[1])):
            self.ts("pool", dst[:, :, 0:16], src, m0, ALU.mult)
            self.ts("pool", dst[:, :, 16:32], src, m1, ALU.mult)
        self.sc.barrier()
        self.ar.reset(m_short)
        s32 = [128, 32, 32]
        TP = [A([32, 32], F32, f"tp{i}") for i in range(4)]
        bi_ = 0
        for s_ in range(8):
            k = 7 - s_
            Are, Aim, e2, e4 = TP
            self.tt("dve", Are, BbBD[0], AR[k].bc(s32), ALU.mult)
            self.tt("pool", e2, BbBD[1], AI[k].bc(s32), ALU.mult)
            self.tt("dve", Are, Are, e2, ALU.subtract)
            self.tt("dve", Aim, BbBD[1], AR[k].bc(s32), ALU.mult)
            self.tt("pool", e4, BbBD[0], AI[k].bc(s32), ALU.mult)
            self.tt("dve", Aim, Aim, e4, ALU.add)
            for ri, src_ in enumerate((Are, Aim)):
                for half in range(2):
                    bk = self.bank(bi_ % 4)
                    bi_ += 1
                    for j in range(4):
                        kt = half * 4 + j
                        self.tr(bk[:, j * 128:(j + 1) * 128],
                                src_[:, kt * 4:(kt + 1) * 4, :].re("p a b -> p (a b)"), self.identf)
                    self.cp("act", self.ZupW[ri][:, half * 4:half * 4 + 4, s_, :], bk.re("p (a b) -> p a b", a=4))
        CBD = [A([32, 32], F32, "cbd_re"), A([32, 32], F32, "cbd_im")]
        for ci in range(2):
            for half in range(2):
                bk = self.bank(4 + (ci * 2 + half) % 4)
                for j in range(4):
                    kt = half * 4 + j
                    self.tr(bk[:, j * 128:(j + 1) * 128], CIN[ci][:, kt, :], self.identf)
                self.cp("dve", CBD[ci][:, half * 16:half * 16 + 16, :], bk.re("p (a b) -> p a b", a=16))
        for k in range(9):
            CAre, nCAim, f2, f4 = TP
            self.tt("dve", CAre, CBD[0], AR[k].bc(s32), ALU.mult)
            self.tt("pool", f2, CBD[1], AI[k].bc(s32), ALU.mult)
            self.tt("dve", CAre, CAre, f2, ALU.subtract)
            self.tt("dve", nCAim, CBD[0], AI[k].bc(s32), ALU.mult)
            self.tt("pool", f4, CBD[1], AR[k].bc(s32), ALU.mult)
            self.stt("dve", nCAim, nCAim, -1.0, f4, ALU.mult, ALU.subtract)
            if k >= 1:
                self.cp("act", self.CarW[0][:, :, k - 1, :], CAre)
                self.cp("act", self.CarW[1][:, :, k - 1, :], nCAim)
            if k <= 7:
                for half in range(2):
                    bk = self.bank(half * 2 + (k % 2))
                    for j in range(4):
                        kt = half * 4 + j
                        o = bk[:, j * 128:(j + 1) * 128]
                        fl = lambda t_: t_[:, kt * 4:(kt + 1) * 4, :].re("p a b -> p (a b)")
                        self.mm(o, fl(BbBD[0]), fl(CAre), True, False)
                        self.mm(o, fl(BbBD[1]), fl(nCAim), False, True)
                    self.tt("dve", self.BD[:, half * 4:half * 4 + 4, k, :], bk.re("p (a b) -> p a b", a=4),
                            T(bdm.ap.unsqueeze(1).broadcast_to([128, 4, 128]), bdm.key), ALU.mult)
        self.sc.barrier()
        self.ar.reset(m)

    def phase_a2(self):
        A = self.alloc
        UT = self.UT
        m = self.ar.mark()
        iota_f = A([512], F32, "iota_f")
        mi = self.ar.mark()
        iota_i = A([512], I32, "iota_i")
        self.gen("pool", "iota", [], [iota_i], out=iota_i, pattern=[[1, 512]], base=0, channel_multiplier=0)
        self.cp("pool", iota_f, iota_i)
        self.ar.reset(mi)
        ph = A([512], F32, "ph")
        pi = A([512], I32, "pi")
        ph2 = A([512], F32, "ph2")
        pi2 = A([512], I32, "pi2")
        halfpi = A([1], F32, "halfpi")
        self.memset("pool", halfpi, math.pi / 2.0)
        NSL = 2
        cos_t = [A([512], F32, "cos") for _ in range(NSL)]
        sin_t = [A([512], F32, "sin") for _ in range(NSL)]
        T1 = [A([512], F32, "t1") for _ in range(NSL)]
        T2 = [A([512], F32, "t2") for _ in range(NSL)]
        T3 = [A([512], F32, "t3") for _ in range(NSL)]
        GR = [A([512], F32, "gr") for _ in range(NSL)]
        GI = [A([512], F32, "gi") for _ in range(NSL)]
        NH = 8
        Hre = [A([512], BF16, "hre") for _ in range(NH)]
        Him = [A([512], BF16, "him") for _ in range(NH)]
        evt = [A([512], F32, "evt") for _ in range(2)]
        for h in Hre + Him:
            self.memset("pool", h[:, 0:1], 0.0)
        obi = 0
        n1 = 511
        for kt in range(8):
            for pl in range(4):
                q = kt * 4 + pl
                sl_ = q % NSL
                hs = q % NH
                zr, zi = self.bank(sl_), self.bank(2 + sl_)
                for ri, zb in ((0, zr), (1, zi)):
                    for s_ in range(8):
                        self.mm(zb, self.ZupW[ri][32 * pl:32 * pl + 32, kt, s_, :],
                                UT[32 * pl:32 * pl + 32, kt, s_, :].k(kt, s_),
                                s_ == 0, s_ == 7, tile_position=(32 * pl, 0))
                f8q = self.f8[:, q:q + 1]
                c_, s__ = cos_t[sl_], sin_t[sl_]
                self.act(ph, iota_f, AF.Identity, scale=f8q)
                self.cp("dve", pi, ph)
                self.tt("dve", ph, ph, pi, ALU.subtract)
                self.act(s__, ph, AF.Sin, scale=TWO_PI)
                self.act(ph2, ph, AF.Abs)
                self.act(c_, ph2, AF.Sin, scale=-TWO_PI, bias=halfpi)
                t1, t2, t3, gr, gi = T1[sl_], T2[sl_], T3[sl_], GR[sl_], GI[sl_]
                self.tt("dve", t1, zr, c_, ALU.mult)
                self.tt("dve", t2, zi, s__, ALU.mult)
                self.tt("pool", t1, t1, t2, ALU.add)
                self.tt("dve", t2, zi, c_, ALU.mult)
                self.tt("dve", t3, zr, s__, ALU.mult)
                self.tt("pool", t2, t2, t3, ALU.subtract)
                Rb = self.Rch[:, q:q + 1].bc([128, 512])
                self.gen("dve", "tensor_tensor_scan", [Rb, t1], [gr], out=gr, data0=Rb, data1=t1, initial=0.0,
                         op0=ALU.mult, op1=ALU.add)
                self.gen("dve", "tensor_tensor_scan", [Rb, t2], [gi], out=gi, data0=Rb, data1=t2, initial=0.0,
                         op0=ALU.mult, op1=ALU.add)
                self.tt("dve", t1[:, 0:n1], gr[:, 0:n1], c_[:, 0:n1], ALU.mult)
                self.tt("pool", t3[:, 0:n1], gi[:, 0:n1], s__[:, 0:n1], ALU.mult)
                self.tt("pool", Hre[hs][:, 1:512], t1[:, 0:n1], t3[:, 0:n1], ALU.subtract)
                self.tt("dve", t2[:, 0:n1], gi[:, 0:n1], c_[:, 0:n1], ALU.mult)
                self.tt("pool", t3[:, 0:n1], gr[:, 0:n1], s__[:, 0:n1], ALU.mult)
                self.tt("pool", Him[hs][:, 1:512], t2[:, 0:n1], t3[:, 0:n1], ALU.add)
            for t in range(7, -1, -1):
                ob = self.bank(4 + (obi % 4))
                obi += 1
                for s_ in range(t + 1):
                    self.mm(ob, self.BD[:, kt, t - s_, :], UT[:, kt, s_, :].k(kt, s_), s_ == 0, False)
                for pl in range(4):
                    q = kt * 4 + pl
                    hs = q % NH
                    o_ = ob[32 * pl:32 * pl + 32, :]
                    self.mm(o_, self.CarW[0][:, q, t, :], Hre[hs], False, False, tile_position=(0, 32 * pl))
                    self.mm(o_, self.CarW[1][:, q, t, :], Him[hs], False, True, tile_position=(0, 32 * pl))
                ev = evt[t % 2]
                ut = UT[:, kt, t, :].k(kt, t)
                self.stt("dve", ev, ut, self.Dsk[:, kt:kt + 1], ob, ALU.mult, ALU.add)
                self.act(ut, ev, AF.Gelu)
        self.sc.barrier()
        self.ar.reset(m)

    def load_w(self, dst, src, nkt, split=None):
        v = src.rearrange("(k p) n -> p k n", p=128)
        n = dst.ap.shape[2]
        cw = min(n, 2048)
        if split is None:
            stg = [self.alloc([cw], F32, "wstage") for _ in range(2)]
        else:
            stg = [s_[:, 0:cw] for s_ in split]
        engs = ("pool", "act", "dve")
        i = 0
        for kt in range(nkt):
            for c0 in range(0, n, cw):
                s_ = stg[i % 2]
                self.dma(s_, v[:, kt, c0:c0 + cw])
                self.cp(engs[i % 3], dst[:, kt, c0:c0 + cw].k("w", kt), s_)
                i += 1
    def load_bcast(self, dst, row_ap):
        self.dma(dst, row_ap.partition_broadcast(128))

    def layernorm(self, r, out, gb, bb, scr):
        st, mv, sd = scr["st"], scr["mv"], scr["sd"]
        for c in range(2):
            self.gen("dve", "bn_stats", [r], [st], out=st[:, c, :], in_=r[:, c * 512:(c + 1) * 512])
        self.gen("dve", "bn_aggr", [st], [mv], out=mv, in_=st.re("p a b -> p (a b)"))
        self.act(sd, mv[:, 1:2], AF.Sqrt, scale=1.0, bias=scr["eps"])
        self.gen("dve", "reciprocal", [sd], [sd], out=sd, in_=sd)
        self.ts("dve", out, r, mv[:, 0:1], ALU.subtract, sd, ALU.mult)
        self.tt("pool", out, out, gb, ALU.mult)
        self.tt("pool", out, out, bb, ALU.add)

    def ln_scratch(self):
        A = self.alloc
        eps = A([1], F32, "eps")
        self.memset("pool", eps, LN_EPS)
        return [{"st": A([2, 6], F32, "st"), "mv": A([2], F32, "mv"), "sd": A([1], F32, "sd"), "eps": eps}
                for _ in range(2)]

    def to_feature_major(self, h, hT, banks):
        for half in range(2):
            bk = banks[half]
            for j in range(4):
                kt = half * 4 + j
                self.tr(bk[:, j * 128:(j + 1) * 128], h[:, kt * 128:(kt + 1) * 128], self.identf)
            self.cp("act", hT[:, half * 4:half * 4 + 4, :], bk.re("p (a b) -> p a b", a=4))

    def phase_b1(self):
        A = self.alloc
        dr = self.dram
        UT = self.UT
        m = self.ar.mark()
        self.H1 = self.dscr("H1", [S, D], F32)
        self.H1T = self.dscr("H1T", [128, 8, S], BF16)
        Wg = A([8, 2 * D], BF16, "wglu")
        Wo = A([8, D], BF16, "wout")
        self.load_w(Wg, dr["ssm_w_glu"], 8)
        self.load_w(Wo, dr["ssm_w_out"], 8)
        gb, bb = A([D], F32, "gb"), A([D], F32, "bb")
        self.load_bcast(gb, dr["ln_mix_g"][0])
        self.load_bcast(bb, dr["ln_mix_b"][0])
        lsc = self.ln_scratch()
        zT = [A([8, 512], BF16, "zT") for _ in range(2)]
        sg = [A([512], F32, "sg") for _ in range(2)]
        xt = [A([D], F32, "xt") for _ in range(2)]
        rr = [A([D], F32, "rr") for _ in range(2)]
        h1 = [A([D], F32, "h1") for _ in range(2)]
        h1T = [A([8, 512], BF16, "h1T") for _ in range(2)]
        xperm = self.x.rearrange("(c j t) d -> c t j d", j=64, t=8)
        h1perm = self.H1.rearrange("(c j t) d -> c t j d", j=64, t=8)
        n = 0
        for tb in range(S // 512):
            z = zT[tb % 2]
            hTb = h1T[tb % 2]
            for mt in range(8):
                bv, bg = self.bank(2 * (mt % 2)), self.bank(2 * (mt % 2) + 1)
                for which, bk in ((0, bv), (1, bg)):
                    c0 = which * D + mt * 128
                    for kt in range(8):
                        self.mm(bk, Wg[:, kt, c0:c0 + 128].k("w", kt),
                                UT[:, kt, :, tb * 64:(tb + 1) * 64], kt == 0, kt == 7)
                s_ = sg[mt % 2]
                self.act(s_, bg, AF.Sigmoid)
                self.tt("dve", z[:, mt, :], bv, s_, ALU.mult)
            for sub in range(4):
                tt_ = tb * 4 + sub
                x_, r_, h_ = xt[n % 2], rr[n % 2], h1[n % 2]
                sc_ = lsc[n % 2]
                n += 1
                for tl in range(2):
                    self.dma(x_[tl * 64:(tl + 1) * 64, :], xperm[tb, 2 * sub + tl])
                for nh in range(2):
                    bk = self.bank(4 + nh)
                    for kt in range(8):
                        self.mm(bk, z[:, kt, sub * 128:(sub + 1) * 128], Wo[:, kt, nh * 512:(nh + 1) * 512].k("w", kt),
                                kt == 0, kt == 7)
                    self.stt("dve", r_[:, nh * 512:(nh + 1) * 512], x_[:, nh * 512:(nh + 1) * 512], ALPHA, bk,
                             ALU.mult, ALU.add)
                self.layernorm(r_, h_, gb, bb, sc_)
                for tl in range(2):
                    self.dma(h1perm[tb, 2 * sub + tl], h_[tl * 64:(tl + 1) * 64, :])
                for half in range(2):
                    bk = self.bank(6 + half)
                    for j in range(4):
                        kt = half * 4 + j
                        self.tr(bk[:, j * 128:(j + 1) * 128], h_[:, kt * 128:(kt + 1) * 128], self.identf)
                    o = hTb[:, half * 4:half * 4 + 4, :].re("p a (j t) -> p a t j", t=8)[:, :, 2 * sub:2 * sub + 2, :]
                    self.cp("act", o, bk.re("p (a t j) -> p a t j", a=4, t=2))
            self.dma(self.H1T[:, :, tb * 512:(tb + 1) * 512], hTb)
        self.sc.barrier()
        self.ar.reset(m)

    def phase_ffn(self, layer, Hin, HinT, Hout, HoutT):
        A = self.alloc
        dr = self.dram
        m = self.ar.mark()
        W1 = A([8, DFF], BF16, "w1")
        W2 = A([32, D], BF16, "w2")
        stg = [A([1024], F32, "wstage") for _ in range(4)]
        v1 = dr["w_ff1"][layer].rearrange("(k p) n -> p k n", p=128)
        v2 = dr["w_ff2"][layer].rearrange("(k p) n -> p k n", p=128)
        engs = ("pool", "act", "dve")
        li = 0
        for cb in range(8):
            for kg in range(4):
                s_ = stg[li % 4]
                s2 = s_.re("p (a b) -> p a b", a=2)
                self.dma(s2, v1[:, kg * 2:(kg + 1) * 2, cb * 512:(cb + 1) * 512])
                self.cp(engs[li % 3], W1[:, kg * 2:(kg + 1) * 2, cb * 512:(cb + 1) * 512].k("w1", cb), s2)
                li += 1
            for r2 in range(4):
                s_ = stg[li % 4]
                f0 = cb * 4 + r2
                self.dma(s_, v2[:, f0, :])
                self.cp(engs[li % 3], W2[:, f0, :].k("w2", f0), s_)
                li += 1
        gb, bb = A([D], F32, "gb"), A([D], F32, "bb")
        self.load_bcast(gb, dr["ln_ffn_g"][layer])
        self.load_bcast(bb, dr["ln_ffn_b"][layer])
        lsc = self.ln_scratch()
        hT = [A([8, 256], BF16, "hT") for _ in range(2)]
        hin = [A([D], F32, "hin") for _ in range(4)]
        rr = [A([D], F32, "rr") for _ in range(2)]
        ho = [A([D], F32, "ho") for _ in range(2)]
        hoT = [A([8, 128], BF16, "hoT") for _ in range(2)]
        rl = [A([256], F32, "rl") for _ in range(3)]
        aT = [A([256], BF16, "aT") for _ in range(4)]
        n = 0
        fb = 0
        for blk in range(S // 256):
            t0 = blk * 256
            h_T = hT[blk % 2]
            hs_ = [hin[(2 * blk) % 4], hin[(2 * blk + 1) % 4]]
            self.dma(h_T, HinT[:, :, t0:t0 + 256])
            for sub in range(2):
                self.dma(hs_[sub], Hin[t0 + sub * 128:t0 + (sub + 1) * 128, :])
            acc = [[self.bank(0), self.bank(1)], [self.bank(2), self.bank(3)]]

            def ff2(ft, a_):
                for sub in range(2):
                    for nh in range(2):
                        self.mm(acc[sub][nh], a_[:, sub * 128:(sub + 1) * 128],
                                W2[:, ft, nh * 512:(nh + 1) * 512].k("w2", ft), ft == 0, ft == 31)

            prev = None
            for ft in range(32):
                nfb = 3 if HoutT is None else 2
                bk = self.bank(4 + fb % nfb)
                r_ = rl[fb % 3]
                a_ = aT[fb % 4]
                fb += 1
                for kt in range(8):
                    self.mm(bk[:, 0:256], W1[:, kt, ft * 128:(ft + 1) * 128].k("w1", ft // 4), h_T[:, kt, :], kt == 0, kt == 7)
                self.act(r_, bk[:, 0:256], AF.Relu)
                self.tt("pool", a_, r_, r_, ALU.mult)
                if prev is not None:
                    ff2(*prev)
                prev = (ft, a_)
            ff2(*prev)
            for sub in range(2):
                tt_ = blk * 2 + sub
                r2, h_o, h_oT = rr[n % 2], ho[n % 2], hoT[n % 2]
                sc_ = lsc[n % 2]
                n += 1
                for nh in range(2):
                    self.stt("dve", r2[:, nh * 512:(nh + 1) * 512], hs_[sub][:, nh * 512:(nh + 1) * 512], ALPHA,
                             acc[sub][nh], ALU.mult, ALU.add)
                self.layernorm(r2, h_o, gb, bb, sc_)
                self.dma(Hout[tt_ * 128:(tt_ + 1) * 128, :], h_o)
                if HoutT is not None:
                    self.to_feature_major(h_o, h_oT, (self.bank(6), self.bank(7)))
                    self.dma(HoutT[:, :, tt_ * 128:(tt_ + 1) * 128], h_oT)
        self.sc.barrier()
        self.ar.reset(m)

    def rmsnorm_tm(self, out_bf, bank_ap, n, g_b, st, mv, rs, eps):
        self.gen("dve", "bn_stats", [bank_ap], [st], out=st, in_=bank_ap)
        self.gen("dve", "bn_aggr", [st], [mv], out=mv, in_=st)
        self.stt("dve", rs, mv[:, 0:1], mv[:, 0:1], mv[:, 1:2], ALU.mult, ALU.add)
        self.act(rs, rs, AF.Sqrt, scale=1.0, bias=eps)
        self.gen("dve", "reciprocal", [rs], [rs], out=rs, in_=rs)
        self.stt("dve", out_bf, bank_ap, rs, g_b, ALU.mult, ALU.mult)

    def phase_c0(self):
        A = self.alloc
        dr = self.dram
        self.Wkvb = A([2, 2048], BF16, "wkvb")
        self.Wqb = A([3, 1536], BF16, "wqb")
        self.load_w(self.Wkvb, dr["kv_w_b"], 2)
        self.load_w(self.Wqb, dr["q_w_b"], 3)
        import os
        if os.environ.get("C0_PAD"):
            self.alloc([int(os.environ["C0_PAD"])], F32, "pad")
        self.cosT = A([NT, 32], F32, "cosT")
        self.sinT = A([NT, 32], F32, "sinT")
        self.ckvT = A([2, S], BF16, "ckvT")
        self.kropeT = A([S], BF16, "kropeT")
        self.cqT = A([3, S], BF16, "cqT")
        self.c_top = self.ar.mark()
        Wkva = A([8, 320], BF16, "wkva")
        Wqa = A([8, 384], BF16, "wqa")
        self.load_w(Wkva, dr["kv_w_a"], 8)
        self.load_w(Wqa, dr["q_w_a"], 8)
        gkv, gq = A([256], F32, "gkv"), A([384], F32, "gq")
        self.load_bcast(gkv, dr["kv_norm_g"][0])
        self.load_bcast(gq, dr["q_norm_g"][0])
        eps = A([1], F32, "rmseps")
        self.memset("pool", eps, RMS_EPS)
        import os
        SK = os.environ.get("C0_SKIP", "").split(",")
        C0NT = int(os.environ.get("C0_NT", str(NT)))
        m2 = self.ar.mark()
        pin = A([128], I32, "pin")
        pinf = A([128], F32, "pinf")
        posf = A([NT], F32, "posf")
        invf = A([32], F32, "invf")
        self.memset("pool", pinf, 0.0)
        self.dma(pin[0:NT, :], self.pos)
        self.cp("dve", pinf[0:NT, :], pin[0:NT, :])
        bk = self.bank(0)
        if "ptr" not in SK:
            self.tr(bk[:, 0:NT], pinf[0:NT, :], self.identf[0:NT, 0:NT])
            self.cp("dve", posf, bk[:, 0:NT])
        iv = (np.float32(10000.0) ** (-(np.arange(32, dtype=np.float32) / np.float32(32.0)))).astype(np.float32)
        for i_ in range(32):
            self.memset("pool", invf[:, i_:i_ + 1], float(iv[i_]))
        s3 = [128, NT, 32]
        if "tab" in SK:
            self.sc.barrier()
            self.ar.reset(m2)
            return
        ang = A([NT, 32], F32, "ang")
        angc = A([NT, 32], F32, "angc")
        ai = A([NT, 32], I32, "ai")
        self.tt("dve", ang, posf.bc(s3), invf.ubc(1, s3), ALU.mult)
        self.ts("dve", ang, ang, 1.0 / TWO_PI, ALU.mult)
        self.ts("dve", angc, ang, 0.25, ALU.add)
        self.cp("dve", ai, ang)
        self.tt("dve", ang, ang, ai, ALU.subtract)
        self.act(self.sinT, ang, AF.Sin, scale=TWO_PI)
        self.cp("dve", ai, angc)
        self.tt("dve", angc, angc, ai, ALU.subtract)
        self.act(self.cosT, angc, AF.Sin, scale=TWO_PI)
        self.sc.barrier()
        if not _os.environ.get("NO_M2_RESET"):
            self.ar.reset(m2)
        h2T = [A([8, 128], BF16, "h2T") for _ in range(2)]
        stk = [A([6], F32, "stk") for _ in range(2)]
        stq = [A([6], F32, "stq") for _ in range(2)]
        mvk = [A([2], F32, "mvk") for _ in range(2)]
        mvq = [A([2], F32, "mvq") for _ in range(2)]
        rk = [A([1], F32, "rk") for _ in range(2)]
        rq = [A([1], F32, "rq") for _ in range(2)]
        ckv_tm = [A([256], F32, "ckv_tm") for _ in range(2)]
        kr_tm = [A([128], F32, "kr_tm") for _ in range(2)]
        cq_tm = [A([384], F32, "cq_tm") for _ in range(2)]
        ra = [[A([32], F32, "ra") for _ in range(4)] for _ in range(2)]
        for tt_ in range(C0NT):
            sl_ = tt_ % 2
            tok = slice(tt_ * 128, (tt_ + 1) * 128)
            hT = h2T[sl_]
            self.dma(hT, self.H2T[:, :, tok])
            bA, bB, bC = self.bank(sl_), self.bank(2 + sl_), self.bank(4 + sl_)
            for kt in range(8):
                self.mm(bA[:, 0:320], hT[:, kt, :], Wkva[:, kt, :].k("w", kt), kt == 0, kt == 7)
            for kt in range(8):
                self.mm(bB[:, 0:384], hT[:, kt, :], Wqa[:, kt, :].k("w", kt), kt == 0, kt == 7)
            if "rms" not in SK:
                self.rmsnorm_tm(ckv_tm[sl_], bA[:, 0:256], 256, gkv, stk[sl_], mvk[sl_], rk[sl_], eps)
                self.rmsnorm_tm(cq_tm[sl_], bB[:, 0:384], 384, gq, stq[sl_], mvq[sl_], rq[sl_], eps)
            x1, x2 = bA[:, 256:288], bA[:, 288:320]
            c_, s_ = self.cosT[:, tt_, :], self.sinT[:, tt_, :]
            a1, a2, a3, a4 = ra[sl_]
            if "rope" not in SK:
                self.tt("dve", a1, x1, c_, ALU.mult)
                self.tt("dve", a2, x2, s_, ALU.mult)
                self.tt("dve", a3, x1, s_, ALU.mult)
                self.tt("dve", a4, x2, c_, ALU.mult)
            kr3 = kr_tm[sl_].re("p (r c) -> p r c", r=2)
            s2 = [128, 2, 32]
            for r_ in range(2):
                self.tt(KR_ENG, kr3[:, r_, 0:32], a1, a2, ALU.subtract)
                self.tt(KR_ENG, kr3[:, r_, 32:64], a3, a4, ALU.add)
            bC2 = self.bank(6 + sl_)
            srcs = [ckv_tm[sl_][:, 0:128], ckv_tm[sl_][:, 128:256], kr_tm[sl_], cq_tm[sl_][:, 0:128],
                    cq_tm[sl_][:, 128:256], cq_tm[sl_][:, 256:384]]
            for j, s__ in enumerate(srcs):
                dst = bC[:, j * 128:(j + 1) * 128] if j < 4 else bC2[:, (j - 4) * 128:(j - 3) * 128]
                self.tr(dst, s__, self.identf)
            if "ev1" not in SK:
                self.cp("act", self.ckvT[:, :, tok], bC[:, 0:256].re("p (a b) -> p a b", a=2))
            if "ev2" not in SK:
                self.cp("act", self.kropeT[:, tok], bC[:, 256:384])
            if "ev3" not in SK:
                self.cp("dve", self.cqT[:, 0, tok], bC[:, 384:512])
            if "ev4" not in SK:
                self.cp("dve", self.cqT[:, 1:3, tok], bC2[:, 0:256].re("p (a b) -> p a b", a=2))
        self.sc.barrier()
        self.ar.reset(self.c_top)

    def phase_c(self):
        A = self.alloc
        self.OT = self.dscr("OT", [128, 8, S], BF16)
        onesf = A([128], F32, "onesf")
        self.memset("pool", onesf, 1.0)
        accD = [A([512], F32, "accD") for _ in range(2)]
        accP = [A([512], F32, "accP") for _ in range(2)]
        knT = A([4, S], BF16, "knT")
        vtm = A([NT, 4, 128], BF16, "vtm")
        QN = [A([4, 512], BF16, "QN") for _ in range(2)]
        QR = [A([2, 512], BF16, "QR") for _ in range(2)]
        qr_tm = [A([256], F32, "qr_tm") for _ in range(2)]
        ra = [[A([4, 32], F32, "qra") for _ in range(4)] for _ in range(2)]
        PT = [A([512], BF16, "PT") for _ in range(4)]
        rd = [A([512], F32, "rd") for _ in range(1)]
        oTb = [A([4, 512], BF16, "oTb") for _ in range(2)]
        Wkvb4 = self.Wkvb.re("p k (h two d) -> p k h two d", two=2, d=128)
        Wqb3 = self.Wqb.re("p k (h e) -> p k h e", e=192)
        ev = 0
        for hh in range(2):
            for tb in range(S // 512):
                for hl in range(4):
                    h = 4 * hh + hl
                    bk = self.bank(ev % 3)
                    for j in range(2):
                        self.mm(bk, self.Wkvb[:, j, h * 256:h * 256 + 128].k("w", j),
                                self.ckvT[:, j, tb * 512:(tb + 1) * 512], j == 0, j == 1)
                    self.cp("act" if ev % 2 == 0 else "dve", knT[:, hl, tb * 512:(tb + 1) * 512], bk)
                    ev += 1
            for tt_ in range(NT):
                bk = self.bank(ev % 3)
                for j in range(2):
                    self.mm(bk, self.ckvT[:, j, tt_ * 128:(tt_ + 1) * 128],
                            Wkvb4[:, j, 4 * hh:4 * hh + 4, 1, :].k("w", j), j == 0, j == 1)
                self.cp("act" if ev % 2 == 0 else "dve", vtm[:, tt_, :, :], bk.re("p (a b) -> p a b", a=4))
                ev += 1
            sbi = 0
            pti = 0
            for qb in range(S // 512):
                qn, qr = QN[qb % 2], QR[qb % 2]
                qs = slice(qb * 512, (qb + 1) * 512)
                for hl in range(4):
                    h = 4 * hh + hl
                    bk = self.bank(7)
                    for j in range(3):
                        self.mm(bk, self.Wqb[:, j, h * 192:h * 192 + 128].k("w", j), self.cqT[:, j, qs], j == 0, j == 2)
                    self.cp("dve", qn[:, hl, :], bk)
                for sub in range(4):
                    tt_ = qb * 4 + sub
                    tok = slice(tt_ * 128, (tt_ + 1) * 128)
                    bk = self.bank(7)
                    for j in range(3):
                        self.mm(bk[:, 0:256], self.cqT[:, j, tok], Wqb3[:, j, 4 * hh:4 * hh + 4, 128:192].k("w", j),
                                j == 0, j == 2)
                    b4 = bk[:, 0:256].re("p (h r c) -> p h r c", h=4, r=2)
                    x1, x2 = b4[:, :, 0, :], b4[:, :, 1, :]
                    s4 = [128, 4, 32]
                    c_, s_ = self.cosT[:, tt_, :].ubc(1, s4), self.sinT[:, tt_, :].ubc(1, s4)
                    a1, a2, a3, a4 = ra[sub % 2]
                    self.tt("dve", a1, x1, c_, ALU.mult)
                    self.tt("dve", a2, x2, s_, ALU.mult)
                    self.tt("dve", a3, x1, s_, ALU.mult)
                    self.tt("dve", a4, x2, c_, ALU.mult)
                    q3 = qr_tm[sub % 2].re("p (h e) -> p h e", h=4)
                    self.tt("pool", q3[:, :, 0:32], a1, a2, ALU.subtract)
                    self.tt("pool", q3[:, :, 32:64], a3, a4, ALU.add)
                    for pr in range(2):
                        self.tr(bk[:, 256 + pr * 128:256 + (pr + 1) * 128], qr_tm[sub % 2][:, pr * 128:(pr + 1) * 128],
                                self.identf)
                    self.cp("act", qr[:, :, sub * 128:(sub + 1) * 128],
                            bk[:, 256:512].re("p (a b) -> p a b", a=2))
                ot = oTb[qb % 2]
                for hl in range(4):
                    pr, e = hl // 2, hl % 2
                    bO, bD = self.bank(3 + 2 * (hl % 2)), self.bank(4 + 2 * (hl % 2))
                    nk = 4 * qb + 4

                    def c0_of(kt):
                        i = kt - 4 * qb
                        return 128 * i if i > 0 else 0

                    def scores(kt):
                        sb = self.bank(kt % 3)
                        c0 = c0_of(kt)
                        ks = slice(kt * 128, (kt + 1) * 128)
                        self.mm(sb[:, c0:512], knT[:, hl, ks], qn[:, hl, c0:512], True, False)
                        self.mm(sb[:, c0:512], self.kropeT[64 * e:64 * e + 64, ks], qr[64 * e:64 * e + 64, pr, c0:512],
                                False, True, tile_position=(64 * e, 0))

                    aD, aP = accD[hl % 2], accP[hl % 2]
                    self.memset("pool", aD, 0.0)
                    self.memset("pool", aP, 0.0)
                    scores(0)
                    for kt in range(nk):
                        if kt + 1 < nk:
                            scores(kt + 1)
                        sb = self.bank(kt % 3)
                        c0 = c0_of(kt)
                        pt = PT[pti % 4]
                        pti += 1
                        self.act(pt[:, c0:512], sb[:, c0:512], AF.Exp, scale=SM_SCALE)
                        if kt >= 4 * qb:
                            dg = pt[:, c0:c0 + 128]
                            self.gen("pool", "affine_select", [pt], [pt], out=dg, in_=dg, pattern=[[1, 128]],
                                     compare_op=ALU.is_ge, fill=0.0, base=0, channel_multiplier=-1)
                        self.mm(bO[:, c0:512], vtm[:, kt, hl, :], pt[:, c0:512], kt == 0, kt == nk - 1)
                        if kt % 3 == 2:
                            self.tt("pool", aP[:, c0:512], aP[:, c0:512], pt[:, c0:512], ALU.add)
                        else:
                            self.tt("dve", aD[:, c0:512], aD[:, c0:512], pt[:, c0:512], ALU.add)
                    self.mm(bD, onesf, aD, True, False)
                    self.mm(bD, onesf, aP, False, True)
                    r_ = rd[0]
                    self.gen("dve", "reciprocal", [bD], [r_], out=r_, in_=bD)
                    self.tt("dve", ot[:, hl, :], bO, r_, ALU.mult)
                self.dma(self.OT[:, 4 * hh:4 * hh + 4, qs], ot)
        self.sc.barrier()
        if self.stop_after in ("c", "c2"):
            self.dbg_dump("dbg_ckvT", self.ckvT, [128, 2, S], BF16)
            self.dbg_dump("dbg_kropeT", self.kropeT, [128, S], BF16)
            self.dbg_dump("dbg_cqT", self.cqT, [128, 3, S], BF16)
            self.dbg_dump("dbg_knT", knT, [128, 4, S], BF16)
            self.dbg_dump("dbg_vtm", vtm, [128, NT, 4, 128], BF16)
            self.dbg_dump("dbg_cosT", self.cosT, [128, NT, 32], F32)
            self.sc.barrier()
        self.ar.off = 0
        self.consts()
        self.sc.barrier()

    def phase_c2(self):
        A = self.alloc
        dr = self.dram
        m = self.ar.mark()
        self.H3 = self.H1
        self.H3T = self.H1T
        Wo = A([8, D], BF16, "wo")
        self.load_w(Wo, dr["attn_w_o"], 8)
        gb, bb = A([D], F32, "gb"), A([D], F32, "bb")
        self.load_bcast(gb, dr["ln_mix_g"][1])
        self.load_bcast(bb, dr["ln_mix_b"][1])
        lsc = self.ln_scratch()
        oT = [A([8, 128], BF16, "oT") for _ in range(3)]
        h2 = [A([D], F32, "h2") for _ in range(3)]
        h3 = [A([D], F32, "h3") for _ in range(2)]
        h3T = [A([8, 128], BF16, "h3T") for _ in range(2)]
        for tt_ in range(NT):
            tok = slice(tt_ * 128, (tt_ + 1) * 128)
            o_, r_, h_, hT_ = oT[tt_ % 3], h2[tt_ % 3], h3[tt_ % 2], h3T[tt_ % 2]
            self.dma(o_, self.OT[:, :, tok])
            self.dma(r_, self.H2[tok, :])
            for nh in range(2):
                bk = self.bank(2 * (tt_ % 2) + nh)
                for kt in range(8):
                    self.mm(bk, o_[:, kt, :], Wo[:, kt, nh * 512:(nh + 1) * 512].k("w", kt), kt == 0, kt == 7)
                self.stt("dve", r_[:, nh * 512:(nh + 1) * 512], r_[:, nh * 512:(nh + 1) * 512], ALPHA, bk,
                         ALU.mult, ALU.add)
            self.layernorm(r_, h_, gb, bb, lsc[tt_ % 2])
            self.dma(self.H3[tok, :], h_)
            self.to_feature_major(h_, hT_, (self.bank(4 + 2 * (tt_ % 2)), self.bank(5 + 2 * (tt_ % 2))))
            self.dma(self.H3T[:, :, tok], hT_)
        self.sc.barrier()
        self.ar.reset(m)

    def finish(self):
        nc = self.nc
        with self.es:
            with nc.Block() as block:
                self.sc.emit(nc, block, self.es)
        return nc

    def dbg_dump(self, name, src, shape, dtype):
        t = self.dout(name, shape, dtype)
        self.dma(t, src)
        return t

    def dbg_copyT(self, name, src_dram):
        t = self.dout(name, [128, 8, S], BF16)
        buf = [self.alloc([8, 512], BF16, "dbgbufT") for _ in range(2)]
        for i in range(S // 512):
            self.dma(buf[i % 2], src_dram[:, :, i * 512:(i + 1) * 512])
            self.dma(t[:, :, i * 512:(i + 1) * 512], buf[i % 2])
        return t

    def dbg_copy(self, name, src_dram, shape, dtype):
        t = self.dout(name, shape, dtype)
        buf = [self.alloc([D], F32, "dbgbuf") for _ in range(2)]
        for i in range(shape[0] // 128):
            self.dma(buf[i % 2], src_dram[i * 128:(i + 1) * 128, :])
            self.dma(t[i * 128:(i + 1) * 128, :], buf[i % 2])
        return t


def build(stop_after=None, debug=()):
    b = Prog(stop_after, debug)
    b.declare_io()
    b.consts()
    b.phase_a1()
    if stop_after == "a1":
        b.dbg_dump("dbg_zupw_re", b.ZupW[0], [128, 8, 8, 128], BF16)
        b.dbg_dump("dbg_zupw_im", b.ZupW[1], [128, 8, 8, 128], BF16)
        b.dbg_dump("dbg_carw_re", b.CarW[0], [128, 32, 8, 32], BF16)
        b.dbg_dump("dbg_carw_nim", b.CarW[1], [128, 32, 8, 32], BF16)
        b.dbg_dump("dbg_bd", b.BD, [128, 8, 8, 128], BF16)
        b.dbg_dump("dbg_rch", b.Rch, [128, 32], F32)
        b.dbg_dump("dbg_f8", b.f8, [128, 32], F32)
        b.sc.barrier()
        return b.finish()
    b.phase_a0()
    if stop_after == "a0":
        b.dbg_dump("dbg_UT", b.UT, [128, 8, 8, 512], BF16)
        b.sc.barrier()
        return b.finish()
    b.phase_a2()
    if stop_after == "a2":
        b.dbg_dump("dbg_UT", b.UT, [128, 8, 8, 512], BF16)
        b.sc.barrier()
        return b.finish()
    b.ar.reset(b.ut_top)
    b.phase_b1()
    b.ar.off = 0
    b.consts()
    b.sc.barrier()
    if stop_after == "b1":
        b.dbg_copy("dbg_H1", b.H1, [S, D], F32)
        b.sc.barrier()
        return b.finish()
    b.H2 = b.dscr("H2", [S, D], F32)
    b.H2T = b.dscr("H2T", [128, 8, S], BF16)
    b.phase_ffn(0, b.H1, b.H1T, b.H2, b.H2T)
    if stop_after == "b2":
        b.dbg_copy("dbg_H2", b.H2, [S, D], F32)
        b.sc.barrier()
        return b.finish()
    b.phase_c0()
    if stop_after == "c0":
        b.dbg_dump("dbg_ckvT", b.ckvT, [128, 2, S], BF16)
        b.dbg_dump("dbg_kropeT", b.kropeT, [128, S], BF16)
        b.dbg_dump("dbg_cqT", b.cqT, [128, 3, S], BF16)
        b.dbg_dump("dbg_cosT", b.cosT, [128, NT, 32], F32)
        b.dbg_dump("dbg_sinT", b.sinT, [128, NT, 32], F32)
        b.sc.barrier()
        return b.finish()
    b.phase_c()
    if stop_after == "c":
        b.dbg_copyT("dbg_OT", b.OT)
        b.sc.barrier()
        return b.finish()
    b.phase_c2()
    if stop_after == "c2":
        b.dbg_copy("dbg_H3", b.H3, [S, D], F32)
        b.sc.barrier()
        return b.finish()
    b.phase_ffn(1, b.H3, b.H3T, b.out, None)
    return b.finish()


WEIGHT_NAMES = ["ln_mix_g", "ln_mix_b", "ln_ffn_g", "ln_ffn_b", "w_ff1", "w_ff2", "ssm_lam_re", "ssm_lam_im",
                "ssm_log_dt", "ssm_b_re", "ssm_b_im", "ssm_c_re", "ssm_c_im", "ssm_d", "ssm_w_glu", "ssm_w_out",
                "kv_w_a", "kv_norm_g", "kv_w_b", "q_w_a", "q_norm_g", "q_w_b", "attn_w_o"]


def make_in_maps(inputs, n_cores=8):
    f = lambda a: np.ascontiguousarray(np.asarray(a))
    shared = {
        "ln_mix_g": f(inputs["ln_mix_g"]), "ln_mix_b": f(inputs["ln_mix_b"]),
        "ln_ffn_g": f(inputs["ln_ffn_g"]), "ln_ffn_b": f(inputs["ln_ffn_b"]),
        "w_ff1": f(inputs["w_ff1"]), "w_ff2": f(inputs["w_ff2"]),
        "ssm_lam_re": f(inputs["ssm_lam_re"])[0], "ssm_lam_im": f(inputs["ssm_lam_im"])[0],
        "ssm_log_dt": f(inputs["ssm_log_dt"]).reshape(1, 64),
        "ssm_b_re": f(inputs["ssm_b_re"])[0], "ssm_b_im": f(inputs["ssm_b_im"])[0],
        "ssm_c_re": f(inputs["ssm_c_re"])[0], "ssm_c_im": f(inputs["ssm_c_im"])[0],
        "ssm_d": f(inputs["ssm_d"])[0], "ssm_w_glu": f(inputs["ssm_w_glu"])[0],
        "ssm_w_out": f(inputs["ssm_w_out"])[0],
        "kv_w_a": f(inputs["kv_w_a"]), "kv_norm_g": f(inputs["kv_norm_g"]).reshape(1, 256),
        "kv_w_b": f(inputs["kv_w_b"]),
        "q_w_a": f(inputs["q_w_a"])[0], "q_norm_g": f(inputs["q_norm_g"]).reshape(1, 384),
        "q_w_b": f(inputs["q_w_b"])[0], "attn_w_o": f(inputs["attn_w_o"])[0],
    }
    x = f(inputs["x"])
    pos = f(inputs["positions"]).astype(np.int32)
    maps = []
    for c in range(n_cores):
        m = dict(shared)
        m["x"] = x[c]
        m["positions"] = pos[c].reshape(NT, 128)
        maps.append(m)
    return maps


def kernel(**inputs):
    nc = build()
    maps = make_in_maps(inputs)
    res = run_bass_kernel_spmd(nc, maps, core_ids=list(range(8)))
    return np.stack([np.asarray(r["out"]) for r in res.results], axis=0).astype(np.float32)
```

```python
import math
import os as _os
from contextlib import ExitStack

import numpy as np
import concourse.bass as bass
import concourse.mybir as mybir
from concourse.bass_utils import run_bass_kernel_spmd

F32 = mybir.dt.float32
BF16 = mybir.dt.bfloat16
I32 = mybir.dt.int32
AF = mybir.ActivationFunctionType
ALU = mybir.AluOpType
AX = mybir.AxisListType

S = 4096
D = 1024
NT = S // 128
DFF = 4096
PAD = 8
ALPHA = 4.0 ** 0.25
LN_EPS = 1e-5
RMS_EPS = 1e-6
SM_SCALE = 192.0 ** -0.5
TWO_PI = 2.0 * math.pi
KR_ENG = _os.environ.get("KR_ENG", "pool")
ATTACH_WAIT = _os.environ.get("ATTACH_WAIT", "1") == "1"


def sl(start, n, step=1):
    return slice(start, start + (n - 1) * step + 1, step)


import heapq


class _Op:
    __slots__ = ("eng", "fn", "deps", "odeps", "needs_inc", "is_dma", "dsem", "dval", "sem_idx", "count", "idx",
                 "cost", "nbytes", "seg", "start", "fin", "nsucc", "succ", "est", "bar_waits")

    def __init__(self, eng, fn, is_dma=False):
        self.eng = eng
        self.fn = fn
        self.deps = []
        self.odeps = []
        self.needs_inc = False
        self.is_dma = is_dma
        self.dsem = None
        self.dval = 0
        self.sem_idx = 0
        self.count = 0
        self.cost = 0.1
        self.nbytes = 0
        self.bar_waits = None


class Sched:
    ENGS = ("sp", "act", "dve", "pool", "pe")
    LIMIT = 30000
    XLAT = 1.3
    DMA_LAT = 2.0
    DMA_BW = 160e3

    def __init__(self, n_dma_sems=int(_os.environ.get("NDMA", "24")), reorder=True):
        self.segs = [[]]
        self.lw = {}
        self.rd = {}
        self.n_dma = n_dma_sems
        nsw = 8
        self.dma_pool = {"sp": list(range(0, n_dma_sems - nsw)), "act": list(range(0, n_dma_sems - nsw)),
                         "pool": list(range(n_dma_sems - nsw, n_dma_sems))}
        self.nops = 0
        self.reorder = reorder

    def _deps(self, o, reads, writes):
        extra = [k for k in reads if isinstance(k, tuple) and k and k[0] == "ps" and k not in writes]
        if extra:
            writes = list(writes) + extra
        deps = {}
        for k in reads:
            w = self.lw.get(k)
            if w is not None:
                deps[id(w)] = w
        for k in writes:
            w = self.lw.get(k)
            if w is not None:
                deps[id(w)] = w
            for r in self.rd.get(k, ()):
                deps[id(r)] = r
        for d in deps.values():
            if d is o:
                continue
            if (not d.is_dma) and d.eng == o.eng and (not o.is_dma) and o.eng == "pe":
                o.odeps.append(d)
                continue
            if not d.is_dma:
                d.needs_inc = True
            o.deps.append(d)
        for k in reads:
            self.rd.setdefault(k, []).append(o)
        for k in writes:
            self.lw[k] = o
            self.rd[k] = []

    def op(self, eng, fn, reads=(), writes=(), cost=0.1):
        o = _Op(eng, fn)
        o.cost = cost
        o.idx = self.nops
        self.nops += 1
        self._deps(o, reads, writes)
        self.segs[-1].append(o)
        return o

    def dma(self, eng, fn, reads=(), writes=(), nbytes=0):
        o = _Op(eng, fn, is_dma=True)
        o.nbytes = nbytes
        o.cost = 0.06 if eng != "pool" else 1.0
        o.idx = self.nops
        self.nops += 1
        self._deps(o, reads, writes)
        self.segs[-1].append(o)
        return o

    def barrier(self):
        if self.segs[-1]:
            self.segs.append([])
        self.lw = {}
        self.rd = {}

    def _schedule_segment(self, seg, t0):
        order = {e: [] for e in self.ENGS}
        if not seg:
            return order, t0
        inseg = set(id(o) for o in seg)
        for o in seg:
            o.succ = []
            o.nsucc = 0
            o.est = t0
        for o in seg:
            for d in o.deps + o.odeps:
                if id(d) in inseg:
                    d.succ.append(o)
                    o.nsucc += 1
        if not self.reorder:
            t = {e: t0 for e in self.ENGS}
            tend = t0
            for o in seg:
                order[o.eng].append(o)
            return order, tend
        bl = {}
        for o in reversed(seg):
            c = o.cost + (self.DMA_LAT + o.nbytes / self.DMA_BW if o.is_dma else 0.0)
            m_ = 0.0
            for s_ in o.succ:
                v = bl[id(s_)] + (0.0 if (s_.eng == o.eng and not o.is_dma) else self.XLAT)
                if v > m_:
                    m_ = v
            bl[id(o)] = c + m_
        PRI = "bl"
        for o in seg:
            o.idx = (-bl[id(o)], o.idx) if PRI == "bl" else o.idx
        hest = {e: [] for e in self.ENGS}
        hidx = {e: [] for e in self.ENGS}
        free = {e: t0 for e in self.ENGS}
        for o in seg:
            if o.nsucc == 0:
                heapq.heappush(hest[o.eng], (o.est, o.idx, o))
        dma_free = t0
        remaining = len(seg)
        tend = t0
        while remaining:
            best = None
            for e in self.ENGS:
                he, hi = hest[e], hidx[e]
                while he and he[0][0] <= free[e]:
                    _, ix, oo = heapq.heappop(he)
                    heapq.heappush(hi, (ix, oo))
                if hi:
                    st, ix = free[e], hi[0][0]
                elif he:
                    st, ix = he[0][0], he[0][1]
                else:
                    continue
                if best is None or (st, ix) < best[:2]:
                    best = (st, ix, e)
            st, ix, e = best
            if hidx[e]:
                _, o = heapq.heappop(hidx[e])
            else:
                _, _, o = heapq.heappop(hest[e])
            o.start = st
            if o.is_dma:
                issue_done = st + o.cost
                ts = max(issue_done, dma_free)
                done = ts + o.nbytes / self.DMA_BW
                dma_free = done
                o.fin = done + self.DMA_LAT
                free[e] = issue_done
            else:
                o.fin = st + o.cost
                free[e] = o.fin
            tend = max(tend, o.fin)
            order[e].append(o)
            remaining -= 1
            for s_ in o.succ:
                lat = 0.0 if (s_.eng == o.eng and not o.is_dma) else self.XLAT
                if o.fin + lat > s_.est:
                    s_.est = o.fin + lat
                s_.nsucc -= 1
                if s_.nsucc == 0:
                    heapq.heappush(hest[s_.eng], (s_.est, s_.idx, s_))
        return order, tend

    def finalize(self):
        if self.segs[-1]:
            self.segs.append([])
        t0 = 0.0
        self.eng_ops = {e: [] for e in self.ENGS}
        seg_orders = []
        for seg in self.segs:
            order, t0 = self._schedule_segment(seg, t0)
            seg_orders.append(order)
        self.est_total_us = t0
        cnt = {e: 0 for e in self.ENGS}
        si = {e: 0 for e in self.ENGS}
        dma_cnt = [0] * self.n_dma
        dma_rr = {"sp": 0, "act": 0, "pool": 0}
        for sidx, order in enumerate(seg_orders):
            if sidx > 0:
                prev = seg_orders[sidx - 1]
                lasts = []
                for e in self.ENGS:
                    for o in reversed(self.eng_ops[e]):
                        if o.fn is not None and not o.is_dma:
                            lasts.append(o)
                            break
                bw = {}
                for o in lasts:
                    if not o.needs_inc:
                        o.needs_inc = True
                        if cnt[o.eng] >= self.LIMIT:
                            si[o.eng] += 1
                            cnt[o.eng] = 0
                        cnt[o.eng] += 1
                        o.count = cnt[o.eng]
                        o.sem_idx = si[o.eng]
                    bw[(o.eng, o.sem_idx)] = o.count
                for s_ in range(self.n_dma):
                    if dma_cnt[s_]:
                        bw[("d", s_)] = dma_cnt[s_]
                for e in self.ENGS:
                    b = _Op(e, None)
                    b.bar_waits = dict(bw)
                    self.eng_ops[e].append(b)
            for e in self.ENGS:
                for o in order[e]:
                    if o.is_dma:
                        pool = self.dma_pool[e]
                        rrk = "sp" if e == "act" else e
                        s_ = pool[dma_rr[rrk] % len(pool)]
                        dma_rr[rrk] += 1
                        o.bar_waits = {("d", s_): dma_cnt[s_]} if dma_cnt[s_] else None
                        dma_cnt[s_] += 16
                        o.dsem = s_
                        o.dval = dma_cnt[s_]
                    elif o.needs_inc:
                        if cnt[e] >= self.LIMIT:
                            si[e] += 1
                            cnt[e] = 0
                        cnt[e] += 1
                        o.count = cnt[e]
                        o.sem_idx = si[e]
                    self.eng_ops[e].append(o)
        self.nsem = {e: si[e] + 1 for e in self.ENGS}

    def emit(self, nc, block, es):
        self.finalize()
        esem = {e: [es.enter_context(nc.semaphore(f"s_{e}_{i}")) for i in range(self.nsem[e])] for e in self.ENGS}
        dsem = [es.enter_context(nc.semaphore(f"s_dma_{i}")) for i in range(self.n_dma)]

        def run(e, eng):
            waited = {}
            for o in self.eng_ops[e]:
                need = {}
                if o.bar_waits:
                    need.update(o.bar_waits)
                for d in o.deps:
                    if d.is_dma:
                        key = ("d", d.dsem)
                        val = d.dval
                    else:
                        key = (d.eng, d.sem_idx)
                        val = d.count
                    if val > need.get(key, 0):
                        need[key] = val
                todo = []
                for key, val in need.items():
                    if waited.get(key, 0) >= val:
                        continue
                    waited[key] = val
                    sem = dsem[key[1]] if key[0] == "d" else esem[key[0]][key[1]]
                    todo.append((1 if key[0] == "d" else 0, sem, val))
                todo.sort(key=lambda t_: t_[0])
                attach = None
                if todo and o.fn is not None and ATTACH_WAIT:
                    attach = todo.pop()
                for _, sem, val in todo:
                    eng.wait_ge(sem, val)
                if o.fn is None:
                    continue
                ins = o.fn(eng)
                if attach is not None:
                    ins._wait_ge(attach[1], attach[2])
                if o.is_dma:
                    ins.then_inc(dsem[o.dsem], 16)
                elif o.needs_inc:
                    ins.then_inc(esem[e][o.sem_idx], 1)

        @block.sync
        def _(eng):
            run("sp", eng)

        @block.scalar
        def _(eng):
            run("act", eng)

        @block.vector
        def _(eng):
            run("dve", eng)

        @block.gpsimd
        def _(eng):
            run("pool", eng)

        @block.tensor
        def _(eng):
            run("pe", eng)


class Arena:
    def __init__(self, nc, nbytes):
        self.nc = nc
        self.words = nbytes // 4
        self.t = nc.alloc_sbuf_tensor("arena", [128, self.words], F32)
        self.off = 0

    def alloc(self, shape, dtype=F32):
        n = 1
        for s_ in shape:
            n *= s_
        bpe = 4 if dtype in (F32, I32) else 2
        nw = (n * bpe + 3) // 4
        nw = (nw + 7) // 8 * 8
        assert self.off + nw <= self.words, f"SBUF arena overflow: need {self.off + nw} words of {self.words}"
        v = self.t[:, self.off:self.off + nw]
        self.off += nw
        if dtype != F32:
            v = v.bitcast(dtype)
        v = v[:, 0:n]
        if len(shape) == 2:
            v = v.rearrange("p (a b) -> p a b", a=shape[0])
        elif len(shape) == 3:
            v = v.rearrange("p (a b c) -> p a b c", a=shape[0], b=shape[1])
        elif len(shape) == 4:
            v = v.rearrange("p (a b c d) -> p a b c d", a=shape[0], b=shape[1], c=shape[2])
        return v

    def mark(self):
        return self.off

    def reset(self, m):
        self.off = m


class Builder:
    def __init__(self, stop_after=None, debug=()):
        self.stop_after = stop_after
        self.debug = set(debug)
        self.nc = bass.Bass("TRN2", target_bir_lowering=False)
        self.sc = Sched()
        self.es = ExitStack()
        nc = self.nc
        self.ar = Arena(nc, 212480)
        self.ps = [nc.alloc_psum_tensor(f"psb{i}", [128, 512], F32) for i in range(8)]
        self.dram = {}
        self.uid = 0

    def din(self, name, shape, dtype=F32):
        t = self.nc.dram_tensor(name, list(shape), dtype, kind="ExternalInput").ap()
        self.dram[name] = t
        return t

    def dout(self, name, shape, dtype=F32):
        t = self.nc.dram_tensor(name, list(shape), dtype, kind="ExternalOutput").ap()
        self.dram[name] = t
        return t

    def dscr(self, name, shape, dtype=F32):
        t = self.nc.dram_tensor(name, list(shape), dtype, kind="Internal").ap()
        self.dram[name] = t
        return t

    def alloc(self, shape, dtype=F32, name=None):
        self.uid += 1
        return T(self.ar.alloc(shape, dtype), (name or "t", self.uid))

    def bank(self, i):
        return T(self.ps[i][:, :], ("ps", i))

    @staticmethod
    def _ap(x):
        return x.ap if isinstance(x, T) else x

    @staticmethod
    def _keys(*xs):
        return [x.key for x in xs if isinstance(x, T)]

    @staticmethod
    def _n(ap):
        n = 1
        for s_ in ap.shape[1:]:
            n *= s_
        return n

    @staticmethod
    def _bpe(ap):
        return 4 if ap.dtype in (F32, I32) else 2

    def _ecost(self, eng, n, mult=1.0):
        if eng == "act":
            return 0.2 + n / 1400.0
        if eng == "dve":
            return 0.08 + mult * n / 960.0
        return 0.25 + mult * n / 500.0

    def dma(self, out, in_, eng="sp", reads_extra=(), **kw):
        o, i = self._ap(out), self._ap(in_)
        nb = self._n(o) * o.shape[0] * self._bpe(o)
        return self.sc.dma(eng, lambda e: e.dma_start(out=o, in_=i, **kw), self._keys(in_, *reads_extra),
                           self._keys(out), nbytes=nb)

    def mm(self, out, lhsT, rhs, start=True, stop=True, **kw):
        o, l, r = out.ap, lhsT.ap, rhs.ap
        c = 0.02 + max(self._n(r), 64) / 1950.0
        if l.dtype == F32:
            c *= 4
        return self.sc.op("pe", lambda e: e.matmul(o, l, r, start=start, stop=stop, **kw),
                          self._keys(lhsT, rhs), self._keys(out), cost=c)

    def tr(self, out, in_, ident, **kw):
        o, i, d = out.ap, in_.ap, ident.ap
        return self.sc.op("pe", lambda e: e.transpose(o, i, d, **kw), self._keys(in_, ident), self._keys(out),
                          cost=0.12)

    def act(self, out, in_, func, scale=1.0, bias=0.0, accum=None):
        o, i = out.ap, in_.ap
        kw = {}
        rd = self._keys(in_, scale, bias)
        wr = self._keys(out)
        if accum is not None:
            kw["accum_out"] = accum.ap
            wr += self._keys(accum)
        sc_, bi_ = self._ap(scale), self._ap(bias)
        return self.sc.op("act", lambda e: e.activation(out=o, in_=i, func=func, scale=sc_, bias=bi_, **kw), rd, wr,
                          cost=self._ecost("act", self._n(i)))

    def tt(self, eng, out, a, b, op):
        o, x, y = out.ap, a.ap, b.ap
        return self.sc.op(eng, lambda e: e.tensor_tensor(out=o, in0=x, in1=y, op=op), self._keys(a, b), self._keys(out),
                          cost=self._ecost(eng, self._n(o)))

    def ts(self, eng, out, a, s1, op0, s2=None, op1=None):
        o, x = out.ap, a.ap
        p1, p2 = self._ap(s1), self._ap(s2)
        kw = {} if op1 is None else {"op1": op1}
        return self.sc.op(eng, lambda e: e.tensor_scalar(out=o, in0=x, scalar1=p1, scalar2=p2, op0=op0, **kw),
                          self._keys(a, s1, s2), self._keys(out), cost=self._ecost(eng, self._n(o)))

    def stt(self, eng, out, a, scalar, b, op0, op1):
        o, x, y = out.ap, a.ap, b.ap
        s_ = self._ap(scalar)
        return self.sc.op(eng, lambda e: e.scalar_tensor_tensor(out=o, in0=x, scalar=s_, in1=y, op0=op0, op1=op1),
                          self._keys(a, scalar, b), self._keys(out), cost=self._ecost(eng, self._n(o)))

    def cp(self, eng, out, a):
        o, x = out.ap, a.ap
        c = self._ecost(eng, self._n(o))
        if eng == "act":
            return self.sc.op("act", lambda e: e.activation(out=o, in_=x, func=AF.Copy), self._keys(a), self._keys(out),
                              cost=c)
        return self.sc.op(eng, lambda e: e.tensor_copy(out=o, in_=x), self._keys(a), self._keys(out), cost=c)

    def memset(self, eng, out, val):
        o = out.ap
        return self.sc.op(eng, lambda e: e.memset(o, val), [], self._keys(out), cost=self._ecost(eng, self._n(o)))

    def gen(self, eng, name, reads, writes, **kw):
        kw2 = {k: self._ap(v_) for k, v_ in kw.items()}
        n = self._n(kw2["out"]) if "out" in kw2 else 64
        mult = 2.0 if name == "tensor_tensor_scan" else 1.0
        return self.sc.op(eng, lambda e: getattr(e, name)(**kw2), self._keys(*reads), self._keys(*writes),
                          cost=self._ecost(eng, n, mult))


class T:
    __slots__ = ("ap", "key")

    def __init__(self, ap, key):
        self.ap = ap
        self.key = key

    def __getitem__(self, idx):
        return T(self.ap[idx], self.key)

    def k(self, *suffix):
        return T(self.ap, (self.key,) + tuple(suffix))

    def re(self, pattern, **kw):
        return T(self.ap.rearrange(pattern, **kw), self.key)

    def bc(self, shape):
        ap = self.ap
        while len(ap.shape) < len(shape):
            ap = ap.unsqueeze(len(ap.shape))
        return T(ap.broadcast_to(list(shape)), self.key)

    def cast(self, dt_):
        return T(self.ap.bitcast(dt_), self.key)

    def ubc(self, axis, shape):
        return T(self.ap.unsqueeze(axis).broadcast_to(list(shape)), self.key)


class Prog(Builder):
    def declare_io(self):
        self.x = self.din("x", [S, D])
        self.pos = self.din("positions", [NT, 128], I32)
        for n, shp in (("ln_mix_g", [2, D]), ("ln_mix_b", [2, D]), ("ln_ffn_g", [2, D]), ("ln_ffn_b", [2, D]),
                       ("w_ff1", [2, D, DFF]), ("w_ff2", [2, DFF, D]),
                       ("ssm_lam_re", [64, 64]), ("ssm_lam_im", [64, 64]), ("ssm_log_dt", [1, 64]),
                       ("ssm_b_re", [64, 64, 16]), ("ssm_b_im", [64, 64, 16]),
                       ("ssm_c_re", [64, 16, 64]), ("ssm_c_im", [64, 16, 64]), ("ssm_d", [D]),
                       ("ssm_w_glu", [D, 2 * D]), ("ssm_w_out", [D, D]),
                       ("kv_w_a", [D, 320]), ("kv_norm_g", [1, 256]), ("kv_w_b", [256, 2048]),
                       ("q_w_a", [D, 384]), ("q_norm_g", [1, 384]), ("q_w_b", [384, 1536]),
                       ("attn_w_o", [D, D])):
            self.din(n, shp)
        self.out = self.dout("out", [S, D])

    def consts(self):
        self.identf = self.alloc([128], F32, "identf")
        self.identb = self.alloc([128], BF16, "identb")
        self.memset("pool", self.identf, 0.0)
        self.gen("pool", "affine_select", [self.identf], [self.identf], out=self.identf, in_=self.identf,
                 pattern=[[-1, 128]], compare_op=ALU.not_equal, fill=1.0, base=0, channel_multiplier=1)
        self.cp("pool", self.identb, self.identf)

    def phase_a0(self):
        xs = self.xs
        for tt in range(NT):
            slot = xs[tt % 2]
            self.dma(slot, self.x[tt * 128:(tt + 1) * 128, :])
            for half in range(2):
                bank = self.bank(4 + (tt % 2) * 2 + half)
                for j in range(4):
                    kt = half * 4 + j
                    self.tr(bank[:, j * 128:(j + 1) * 128], slot[:, kt * 128:(kt + 1) * 128], self.identf)
                o = self.UT[:, half * 4:half * 4 + 4, :, tt * 16:(tt + 1) * 16].k(half, tt).re("p a s j -> p a j s")
                i = bank.re("p (a j s) -> p a j s", a=4, s=8)
                self.cp("act" if half == 0 else "dve", o, i)
        self.sc.barrier()
        self.ar.reset(self.a1_mark)

    def phase_a1(self):
        dr = self.dram
        A = self.alloc
        self.UT = A([8, 8, 512], BF16, "UT")
        self.ut_top = self.ar.mark()
        self.ZupW = [A([8, 8, 128], BF16, "zupw_re"), A([8, 8, 128], BF16, "zupw_im")]
        self.CarW = [A([32, 8, 32], BF16, "carw_re"), A([32, 8, 32], BF16, "carw_nim")]
        self.BD = A([8, 8, 128], BF16, "bd")
        self.Rch = A([32], F32, "rch")
        self.f8 = A([32], F32, "f8")
        self.Dsk = A([8], F32, "dsk")
        m = self.ar.mark()
        sm = lambda n: A([32], F32, n)
        CIN = [A([8, 128], F32, "cin_re"), A([8, 128], F32, "cin_im")]
        for ci, nm in enumerate(("ssm_c_re", "ssm_c_im")):
            self.memset("pool", CIN[ci], 0.0)
            v = dr[nm].rearrange("(kt qq g) c p -> qq g c kt p", qq=4, g=2)
            for qq in range(4):
                for g2 in range(2):
                    p0 = qq * 32 + g2 * 16
                    self.dma(CIN[ci][p0:p0 + 16, :, g2 * 64:(g2 + 1) * 64], v[qq, g2])
        ldt = sm("ldt")
        for g2 in range(2):
            src = dr["ssm_log_dt"][0, g2::2].partition_broadcast(64)
            self.dma(ldt[g2 * 64:(g2 + 1) * 64, :], src, allow_slow_non_contiguous=True)
        self.dma(self.Dsk, dr["ssm_d"].rearrange("(k p) -> p k", p=128), allow_slow_non_contiguous=True)
        Bbr, Bbi = A([32, 16], F32, "Bbr"), A([32, 16], F32, "Bbi")
        BbBD = [A([32, 32], F32, "bbbd_re"), A([32, 32], F32, "bbbd_im")]
        lr, li = sm("lr"), sm("li")
        m0, m1 = A([1], F32, "m0"), A([1], F32, "m1")
        self.memset("pool", m0, 0.0)
        self.memset("pool", m0[0:64], 1.0)
        self.memset("pool", m1, 0.0)
        self.memset("pool", m1[64:128], 1.0)
        bdm = A([128], F32, "bdm")
        self.memset("pool", bdm, 1.0)
        for i in range(4):
            blk = bdm[:, 32 * i:32 * i + 32]
            self.gen("pool", "affine_select", [bdm], [bdm], out=blk, in_=blk, pattern=[[0, 32]],
                     compare_op=ALU.is_ge, fill=0.0, base=-32 * i, channel_multiplier=1)
            self.gen("pool", "affine_select", [bdm], [bdm], out=blk, in_=blk, pattern=[[0, 32]],
                     compare_op=ALU.is_ge, fill=0.0, base=32 * i + 31, channel_multiplier=-1)
        cre, cim = sm("cre"), sm("cim")
        AR = [sm(f"ar{k}") for k in range(9)]
        AI = [sm(f"ai{k}") for k in range(9)]
        self.xs = [A([D], F32, "xs") for _ in range(2)]
        m_short = self.ar.mark()
        Lin = A([256], F32, "Lin")
        self.memset("pool", Lin, 0.0)
        self.dma(Lin[0:32, 0:128], dr["ssm_lam_re"].rearrange("(q g) p -> q (g p)", g=2))
        self.dma(Lin[0:32, 128:256], dr["ssm_lam_im"].rearrange("(q g) p -> q (g p)", g=2))
        Br, Bi = A([32, 16], F32, "Br"), A([32, 16], F32, "Bi")
        for g2 in range(2):
            self.dma(Br[g2 * 64:(g2 + 1) * 64], dr["ssm_b_re"].rearrange("(q g) p c -> g p q c", g=2)[g2])
            self.dma(Bi[g2 * 64:(g2 + 1) * 64], dr["ssm_b_im"].rearrange("(q g) p c -> g p q c", g=2)[g2])
        bk = self.bank(0)
        self.tr(bk[:, 0:32], Lin[0:32, 0:128], self.identf[0:32, 0:32])
        self.tr(bk[:, 32:64], Lin[0:32, 128:256], self.identf[0:32, 0:32])
        self.cp("dve", lr, bk[:, 0:32])
        self.cp("dve", li, bk[:, 32:64])
        dt, xr, mag, ang, trn, trc = sm("dt"), sm("xr"), sm("mag"), sm("ang"), sm("trn"), sm("trc")
        ti = A([32], I32, "ti")
        rs, rc, sn, cs = sm("rs"), sm("rc"), sm("sn"), sm("cs")
        self.act(dt, ldt, AF.Exp)
        self.tt("dve", xr, lr, dt, ALU.mult)
        self.act(mag, xr, AF.Exp)
        self.act(self.Rch, xr, AF.Exp, scale=8.0)
        self.tt("dve", ang, li, dt, ALU.mult)
        self.ts("dve", trn, ang, 1.0 / TWO_PI, ALU.mult)
        self.ts("dve", trc, trn, 0.25, ALU.add)
        self.cp("dve", ti, trn)
        self.tt("dve", rs, trn, ti, ALU.subtract)
        ti2 = A([32], I32, "ti2")
        self.cp("dve", ti2, trc)
        self.tt("dve", rc, trc, ti2, ALU.subtract)
        self.act(sn, rs, AF.Sin, scale=TWO_PI)
        self.act(cs, rc, AF.Sin, scale=TWO_PI)
        t8 = sm("t8")
        ti3 = A([32], I32, "ti3")
        self.ts("dve", t8, rs, 8.0, ALU.mult)
        self.cp("dve", ti3, t8)
        self.tt("dve", self.f8, t8, ti3, ALU.subtract)
        are, aim = sm("are"), sm("aim")
        self.tt("dve", are, mag, cs, ALU.mult)
        self.tt("dve", aim, mag, sn, ALU.mult)
        den, t1, t2, inv, am1 = sm("den"), sm("t1"), sm("t2"), sm("inv"), sm("am1")
        self.tt("dve", t1, lr, lr, ALU.mult)
        self.tt("dve", t2, li, li, ALU.mult)
        self.tt("dve", den, t1, t2, ALU.add)
        self.gen("dve", "reciprocal", [den], [inv], out=inv, in_=den)
        self.ts("dve", am1, are, -1.0, ALU.add)
        t3, t4 = sm("t3"), sm("t4")
        self.tt("dve", t3, am1, lr, ALU.mult)
        self.tt("dve", t4, aim, li, ALU.mult)
        self.tt("dve", t1, t3, t4, ALU.add)
        self.tt("dve", cre, t1, inv, ALU.mult)
        t5, t6 = sm("t5"), sm("t6")
        self.tt("dve", t5, aim, lr, ALU.mult)
        self.tt("dve", t6, am1, li, ALU.mult)
        self.tt("dve", t2, t5, t6, ALU.subtract)
        self.tt("dve", cim, t2, inv, ALU.mult)
        self.memset("pool", AR[0], 1.0)
        self.memset("pool", AI[0], 0.0)
        self.cp("dve", AR[1], are)
        self.cp("dve", AI[1], aim)
        u1, u2, u3, u4 = sm("u1"), sm("u2"), sm("u3"), sm("u4")
        for k in range(1, 8):
            self.tt("dve", u1, AR[k], are, ALU.mult)
            self.tt("dve", u2, AI[k], aim, ALU.mult)
            self.tt("dve", AR[k + 1], u1, u2, ALU.subtract)
            self.tt("dve", u3, AR[k], aim, ALU.mult)
            self.tt("dve", u4, AI[k], are, ALU.mult)
            self.tt("dve", AI[k + 1], u3, u4, ALU.add)
        w1, w2 = A([32, 16], F32, "w1"), A([32, 16], F32, "w2")
        s16 = [128, 32, 16]
        self.tt("dve", w1, Br, cre.bc(s16), ALU.mult)
        self.tt("dve", w2, Bi, cim.bc(s16), ALU.mult)
        self.tt("dve", Bbr, w1, w2, ALU.subtract)
        w3, w4 = A([32, 16], F32, "w3"), A([32, 16], F32, "w4")
        self.tt("dve", w3, Bi, cre.bc(s16), ALU.mult)
        self.tt("dve", w4, Br, cim.bc(s16), ALU.mult)
        self.tt("dve", Bbi, w3, w4, ALU.add)
        for src, dst in ((Bbr, BbBD[0]), (Bbi, BbBD[1])):
            self.ts("pool", dst[:, :, 0:16], src, m0, ALU.mult)
            self.ts("pool", dst[:, :, 16:32], src, m1, ALU.mult)
        self.sc.barrier()
        self.ar.reset(m_short)
        s32 = [128, 32, 32]
        TP = [A([32, 32], F32, f"tp{i}") for i in range(4)]
        bi_ = 0
        for s_ in range(8):
            k = 7 - s_
            Are, Aim, e2, e4 = TP
            self.tt("dve", Are, BbBD[0], AR[k].bc(s32), ALU.mult)
            self.tt("pool", e2, BbBD[1], AI[k].bc(s32), ALU.mult)
            self.tt("dve", Are, Are, e2, ALU.subtract)
            self.tt("dve", Aim, BbBD[1], AR[k].bc(s32), ALU.mult)
            self.tt("pool", e4, BbBD[0], AI[k].bc(s32), ALU.mult)
            self.tt("dve", Aim, Aim, e4, ALU.add)
            for ri, src_ in enumerate((Are, Aim)):
                for half in range(2):
                    bk = self.bank(bi_ % 4)
                    bi_ += 1
                    for j in range(4):
                        kt = half * 4 + j
                        self.tr(bk[:, j * 128:(j + 1) * 128],
                                src_[:, kt * 4:(kt + 1) * 4, :].re("p a b -> p (a b)"), self.identf)
                    self.cp("act", self.ZupW[ri][:, half * 4:half * 4 + 4, s_, :], bk.re("p (a b) -> p a b", a=4))
        CBD = [A([32, 32], F32, "cbd_re"), A([32, 32], F32, "cbd_im")]
        for ci in range(2):
            for half in range(2):
                bk = self.bank(4 + (ci * 2 + half) % 4)
                for j in range(4):
                    kt = half * 4 + j
                    self.tr(bk[:, j * 128:(j + 1) * 128], CIN[ci][:, kt, :], self.identf)
                self.cp("dve", CBD[ci][:, half * 16:half * 16 + 16, :], bk.re("p (a b) -> p a b", a=16))
        for k in range(9):
            CAre, nCAim, f2, f4 = TP
            self.tt("dve", CAre, CBD[0], AR[k].bc(s32), ALU.mult)
            self.tt("pool", f2, CBD[1], AI[k].bc(s32), ALU.mult)
            self.tt("dve", CAre, CAre, f2, ALU.subtract)
            self.tt("dve", nCAim, CBD[0], AI[k].bc(s32), ALU.mult)
            self.tt("pool", f4, CBD[1], AR[k].bc(s32), ALU.mult)
            self.stt("dve", nCAim, nCAim, -1.0, f4, ALU.mult, ALU.subtract)
            if k >= 1:
                self.cp("act", self.CarW[0][:, :, k - 1, :], CAre)
                self.cp("act", self.CarW[1][:, :, k - 1, :], nCAim)
            if k <= 7:
                for half in range(2):
                    bk = self.bank(half * 2 + (k % 2))
                    for j in range(4):
                        kt = half * 4 + j
                        o = bk[:, j * 128:(j + 1) * 128]
                        fl = lambda t_: t_[:, kt * 4:(kt + 1) * 4, :].re("p a b -> p (a b)")
                        self.mm(o, fl(BbBD[0]), fl(CAre), True, False)
                        self.mm(o, fl(BbBD[1]), fl(nCAim), False, True)
                    self.tt("dve", self.BD[:, half * 4:half * 4 + 4, k, :], bk.re("p (a b) -> p a b", a=4),
                            T(bdm.ap.unsqueeze(1).broadcast_to([128, 4, 128]), bdm.key), ALU.mult)
        self.a1_mark = m

    def phase_a2(self):
        A = self.alloc
        UT = self.UT
        m = self.ar.mark()
        iota_f = A([512], F32, "iota_f")
        mi = self.ar.mark()
        iota_i = A([512], I32, "iota_i")
        self.gen("pool", "iota", [], [iota_i], out=iota_i, pattern=[[1, 512]], base=0, channel_multiplier=0)
        self.cp("pool", iota_f, iota_i)
        self.ar.reset(mi)
        ph = A([512], F32, "ph")
        pi = A([512], I32, "pi")
        ph2 = A([512], F32, "ph2")
        pi2 = A([512], I32, "pi2")
        halfpi = A([1], F32, "halfpi")
        self.memset("pool", halfpi, math.pi / 2.0)
        NSL = 2
        cos_t = [A([512], F32, "cos") for _ in range(NSL)]
        sin_t = [A([512], F32, "sin") for _ in range(NSL)]
        T1 = [A([512], F32, "t1") for _ in range(NSL)]
        T2 = [A([512], F32, "t2") for _ in range(NSL)]
        T3 = [A([512], F32, "t3") for _ in range(NSL)]
        GR = [A([512], F32, "gr") for _ in range(NSL)]
        GI = [A([512], F32, "gi") for _ in range(NSL)]
        NH = 8
        Hre = [A([512], BF16, "hre") for _ in range(NH)]
        Him = [A([512], BF16, "him") for _ in range(NH)]
        evt = [A([512], F32, "evt") for _ in range(2)]
        for h in Hre + Him:
            self.memset("pool", h[:, 0:1], 0.0)
        obi = 0
        n1 = 511
        for kt in range(8):
            for pl in range(4):
                q = kt * 4 + pl
                sl_ = q % NSL
                hs = q % NH
                zr, zi = self.bank(sl_), self.bank(2 + sl_)
                for ri, zb in ((0, zr), (1, zi)):
                    for s_ in range(8):
                        self.mm(zb, self.ZupW[ri][32 * pl:32 * pl + 32, kt, s_, :],
                                UT[32 * pl:32 * pl + 32, kt, s_, :].k(kt, s_),
                                s_ == 0, s_ == 7, tile_position=(32 * pl, 0))
                f8q = self.f8[:, q:q + 1]
                c_, s__ = cos_t[sl_], sin_t[sl_]
                self.act(ph, iota_f, AF.Identity, scale=f8q)
                self.cp("dve", pi, ph)
                self.tt("dve", ph, ph, pi, ALU.subtract)
                self.act(s__, ph, AF.Sin, scale=TWO_PI)
                self.act(ph2, ph, AF.Abs)
                self.act(c_, ph2, AF.Sin, scale=-TWO_PI, bias=halfpi)
                t1, t2, t3, gr, gi = T1[sl_], T2[sl_], T3[sl_], GR[sl_], GI[sl_]
                self.tt("dve", t1, zr, c_, ALU.mult)
                self.tt("dve", t2, zi, s__, ALU.mult)
                self.tt("pool", t1, t1, t2, ALU.add)
                self.tt("dve", t2, zi, c_, ALU.mult)
                self.tt("dve", t3, zr, s__, ALU.mult)
                self.tt("pool", t2, t2, t3, ALU.subtract)
                Rb = self.Rch[:, q:q + 1].bc([128, 512])
                self.gen("dve", "tensor_tensor_scan", [Rb, t1], [gr], out=gr, data0=Rb, data1=t1, initial=0.0,
                         op0=ALU.mult, op1=ALU.add)
                self.gen("dve", "tensor_tensor_scan", [Rb, t2], [gi], out=gi, data0=Rb, data1=t2, initial=0.0,
                         op0=ALU.mult, op1=ALU.add)
                self.tt("dve", t1[:, 0:n1], gr[:, 0:n1], c_[:, 0:n1], ALU.mult)
                self.tt("pool", t3[:, 0:n1], gi[:, 0:n1], s__[:, 0:n1], ALU.mult)
                self.tt("pool", Hre[hs][:, 1:512], t1[:, 0:n1], t3[:, 0:n1], ALU.subtract)
                self.tt("dve", t2[:, 0:n1], gi[:, 0:n1], c_[:, 0:n1], ALU.mult)
                self.tt("pool", t3[:, 0:n1], gr[:, 0:n1], s__[:, 0:n1], ALU.mult)
                self.tt("pool", Him[hs][:, 1:512], t2[:, 0:n1], t3[:, 0:n1], ALU.add)
            for t in range(7, -1, -1):
                ob = self.bank(4 + (obi % 4))
                obi += 1
                for s_ in range(t + 1):
                    self.mm(ob, self.BD[:, kt, t - s_, :], UT[:, kt, s_, :].k(kt, s_), s_ == 0, False)
                for pl in range(4):
                    q = kt * 4 + pl
                    hs = q % NH
                    o_ = ob[32 * pl:32 * pl + 32, :]
                    self.mm(o_, self.CarW[0][:, q, t, :], Hre[hs], False, False, tile_position=(0, 32 * pl))
                    self.mm(o_, self.CarW[1][:, q, t, :], Him[hs], False, True, tile_position=(0, 32 * pl))
                ev = evt[t % 2]
                ut = UT[:, kt, t, :].k(kt, t)
                self.stt("dve", ev, ut, self.Dsk[:, kt:kt + 1], ob, ALU.mult, ALU.add)
                self.act(ut, ev, AF.Gelu)
        self.sc.barrier()
        self.ar.reset(m)

    def load_w(self, dst, src, nkt, split=None):
        v = src.rearrange("(k p) n -> p k n", p=128)
        n = dst.ap.shape[2]
        cw = min(n, 2048)
        if split is None:
            stg = [self.alloc([cw], F32, "wstage") for _ in range(2)]
        else:
            stg = [s_[:, 0:cw] for s_ in split]
        engs = ("pool", "act", "dve")
        i = 0
        for kt in range(nkt):
            for c0 in range(0, n, cw):
                s_ = stg[i % 2]
                self.dma(s_, v[:, kt, c0:c0 + cw])
                self.cp(engs[i % 3], dst[:, kt, c0:c0 + cw].k("w", kt), s_)
                i += 1
    def load_bcast(self, dst, row_ap):
        self.dma(dst, row_ap.partition_broadcast(128))

    def layernorm(self, r, out, gb, bb, scr):
        st, mv, sd = scr["st"], scr["mv"], scr["sd"]
        for c in range(2):
            self.gen("dve", "bn_stats", [r], [st], out=st[:, c, :], in_=r[:, c * 512:(c + 1) * 512])
        self.gen("dve", "bn_aggr", [st], [mv], out=mv, in_=st.re("p a b -> p (a b)"))
        self.act(sd, mv[:, 1:2], AF.Sqrt, scale=1.0, bias=scr["eps"])
        self.gen("dve", "reciprocal", [sd], [sd], out=sd, in_=sd)
        self.ts("dve", out, r, mv[:, 0:1], ALU.subtract, sd, ALU.mult)
        self.tt("pool", out, out, gb, ALU.mult)
        self.tt("pool", out, out, bb, ALU.add)

    def ln_scratch(self):
        A = self.alloc
        eps = A([1], F32, "eps")
        self.memset("pool", eps, LN_EPS)
        return [{"st": A([2, 6], F32, "st"), "mv": A([2], F32, "mv"), "sd": A([1], F32, "sd"), "eps": eps}
                for _ in range(2)]

    def to_feature_major(self, h, hT, banks):
        for half in range(2):
            bk = banks[half]
            for j in range(4):
                kt = half * 4 + j
                self.tr(bk[:, j * 128:(j + 1) * 128], h[:, kt * 128:(kt + 1) * 128], self.identf)
            self.cp("act", hT[:, half * 4:half * 4 + 4, :], bk.re("p (a b) -> p a b", a=4))

    def phase_b1(self):
        A = self.alloc
        dr = self.dram
        UT = self.UT
        m = self.ar.mark()
        self.H1 = self.dscr("H1", [S, D], F32)
        self.H1T = self.dscr("H1T", [128, 8, S], BF16)
        Wg = A([8, 2 * D], BF16, "wglu")
        Wo = A([8, D], BF16, "wout")
        self.load_w(Wg, dr["ssm_w_glu"], 8)
        self.load_w(Wo, dr["ssm_w_out"], 8)
        gb, bb = A([D], F32, "gb"), A([D], F32, "bb")
        self.load_bcast(gb, dr["ln_mix_g"][0])
        self.load_bcast(bb, dr["ln_mix_b"][0])
        lsc = self.ln_scratch()
        zT = [A([8, 512], BF16, "zT") for _ in range(2)]
        sg = [A([512], F32, "sg") for _ in range(2)]
        xt = [A([D], F32, "xt") for _ in range(2)]
        rr = [A([D], F32, "rr") for _ in range(2)]
        h1 = [A([D], F32, "h1") for _ in range(2)]
        h1T = [A([8, 512], BF16, "h1T") for _ in range(2)]
        xperm = self.x.rearrange("(c j t) d -> c t j d", j=64, t=8)
        h1perm = self.H1.rearrange("(c j t) d -> c t j d", j=64, t=8)
        n = 0
        for tb in range(S // 512):
            z = zT[tb % 2]
            hTb = h1T[tb % 2]
            for mt in range(8):
                bv, bg = self.bank(2 * (mt % 2)), self.bank(2 * (mt % 2) + 1)
                for which, bk in ((0, bv), (1, bg)):
                    c0 = which * D + mt * 128
                    for kt in range(8):
                        self.mm(bk, Wg[:, kt, c0:c0 + 128].k("w", kt),
                                UT[:, kt, :, tb * 64:(tb + 1) * 64], kt == 0, kt == 7)
                s_ = sg[mt % 2]
                self.act(s_, bg, AF.Sigmoid)
                self.tt("dve", z[:, mt, :], bv, s_, ALU.mult)
            for sub in range(4):
                tt_ = tb * 4 + sub
                x_, r_, h_ = xt[n % 2], rr[n % 2], h1[n % 2]
                sc_ = lsc[n % 2]
                n += 1
                for tl in range(2):
                    self.dma(x_[tl * 64:(tl + 1) * 64, :], xperm[tb, 2 * sub + tl])
                for nh in range(2):
                    bk = self.bank(4 + nh)
                    for kt in range(8):
                        self.mm(bk, z[:, kt, sub * 128:(sub + 1) * 128], Wo[:, kt, nh * 512:(nh + 1) * 512].k("w", kt),
                                kt == 0, kt == 7)
                    self.stt("dve", r_[:, nh * 512:(nh + 1) * 512], x_[:, nh * 512:(nh + 1) * 512], ALPHA, bk,
                             ALU.mult, ALU.add)
                self.layernorm(r_, h_, gb, bb, sc_)
                for tl in range(2):
                    self.dma(h1perm[tb, 2 * sub + tl], h_[tl * 64:(tl + 1) * 64, :])
                for half in range(2):
                    bk = self.bank(6 + half)
                    for j in range(4):
                        kt = half * 4 + j
                        self.tr(bk[:, j * 128:(j + 1) * 128], h_[:, kt * 128:(kt + 1) * 128], self.identf)
                    o = hTb[:, half * 4:half * 4 + 4, :].re("p a (j t) -> p a t j", t=8)[:, :, 2 * sub:2 * sub + 2, :]
                    self.cp("act", o, bk.re("p (a t j) -> p a t j", a=4, t=2))
            self.dma(self.H1T[:, :, tb * 512:(tb + 1) * 512], hTb)
        self.sc.barrier()
        self.ar.reset(m)

    def phase_ffn(self, layer, Hin, HinT, Hout, HoutT):
        A = self.alloc
        dr = self.dram
        m = self.ar.mark()
        W1 = A([8, DFF], BF16, "w1")
        W2 = A([32, D], BF16, "w2")
        stg = [A([1024], F32, "wstage") for _ in range(4)]
        v1 = dr["w_ff1"][layer].rearrange("(k p) n -> p k n", p=128)
        v2 = dr["w_ff2"][layer].rearrange("(k p) n -> p k n", p=128)
        engs = ("pool", "act", "dve")
        li = 0
        for cb in range(8):
            for kg in range(4):
                s_ = stg[li % 4]
                s2 = s_.re("p (a b) -> p a b", a=2)
                self.dma(s2, v1[:, kg * 2:(kg + 1) * 2, cb * 512:(cb + 1) * 512])
                self.cp(engs[li % 3], W1[:, kg * 2:(kg + 1) * 2, cb * 512:(cb + 1) * 512].k("w1", cb), s2)
                li += 1
            for r2 in range(4):
                s_ = stg[li % 4]
                f0 = cb * 4 + r2
                self.dma(s_, v2[:, f0, :])
                self.cp(engs[li % 3], W2[:, f0, :].k("w2", f0), s_)
                li += 1
        gb, bb = A([D], F32, "gb"), A([D], F32, "bb")
        self.load_bcast(gb, dr["ln_ffn_g"][layer])
        self.load_bcast(bb, dr["ln_ffn_b"][layer])
        lsc = self.ln_scratch()
        hT = [A([8, 256], BF16, "hT") for _ in range(2)]
        hin = [A([D], F32, "hin") for _ in range(4)]
        rr = [A([D], F32, "rr") for _ in range(2)]
        ho = [A([D], F32, "ho") for _ in range(2)]
        hoT = [A([8, 128], BF16, "hoT") for _ in range(2)]
        rl = [A([256], F32, "rl") for _ in range(3)]
        aT = [A([256], BF16, "aT") for _ in range(4)]
        n = 0
        fb = 0
        for blk in range(S // 256):
            t0 = blk * 256
            h_T = hT[blk % 2]
            hs_ = [hin[(2 * blk) % 4], hin[(2 * blk + 1) % 4]]
            self.dma(h_T, HinT[:, :, t0:t0 + 256])
            for sub in range(2):
                self.dma(hs_[sub], Hin[t0 + sub * 128:t0 + (sub + 1) * 128, :])
            acc = [[self.bank(0), self.bank(1)], [self.bank(2), self.bank(3)]]

            def ff2(ft, a_):
                for sub in range(2):
                    for nh in range(2):
                        self.mm(acc[sub][nh], a_[:, sub * 128:(sub + 1) * 128],
                                W2[:, ft, nh * 512:(nh + 1) * 512].k("w2", ft), ft == 0, ft == 31)

            prev = None
            for ft in range(32):
                nfb = 3 if HoutT is None else 2
                bk = self.bank(4 + fb % nfb)
                r_ = rl[fb % 3]
                a_ = aT[fb % 4]
                fb += 1
                for kt in range(8):
                    self.mm(bk[:, 0:256], W1[:, kt, ft * 128:(ft + 1) * 128].k("w1", ft // 4), h_T[:, kt, :], kt == 0, kt == 7)
                self.act(r_, bk[:, 0:256], AF.Relu)
                self.tt("pool", a_, r_, r_, ALU.mult)
                if prev is not None:
                    ff2(*prev)
                prev = (ft, a_)
            ff2(*prev)
            for sub in range(2):
                tt_ = blk * 2 + sub
                r2, h_o, h_oT = rr[n % 2], ho[n % 2], hoT[n % 2]
                sc_ = lsc[n % 2]
                n += 1
                for nh in range(2):
                    self.stt("dve", r2[:, nh * 512:(nh + 1) * 512], hs_[sub][:, nh * 512:(nh + 1) * 512], ALPHA,
                             acc[sub][nh], ALU.mult, ALU.add)
                self.layernorm(r2, h_o, gb, bb, sc_)
                self.dma(Hout[tt_ * 128:(tt_ + 1) * 128, :], h_o)
                if HoutT is not None:
                    self.to_feature_major(h_o, h_oT, (self.bank(6), self.bank(7)))
                    self.dma(HoutT[:, :, tt_ * 128:(tt_ + 1) * 128], h_oT)
        self.sc.barrier()
        self.ar.reset(m)

    def rmsnorm_tm(self, out_bf, bank_ap, n, g_b, st, mv, rs, eps):
        self.gen("dve", "bn_stats", [bank_ap], [st], out=st, in_=bank_ap)
        self.gen("dve", "bn_aggr", [st], [mv], out=mv, in_=st)
        self.stt("dve", rs, mv[:, 0:1], mv[:, 0:1], mv[:, 1:2], ALU.mult, ALU.add)
        self.act(rs, rs, AF.Sqrt, scale=1.0, bias=eps)
        self.gen("dve", "reciprocal", [rs], [rs], out=rs, in_=rs)
        self.stt("dve", out_bf, bank_ap, rs, g_b, ALU.mult, ALU.mult)

    def phase_c0(self):
        A = self.alloc
        dr = self.dram
        self.Wkvb = A([2, 2048], BF16, "wkvb")
        self.Wqb = A([3, 1536], BF16, "wqb")
        self.load_w(self.Wkvb, dr["kv_w_b"], 2)
        self.load_w(self.Wqb, dr["q_w_b"], 3)
        import os
        if os.environ.get("C0_PAD"):
            self.alloc([int(os.environ["C0_PAD"])], F32, "pad")
        self.cosT = A([NT, 32], F32, "cosT")
        self.sinT = A([NT, 32], F32, "sinT")
        self.ckvT = A([2, S], BF16, "ckvT")
        self.kropeT = A([S], BF16, "kropeT")
        self.cqT = A([3, S], BF16, "cqT")
        self.c_top = self.ar.mark()
        Wkva = A([8, 320], BF16, "wkva")
        Wqa = A([8, 384], BF16, "wqa")
        self.load_w(Wkva, dr["kv_w_a"], 8)
        self.load_w(Wqa, dr["q_w_a"], 8)
        gkv, gq = A([256], F32, "gkv"), A([384], F32, "gq")
        self.load_bcast(gkv, dr["kv_norm_g"][0])
        self.load_bcast(gq, dr["q_norm_g"][0])
        eps = A([1], F32, "rmseps")
        self.memset("pool", eps, RMS_EPS)
        import os
        SK = os.environ.get("C0_SKIP", "").split(",")
        C0NT = int(os.environ.get("C0_NT", str(NT)))
        m2 = self.ar.mark()
        pin = A([128], I32, "pin")
        pinf = A([128], F32, "pinf")
        posf = A([NT], F32, "posf")
        invf = A([32], F32, "invf")
        self.memset("pool", pinf, 0.0)
        self.dma(pin[0:NT, :], self.pos)
        self.cp("dve", pinf[0:NT, :], pin[0:NT, :])
        bk = self.bank(0)
        if "ptr" not in SK:
            self.tr(bk[:, 0:NT], pinf[0:NT, :], self.identf[0:NT, 0:NT])
            self.cp("dve", posf, bk[:, 0:NT])
        iv = (np.float32(10000.0) ** (-(np.arange(32, dtype=np.float32) / np.float32(32.0)))).astype(np.float32)
        for i_ in range(32):
            self.memset("pool", invf[:, i_:i_ + 1], float(iv[i_]))
        s3 = [128, NT, 32]
        if "tab" in SK:
            self.sc.barrier()
            self.ar.reset(m2)
            return
        ang = A([NT, 32], F32, "ang")
        angc = A([NT, 32], F32, "angc")
        ai = A([NT, 32], I32, "ai")
        self.tt("dve", ang, posf.bc(s3), invf.ubc(1, s3), ALU.mult)
        self.ts("dve", ang, ang, 1.0 / TWO_PI, ALU.mult)
        self.ts("dve", angc, ang, 0.25, ALU.add)
        self.cp("dve", ai, ang)
        self.tt("dve", ang, ang, ai, ALU.subtract)
        self.act(self.sinT, ang, AF.Sin, scale=TWO_PI)
        self.cp("dve", ai, angc)
        self.tt("dve", angc, angc, ai, ALU.subtract)
        self.act(self.cosT, angc, AF.Sin, scale=TWO_PI)
        self.sc.barrier()
        if not _os.environ.get("NO_M2_RESET"):
            self.ar.reset(m2)
        h2T = [A([8, 128], BF16, "h2T") for _ in range(2)]
        stk = [A([6], F32, "stk") for _ in range(2)]
        stq = [A([6], F32, "stq") for _ in range(2)]
        mvk = [A([2], F32, "mvk") for _ in range(2)]
        mvq = [A([2], F32, "mvq") for _ in range(2)]
        rk = [A([1], F32, "rk") for _ in range(2)]
        rq = [A([1], F32, "rq") for _ in range(2)]
        ckv_tm = [A([256], F32, "ckv_tm") for _ in range(2)]
        kr_tm = [A([128], F32, "kr_tm") for _ in range(2)]
        cq_tm = [A([384], F32, "cq_tm") for _ in range(2)]
        ra = [[A([32], F32, "ra") for _ in range(4)] for _ in range(2)]
        for tt_ in range(C0NT):
            sl_ = tt_ % 2
            tok = slice(tt_ * 128, (tt_ + 1) * 128)
            hT = h2T[sl_]
            self.dma(hT, self.H2T[:, :, tok])
            bA, bB, bC = self.bank(sl_), self.bank(2 + sl_), self.bank(4 + sl_)
            for kt in range(8):
                self.mm(bA[:, 0:320], hT[:, kt, :], Wkva[:, kt, :].k("w", kt), kt == 0, kt == 7)
            for kt in range(8):
                self.mm(bB[:, 0:384], hT[:, kt, :], Wqa[:, kt, :].k("w", kt), kt == 0, kt == 7)
            if "rms" not in SK:
                self.rmsnorm_tm(ckv_tm[sl_], bA[:, 0:256], 256, gkv, stk[sl_], mvk[sl_], rk[sl_], eps)
                self.rmsnorm_tm(cq_tm[sl_], bB[:, 0:384], 384, gq, stq[sl_], mvq[sl_], rq[sl_], eps)
            x1, x2 = bA[:, 256:288], bA[:, 288:320]
            c_, s_ = self.cosT[:, tt_, :], self.sinT[:, tt_, :]
            a1, a2, a3, a4 = ra[sl_]
            if "rope" not in SK:
                self.tt("dve", a1, x1, c_, ALU.mult)
                self.tt("dve", a2, x2, s_, ALU.mult)
                self.tt("dve", a3, x1, s_, ALU.mult)
                self.tt("dve", a4, x2, c_, ALU.mult)
            kr3 = kr_tm[sl_].re("p (r c) -> p r c", r=2)
            s2 = [128, 2, 32]
            for r_ in range(2):
                self.tt(KR_ENG, kr3[:, r_, 0:32], a1, a2, ALU.subtract)
                self.tt(KR_ENG, kr3[:, r_, 32:64], a3, a4, ALU.add)
            bC2 = self.bank(6 + sl_)
            srcs = [ckv_tm[sl_][:, 0:128], ckv_tm[sl_][:, 128:256], kr_tm[sl_], cq_tm[sl_][:, 0:128],
                    cq_tm[sl_][:, 128:256], cq_tm[sl_][:, 256:384]]
            for j, s__ in enumerate(srcs):
                dst = bC[:, j * 128:(j + 1) * 128] if j < 4 else bC2[:, (j - 4) * 128:(j - 3) * 128]
                self.tr(dst, s__, self.identf)
            if "ev1" not in SK:
                self.cp("act", self.ckvT[:, :, tok], bC[:, 0:256].re("p (a b) -> p a b", a=2))
            if "ev2" not in SK:
                self.cp("act", self.kropeT[:, tok], bC[:, 256:384])
            if "ev3" not in SK:
                self.cp("dve", self.cqT[:, 0, tok], bC[:, 384:512])
            if "ev4" not in SK:
                self.cp("dve", self.cqT[:, 1:3, tok], bC2[:, 0:256].re("p (a b) -> p a b", a=2))
        self.sc.barrier()
        self.ar.reset(self.c_top)

    def phase_c(self):
        A = self.alloc
        self.OT = self.dscr("OT", [128, 8, S], BF16)
        onesf = A([128], F32, "onesf")
        self.memset("pool", onesf, 1.0)
        accD = [A([512], F32, "accD") for _ in range(2)]
        accP = [A([512], F32, "accP") for _ in range(2)]
        knT = A([4, S], BF16, "knT")
        vtm = A([NT, 4, 128], BF16, "vtm")
        QN = [A([4, 512], BF16, "QN") for _ in range(2)]
        QR = [A([2, 512], BF16, "QR") for _ in range(2)]
        qr_tm = [A([256], F32, "qr_tm") for _ in range(2)]
        ra = [[A([4, 32], F32, "qra") for _ in range(4)] for _ in range(2)]
        PT = [A([512], BF16, "PT") for _ in range(4)]
        rd = [A([512], F32, "rd") for _ in range(1)]
        oTb = [A([4, 512], BF16, "oTb") for _ in range(2)]
        Wkvb4 = self.Wkvb.re("p k (h two d) -> p k h two d", two=2, d=128)
        Wqb3 = self.Wqb.re("p k (h e) -> p k h e", e=192)
        ev = 0
        for hh in range(2):
            for tb in range(S // 512):
                for hl in range(4):
                    h = 4 * hh + hl
                    bk = self.bank(ev % 3)
                    for j in range(2):
                        self.mm(bk, self.Wkvb[:, j, h * 256:h * 256 + 128].k("w", j),
                                self.ckvT[:, j, tb * 512:(tb + 1) * 512], j == 0, j == 1)
                    self.cp("act" if ev % 2 == 0 else "dve", knT[:, hl, tb * 512:(tb + 1) * 512], bk)
                    ev += 1
            for tt_ in range(NT):
                bk = self.bank(ev % 3)
                for j in range(2):
                    self.mm(bk, self.ckvT[:, j, tt_ * 128:(tt_ + 1) * 128],
                            Wkvb4[:, j, 4 * hh:4 * hh + 4, 1, :].k("w", j), j == 0, j == 1)
                self.cp("act" if ev % 2 == 0 else "dve", vtm[:, tt_, :, :], bk.re("p (a b) -> p a b", a=4))
                ev += 1
            sbi = 0
            pti = 0
            for qb in range(S // 512):
                qn, qr = QN[qb % 2], QR[qb % 2]
                qs = slice(qb * 512, (qb + 1) * 512)
                for hl in range(4):
                    h = 4 * hh + hl
                    bk = self.bank(7)
                    for j in range(3):
                        self.mm(bk, self.Wqb[:, j, h * 192:h * 192 + 128].k("w", j), self.cqT[:, j, qs], j == 0, j == 2)
                    self.cp("dve", qn[:, hl, :], bk)
                for sub in range(4):
                    tt_ = qb * 4 + sub
                    tok = slice(tt_ * 128, (tt_ + 1) * 128)
                    bk = self.bank(7)
                    for j in range(3):
                        self.mm(bk[:, 0:256], self.cqT[:, j, tok], Wqb3[:, j, 4 * hh:4 * hh + 4, 128:192].k("w", j),
                                j == 0, j == 2)
                    b4 = bk[:, 0:256].re("p (h r c) -> p h r c", h=4, r=2)
                    x1, x2 = b4[:, :, 0, :], b4[:, :, 1, :]
                    s4 = [128, 4, 32]
                    c_, s_ = self.cosT[:, tt_, :].ubc(1, s4), self.sinT[:, tt_, :].ubc(1, s4)
                    a1, a2, a3, a4 = ra[sub % 2]
                    self.tt("dve", a1, x1, c_, ALU.mult)
                    self.tt("dve", a2, x2, s_, ALU.mult)
                    self.tt("dve", a3, x1, s_, ALU.mult)
                    self.tt("dve", a4, x2, c_, ALU.mult)
                    q3 = qr_tm[sub % 2].re("p (h e) -> p h e", h=4)
                    self.tt("pool", q3[:, :, 0:32], a1, a2, ALU.subtract)
                    self.tt("pool", q3[:, :, 32:64], a3, a4, ALU.add)
                    for pr in range(2):
                        self.tr(bk[:, 256 + pr * 128:256 + (pr + 1) * 128], qr_tm[sub % 2][:, pr * 128:(pr + 1) * 128],
                                self.identf)
                    self.cp("act", qr[:, :, sub * 128:(sub + 1) * 128],
                            bk[:, 256:512].re("p (a b) -> p a b", a=2))
                ot = oTb[qb % 2]
                for hl in range(4):
                    pr, e = hl // 2, hl % 2
                    bO, bD = self.bank(3 + 2 * (hl % 2)), self.bank(4 + 2 * (hl % 2))
                    nk = 4 * qb + 4

                    def c0_of(kt):
                        i = kt - 4 * qb
                        return 128 * i if i > 0 else 0

                    def scores(kt):
                        sb = self.bank(kt % 3)
                        c0 = c0_of(kt)
                        ks = slice(kt * 128, (kt + 1) * 128)
                        self.mm(sb[:, c0:512], knT[:, hl, ks], qn[:, hl, c0:512], True, False)
                        self.mm(sb[:, c0:512], self.kropeT[64 * e:64 * e + 64, ks], qr[64 * e:64 * e + 64, pr, c0:512],
                                False, True, tile_position=(64 * e, 0))

                    aD, aP = accD[hl % 2], accP[hl % 2]
                    self.memset("pool", aD, 0.0)
                    self.memset("pool", aP, 0.0)
                    scores(0)
                    for kt in range(nk):
                        if kt + 1 < nk:
                            scores(kt + 1)
                        sb = self.bank(kt % 3)
                        c0 = c0_of(kt)
                        pt = PT[pti % 4]
                        pti += 1
                        self.act(pt[:, c0:512], sb[:, c0:512], AF.Exp, scale=SM_SCALE)
                        if kt >= 4 * qb:
                            dg = pt[:, c0:c0 + 128]
                            self.gen("pool", "affine_select", [pt], [pt], out=dg, in_=dg, pattern=[[1, 128]],
                                     compare_op=ALU.is_ge, fill=0.0, base=0, channel_multiplier=-1)
                        self.mm(bO[:, c0:512], vtm[:, kt, hl, :], pt[:, c0:512], kt == 0, kt == nk - 1)
                        if kt % 3 == 2:
                            self.tt("pool", aP[:, c0:512], aP[:, c0:512], pt[:, c0:512], ALU.add)
                        else:
                            self.tt("dve", aD[:, c0:512], aD[:, c0:512], pt[:, c0:512], ALU.add)
                    self.mm(bD, onesf, aD, True, False)
                    self.mm(bD, onesf, aP, False, True)
                    r_ = rd[0]
                    self.gen("dve", "reciprocal", [bD], [r_], out=r_, in_=bD)
                    self.tt("dve", ot[:, hl, :], bO, r_, ALU.mult)
                self.dma(self.OT[:, 4 * hh:4 * hh + 4, qs], ot)
        self.sc.barrier()
        if self.stop_after in ("c", "c2"):
            self.dbg_dump("dbg_ckvT", self.ckvT, [128, 2, S], BF16)
            self.dbg_dump("dbg_kropeT", self.kropeT, [128, S], BF16)
            self.dbg_dump("dbg_cqT", self.cqT, [128, 3, S], BF16)
            self.dbg_dump("dbg_knT", knT, [128, 4, S], BF16)
            self.dbg_dump("dbg_vtm", vtm, [128, NT, 4, 128], BF16)
            self.dbg_dump("dbg_cosT", self.cosT, [128, NT, 32], F32)
            self.sc.barrier()
        self.ar.off = 0
        self.consts()
        self.sc.barrier()

    def phase_c2(self):
        A = self.alloc
        dr = self.dram
        m = self.ar.mark()
        self.H3 = self.H1
        self.H3T = self.H1T
        Wo = A([8, D], BF16, "wo")
        self.load_w(Wo, dr["attn_w_o"], 8)
        gb, bb = A([D], F32, "gb"), A([D], F32, "bb")
        self.load_bcast(gb, dr["ln_mix_g"][1])
        self.load_bcast(bb, dr["ln_mix_b"][1])
        lsc = self.ln_scratch()
        oT = [A([8, 128], BF16, "oT") for _ in range(3)]
        h2 = [A([D], F32, "h2") for _ in range(3)]
        h3 = [A([D], F32, "h3") for _ in range(2)]
        h3T = [A([8, 128], BF16, "h3T") for _ in range(2)]
        for tt_ in range(NT):
            tok = slice(tt_ * 128, (tt_ + 1) * 128)
            o_, r_, h_, hT_ = oT[tt_ % 3], h2[tt_ % 3], h3[tt_ % 2], h3T[tt_ % 2]
            self.dma(o_, self.OT[:, :, tok])
            self.dma(r_, self.H2[tok, :])
            for nh in range(2):
                bk = self.bank(2 * (tt_ % 2) + nh)
                for kt in range(8):
                    self.mm(bk, o_[:, kt, :], Wo[:, kt, nh * 512:(nh + 1) * 512].k("w", kt), kt == 0, kt == 7)
                self.stt("dve", r_[:, nh * 512:(nh + 1) * 512], r_[:, nh * 512:(nh + 1) * 512], ALPHA, bk,
                         ALU.mult, ALU.add)
            self.layernorm(r_, h_, gb, bb, lsc[tt_ % 2])
            self.dma(self.H3[tok, :], h_)
            self.to_feature_major(h_, hT_, (self.bank(4 + 2 * (tt_ % 2)), self.bank(5 + 2 * (tt_ % 2))))
            self.dma(self.H3T[:, :, tok], hT_)
        self.sc.barrier()
        self.ar.reset(m)

    def finish(self):
        nc = self.nc
        with self.es:
            with nc.Block() as block:
                self.sc.emit(nc, block, self.es)
        return nc

    def dbg_dump(self, name, src, shape, dtype):
        t = self.dout(name, shape, dtype)
        self.dma(t, src)
        return t

    def dbg_copyT(self, name, src_dram):
        t = self.dout(name, [128, 8, S], BF16)
        buf = [self.alloc([8, 512], BF16, "dbgbufT") for _ in range(2)]
        for i in range(S // 512):
            self.dma(buf[i % 2], src_dram[:, :, i * 512:(i + 1) * 512])
            self.dma(t[:, :, i * 512:(i + 1) * 512], buf[i % 2])
        return t

    def dbg_copy(self, name, src_dram, shape, dtype):
        t = self.dout(name, shape, dtype)
        buf = [self.alloc([D], F32, "dbgbuf") for _ in range(2)]
        for i in range(shape[0] // 128):
            self.dma(buf[i % 2], src_dram[i * 128:(i + 1) * 128, :])
            self.dma(t[i * 128:(i + 1) * 128, :], buf[i % 2])
        return t


def build(stop_after=None, debug=()):
    b = Prog(stop_after, debug)
    b.declare_io()
    b.consts()
    b.phase_a1()
    if stop_after == "a1":
        b.dbg_dump("dbg_zupw_re", b.ZupW[0], [128, 8, 8, 128], BF16)
        b.dbg_dump("dbg_zupw_im", b.ZupW[1], [128, 8, 8, 128], BF16)
        b.dbg_dump("dbg_carw_re", b.CarW[0], [128, 32, 8, 32], BF16)
        b.dbg_dump("dbg_carw_nim", b.CarW[1], [128, 32, 8, 32], BF16)
        b.dbg_dump("dbg_bd", b.BD, [128, 8, 8, 128], BF16)
        b.dbg_dump("dbg_rch", b.Rch, [128, 32], F32)
        b.dbg_dump("dbg_f8", b.f8, [128, 32], F32)
        b.sc.barrier()
        return b.finish()
    b.phase_a0()
    if stop_after == "a0":
        b.dbg_dump("dbg_UT", b.UT, [128, 8, 8, 512], BF16)
        b.sc.barrier()
        return b.finish()
    b.phase_a2()
    if stop_after == "a2":
        b.dbg_dump("dbg_UT", b.UT, [128, 8, 8, 512], BF16)
        b.sc.barrier()
        return b.finish()
    b.ar.reset(b.ut_top)
    b.phase_b1()
    b.ar.off = 0
    b.consts()
    b.sc.barrier()
    if stop_after == "b1":
        b.dbg_copy("dbg_H1", b.H1, [S, D], F32)
        b.sc.barrier()
        return b.finish()
    b.H2 = b.dscr("H2", [S, D], F32)
    b.H2T = b.dscr("H2T", [128, 8, S], BF16)
    b.phase_ffn(0, b.H1, b.H1T, b.H2, b.H2T)
    if stop_after == "b2":
        b.dbg_copy("dbg_H2", b.H2, [S, D], F32)
        b.sc.barrier()
        return b.finish()
    b.phase_c0()
    if stop_after == "c0":
        b.dbg_dump("dbg_ckvT", b.ckvT, [128, 2, S], BF16)
        b.dbg_dump("dbg_kropeT", b.kropeT, [128, S], BF16)
        b.dbg_dump("dbg_cqT", b.cqT, [128, 3, S], BF16)
        b.dbg_dump("dbg_cosT", b.cosT, [128, NT, 32], F32)
        b.dbg_dump("dbg_sinT", b.sinT, [128, NT, 32], F32)
        b.sc.barrier()
        return b.finish()
    b.phase_c()
    if stop_after == "c":
        b.dbg_copyT("dbg_OT", b.OT)
        b.sc.barrier()
        return b.finish()
    b.phase_c2()
    if stop_after == "c2":
        b.dbg_copy("dbg_H3", b.H3, [S, D], F32)
        b.sc.barrier()
        return b.finish()
    b.phase_ffn(1, b.H3, b.H3T, b.out, None)
    return b.finish()


WEIGHT_NAMES = ["ln_mix_g", "ln_mix_b", "ln_ffn_g", "ln_ffn_b", "w_ff1", "w_ff2", "ssm_lam_re", "ssm_lam_im",
                "ssm_log_dt", "ssm_b_re", "ssm_b_im", "ssm_c_re", "ssm_c_im", "ssm_d", "ssm_w_glu", "ssm_w_out",
                "kv_w_a", "kv_norm_g", "kv_w_b", "q_w_a", "q_norm_g", "q_w_b", "attn_w_o"]


def make_in_maps(inputs, n_cores=8):
    f = lambda a: np.ascontiguousarray(np.asarray(a))
    shared = {
        "ln_mix_g": f(inputs["ln_mix_g"]), "ln_mix_b": f(inputs["ln_mix_b"]),
        "ln_ffn_g": f(inputs["ln_ffn_g"]), "ln_ffn_b": f(inputs["ln_ffn_b"]),
        "w_ff1": f(inputs["w_ff1"]), "w_ff2": f(inputs["w_ff2"]),
        "ssm_lam_re": f(inputs["ssm_lam_re"])[0], "ssm_lam_im": f(inputs["ssm_lam_im"])[0],
        "ssm_log_dt": f(inputs["ssm_log_dt"]).reshape(1, 64),
        "ssm_b_re": f(inputs["ssm_b_re"])[0], "ssm_b_im": f(inputs["ssm_b_im"])[0],
        "ssm_c_re": f(inputs["ssm_c_re"])[0], "ssm_c_im": f(inputs["ssm_c_im"])[0],
        "ssm_d": f(inputs["ssm_d"])[0], "ssm_w_glu": f(inputs["ssm_w_glu"])[0],
        "ssm_w_out": f(inputs["ssm_w_out"])[0],
        "kv_w_a": f(inputs["kv_w_a"]), "kv_norm_g": f(inputs["kv_norm_g"]).reshape(1, 256),
        "kv_w_b": f(inputs["kv_w_b"]),
        "q_w_a": f(inputs["q_w_a"])[0], "q_norm_g": f(inputs["q_norm_g"]).reshape(1, 384),
        "q_w_b": f(inputs["q_w_b"])[0], "attn_w_o": f(inputs["attn_w_o"])[0],
    }
    x = f(inputs["x"])
    pos = f(inputs["positions"]).astype(np.int32)
    maps = []
    for c in range(n_cores):
        m = dict(shared)
        m["x"] = x[c]
        m["positions"] = pos[c].reshape(NT, 128)
        maps.append(m)
    return maps


def kernel(**inputs):
    nc = build()
    maps = make_in_maps(inputs)
    res = run_bass_kernel_spmd(nc, maps, core_ids=list(range(8)))
    return np.stack([np.asarray(r["out"]) for r in res.results], axis=0).astype(np.float32)
```

```python
import math
import os as _os
from contextlib import ExitStack

import numpy as np
import concourse.bass as bass
import concourse.mybir as mybir
from concourse.bass_utils import run_bass_kernel_spmd

F32 = mybir.dt.float32
BF16 = mybir.dt.bfloat16
I32 = mybir.dt.int32
AF = mybir.ActivationFunctionType
ALU = mybir.AluOpType
AX = mybir.AxisListType

S = 4096
D = 1024
NT = S // 128
DFF = 4096
PAD = 8
ALPHA = 4.0 ** 0.25
LN_EPS = 1e-5
RMS_EPS = 1e-6
SM_SCALE = 192.0 ** -0.5
TWO_PI = 2.0 * math.pi
KR_ENG = _os.environ.get("KR_ENG", "pool")
ATTACH_WAIT = _os.environ.get("ATTACH_WAIT", "1") == "1"


def sl(start, n, step=1):
    return slice(start, start + (n - 1) * step + 1, step)


import heapq


class _Op:
    __slots__ = ("eng", "fn", "deps", "odeps", "needs_inc", "is_dma", "dsem", "dval", "sem_idx", "count", "idx",
                 "cost", "nbytes", "seg", "start", "fin", "nsucc", "succ", "est", "bar_waits")

    def __init__(self, eng, fn, is_dma=False):
        self.eng = eng
        self.fn = fn
        self.deps = []
        self.odeps = []
        self.needs_inc = False
        self.is_dma = is_dma
        self.dsem = None
        self.dval = 0
        self.sem_idx = 0
        self.count = 0
        self.cost = 0.1
        self.nbytes = 0
        self.bar_waits = None


class Sched:
    ENGS = ("sp", "act", "dve", "pool", "pe")
    LIMIT = 30000
    XLAT = 1.6
    DMA_LAT = 2.0
    DMA_BW = 160e3

    def __init__(self, n_dma_sems=int(_os.environ.get("NDMA", "24")), reorder=True):
        self.segs = [[]]
        self.lw = {}
        self.rd = {}
        self.n_dma = n_dma_sems
        nsw = 8
        self.dma_pool = {"sp": list(range(0, n_dma_sems - nsw)), "act": list(range(0, n_dma_sems - nsw)),
                         "pool": list(range(n_dma_sems - nsw, n_dma_sems))}
        self.nops = 0
        self.reorder = reorder

    def _deps(self, o, reads, writes):
        extra = [k for k in reads if isinstance(k, tuple) and k and k[0] == "ps" and k not in writes]
        if extra:
            writes = list(writes) + extra
        deps = {}
        for k in reads:
            w = self.lw.get(k)
            if w is not None:
                deps[id(w)] = w
        for k in writes:
            w = self.lw.get(k)
            if w is not None:
                deps[id(w)] = w
            for r in self.rd.get(k, ()):
                deps[id(r)] = r
        for d in deps.values():
            if d is o:
                continue
            if (not d.is_dma) and d.eng == o.eng and (not o.is_dma) and o.eng == "pe":
                o.odeps.append(d)
                continue
            if not d.is_dma:
                d.needs_inc = True
            o.deps.append(d)
        for k in reads:
            self.rd.setdefault(k, []).append(o)
        for k in writes:
            self.lw[k] = o
            self.rd[k] = []

    def op(self, eng, fn, reads=(), writes=(), cost=0.1):
        o = _Op(eng, fn)
        o.cost = cost
        o.idx = self.nops
        self.nops += 1
        self._deps(o, reads, writes)
        self.segs[-1].append(o)
        return o

    def dma(self, eng, fn, reads=(), writes=(), nbytes=0):
        o = _Op(eng, fn, is_dma=True)
        o.nbytes = nbytes
        o.cost = 0.06 if eng != "pool" else 1.0
        o.idx = self.nops
        self.nops += 1
        self._deps(o, reads, writes)
        self.segs[-1].append(o)
        return o

    def barrier(self):
        if self.segs[-1]:
            self.segs.append([])
        self.lw = {}
        self.rd = {}

    def _schedule_segment(self, seg, t0):
        order = {e: [] for e in self.ENGS}
        if not seg:
            return order, t0
        inseg = set(id(o) for o in seg)
        for o in seg:
            o.succ = []
            o.nsucc = 0
            o.est = t0
        for o in seg:
            for d in o.deps + o.odeps:
                if id(d) in inseg:
                    d.succ.append(o)
                    o.nsucc += 1
        if not self.reorder:
            t = {e: t0 for e in self.ENGS}
            tend = t0
            for o in seg:
                order[o.eng].append(o)
            return order, tend
        bl = {}
        for o in reversed(seg):
            c = o.cost + (self.DMA_LAT + o.nbytes / self.DMA_BW if o.is_dma else 0.0)
            m_ = 0.0
            for s_ in o.succ:
                v = bl[id(s_)] + (0.0 if (s_.eng == o.eng and not o.is_dma) else self.XLAT)
                if v > m_:
                    m_ = v
            bl[id(o)] = c + m_
        PRI = "bl"
        for o in seg:
            o.idx = (-bl[id(o)], o.idx) if PRI == "bl" else o.idx
        hest = {e: [] for e in self.ENGS}
        hidx = {e: [] for e in self.ENGS}
        free = {e: t0 for e in self.ENGS}
        for o in seg:
            if o.nsucc == 0:
                heapq.heappush(hest[o.eng], (o.est, o.idx, o))
        dma_free = t0
        remaining = len(seg)
        tend = t0
        while remaining:
            best = None
            for e in self.ENGS:
                he, hi = hest[e], hidx[e]
                while he and he[0][0] <= free[e]:
                    _, ix, oo = heapq.heappop(he)
                    heapq.heappush(hi, (ix, oo))
                if hi:
                    st, ix = free[e], hi[0][0]
                elif he:
                    st, ix = he[0][0], he[0][1]
                else:
                    continue
                if best is None or (st, ix) < best[:2]:
                    best = (st, ix, e)
            st, ix, e = best
            if hidx[e]:
                _, o = heapq.heappop(hidx[e])
            else:
                _, _, o = heapq.heappop(hest[e])
            o.start = st
            if o.is_dma:
                issue_done = st + o.cost
                ts = max(issue_done, dma_free)
                done = ts + o.nbytes / self.DMA_BW
                dma_free = done
                o.fin = done + self.DMA_LAT
                free[e] = issue_done
            else:
                o.fin = st + o.cost
                free[e] = o.fin
            tend = max(tend, o.fin)
            order[e].append(o)
            remaining -= 1
            for s_ in o.succ:
                lat = 0.0 if (s_.eng == o.eng and not o.is_dma) else self.XLAT
                if o.fin + lat > s_.est:
                    s_.est = o.fin + lat
                s_.nsucc -= 1
                if s_.nsucc == 0:
                    heapq.heappush(hest[s_.eng], (s_.est, s_.idx, s_))
        return order, tend

    def finalize(self):
        if self.segs[-1]:
            self.segs.append([])
        t0 = 0.0
        self.eng_ops = {e: [] for e in self.ENGS}
        seg_orders = []
        for seg in self.segs:
            order, t0 = self._schedule_segment(seg, t0)
            seg_orders.append(order)
        self.est_total_us = t0
        cnt = {e: 0 for e in self.ENGS}
        si = {e: 0 for e in self.ENGS}
        dma_cnt = [0] * self.n_dma
        dma_rr = {"sp": 0, "act": 0, "pool": 0}
        for sidx, order in enumerate(seg_orders):
            if sidx > 0:
                prev = seg_orders[sidx - 1]
                lasts = []
                for e in self.ENGS:
                    for o in reversed(self.eng_ops[e]):
                        if o.fn is not None and not o.is_dma:
                            lasts.append(o)
                            break
                bw = {}
                for o in lasts:
                    if not o.needs_inc:
                        o.needs_inc = True
                        if cnt[o.eng] >= self.LIMIT:
                            si[o.eng] += 1
                            cnt[o.eng] = 0
                        cnt[o.eng] += 1
                        o.count = cnt[o.eng]
                        o.sem_idx = si[o.eng]
                    bw[(o.eng, o.sem_idx)] = o.count
                for s_ in range(self.n_dma):
                    if dma_cnt[s_]:
                        bw[("d", s_)] = dma_cnt[s_]
                for e in self.ENGS:
                    b = _Op(e, None)
                    b.bar_waits = dict(bw)
                    self.eng_ops[e].append(b)
            for e in self.ENGS:
                for o in order[e]:
                    if o.is_dma:
                        pool = self.dma_pool[e]
                        rrk = "sp" if e == "act" else e
                        s_ = pool[dma_rr[rrk] % len(pool)]
                        dma_rr[rrk] += 1
                        o.bar_waits = {("d", s_): dma_cnt[s_]} if dma_cnt[s_] else None
                        dma_cnt[s_] += 16
                        o.dsem = s_
                        o.dval = dma_cnt[s_]
                    elif o.needs_inc:
                        if cnt[e] >= self.LIMIT:
                            si[e] += 1
                            cnt[e] = 0
                        cnt[e] += 1
                        o.count = cnt[e]
                        o.sem_idx = si[e]
                    self.eng_ops[e].append(o)
        self.nsem = {e: si[e] + 1 for e in self.ENGS}

    def emit(self, nc, block, es):
        self.finalize()
        esem = {e: [es.enter_context(nc.semaphore(f"s_{e}_{i}")) for i in range(self.nsem[e])] for e in self.ENGS}
        dsem = [es.enter_context(nc.semaphore(f"s_dma_{i}")) for i in range(self.n_dma)]

        def run(e, eng):
            waited = {}
            for o in self.eng_ops[e]:
                need = {}
                if o.bar_waits:
                    need.update(o.bar_waits)
                for d in o.deps:
                    if d.is_dma:
                        key = ("d", d.dsem)
                        val = d.dval
                    else:
                        key = (d.eng, d.sem_idx)
                        val = d.count
                    if val > need.get(key, 0):
                        need[key] = val
                todo = []
                for key, val in need.items():
                    if waited.get(key, 0) >= val:
                        continue
                    waited[key] = val
                    sem = dsem[key[1]] if key[0] == "d" else esem[key[0]][key[1]]
                    todo.append((1 if key[0] == "d" else 0, sem, val))
                todo.sort(key=lambda t_: t_[0])
                attach = None
                if todo and o.fn is not None and ATTACH_WAIT:
                    attach = todo.pop()
                for _, sem, val in todo:
                    eng.wait_ge(sem, val)
                if o.fn is None:
                    continue
                ins = o.fn(eng)
                if attach is not None:
                    ins._wait_ge(attach[1], attach[2])
                if o.is_dma:
                    ins.then_inc(dsem[o.dsem], 16)
                elif o.needs_inc:
                    ins.then_inc(esem[e][o.sem_idx], 1)

        @block.sync
        def _(eng):
            run("sp", eng)

        @block.scalar
        def _(eng):
            run("act", eng)

        @block.vector
        def _(eng):
            run("dve", eng)

        @block.gpsimd
        def _(eng):
            run("pool", eng)

        @block.tensor
        def _(eng):
            run("pe", eng)


class Arena:
    def __init__(self, nc, nbytes):
        self.nc = nc
        self.words = nbytes // 4
        self.t = nc.alloc_sbuf_tensor("arena", [128, self.words], F32)
        self.off = 0

    def alloc(self, shape, dtype=F32):
        n = 1
        for s_ in shape:
            n *= s_
        bpe = 4 if dtype in (F32, I32) else 2
        nw = (n * bpe + 3) // 4
        nw = (nw + 7) // 8 * 8
        assert self.off + nw <= self.words, f"SBUF arena overflow: need {self.off + nw} words of {self.words}"
        v = self.t[:, self.off:self.off + nw]
        self.off += nw
        if dtype != F32:
            v = v.bitcast(dtype)
        v = v[:, 0:n]
        if len(shape) == 2:
            v = v.rearrange("p (a b) -> p a b", a=shape[0])
        elif len(shape) == 3:
            v = v.rearrange("p (a b c) -> p a b c", a=shape[0], b=shape[1])
        elif len(shape) == 4:
            v = v.rearrange("p (a b c d) -> p a b c d", a=shape[0], b=shape[1], c=shape[2])
        return v

    def mark(self):
        return self.off

    def reset(self, m):
        self.off = m


class Builder:
    def __init__(self, stop_after=None, debug=()):
        self.stop_after = stop_after
        self.debug = set(debug)
        self.nc = bass.Bass("TRN2", target_bir_lowering=False)
        self.sc = Sched()
        self.es = ExitStack()
        nc = self.nc
        self.ar = Arena(nc, 212480)
        self.ps = [nc.alloc_psum_tensor(f"psb{i}", [128, 512], F32) for i in range(8)]
        self.dram = {}
        self.uid = 0

    def din(self, name, shape, dtype=F32):
        t = self.nc.dram_tensor(name, list(shape), dtype, kind="ExternalInput").ap()
        self.dram[name] = t
        return t

    def dout(self, name, shape, dtype=F32):
        t = self.nc.dram_tensor(name, list(shape), dtype, kind="ExternalOutput").ap()
        self.dram[name] = t
        return t

    def dscr(self, name, shape, dtype=F32):
        t = self.nc.dram_tensor(name, list(shape), dtype, kind="Internal").ap()
        self.dram[name] = t
        return t

    def alloc(self, shape, dtype=F32, name=None):
        self.uid += 1
        return T(self.ar.alloc(shape, dtype), (name or "t", self.uid))

    def bank(self, i):
        return T(self.ps[i][:, :], ("ps", i))

    @staticmethod
    def _ap(x):
        return x.ap if isinstance(x, T) else x

    @staticmethod
    def _keys(*xs):
        return [x.key for x in xs if isinstance(x, T)]

    @staticmethod
    def _n(ap):
        n = 1
        for s_ in ap.shape[1:]:
            n *= s_
        return n

    @staticmethod
    def _bpe(ap):
        return 4 if ap.dtype in (F32, I32) else 2

    def _ecost(self, eng, n, mult=1.0):
        if eng == "act":
            return 0.2 + n / 1400.0
        if eng == "dve":
            return 0.08 + mult * n / 960.0
        return 0.25 + mult * n / 500.0

    def dma(self, out, in_, eng="sp", reads_extra=(), **kw):
        o, i = self._ap(out), self._ap(in_)
        nb = self._n(o) * o.shape[0] * self._bpe(o)
        return self.sc.dma(eng, lambda e: e.dma_start(out=o, in_=i, **kw), self._keys(in_, *reads_extra),
                           self._keys(out), nbytes=nb)

    def mm(self, out, lhsT, rhs, start=True, stop=True, **kw):
        o, l, r = out.ap, lhsT.ap, rhs.ap
        c = 0.02 + max(self._n(r), 64) / 1950.0
        if l.dtype == F32:
            c *= 4
        return self.sc.op("pe", lambda e: e.matmul(o, l, r, start=start, stop=stop, **kw),
                          self._keys(lhsT, rhs), self._keys(out), cost=c)

    def tr(self, out, in_, ident, **kw):
        o, i, d = out.ap, in_.ap, ident.ap
        return self.sc.op("pe", lambda e: e.transpose(o, i, d, **kw), self._keys(in_, ident), self._keys(out),
                          cost=0.12)

    def act(self, out, in_, func, scale=1.0, bias=0.0, accum=None):
        o, i = out.ap, in_.ap
        kw = {}
        rd = self._keys(in_, scale, bias)
        wr = self._keys(out)
        if accum is not None:
            kw["accum_out"] = accum.ap
            wr += self._keys(accum)
        sc_, bi_ = self._ap(scale), self._ap(bias)
        return self.sc.op("act", lambda e: e.activation(out=o, in_=i, func=func, scale=sc_, bias=bi_, **kw), rd, wr,
                          cost=self._ecost("act", self._n(i)))

    def tt(self, eng, out, a, b, op):
        o, x, y = out.ap, a.ap, b.ap
        return self.sc.op(eng, lambda e: e.tensor_tensor(out=o, in0=x, in1=y, op=op), self._keys(a, b), self._keys(out),
                          cost=self._ecost(eng, self._n(o)))

    def ts(self, eng, out, a, s1, op0, s2=None, op1=None):
        o, x = out.ap, a.ap
        p1, p2 = self._ap(s1), self._ap(s2)
        kw = {} if op1 is None else {"op1": op1}
        return self.sc.op(eng, lambda e: e.tensor_scalar(out=o, in0=x, scalar1=p1, scalar2=p2, op0=op0, **kw),
                          self._keys(a, s1, s2), self._keys(out), cost=self._ecost(eng, self._n(o)))

    def stt(self, eng, out, a, scalar, b, op0, op1):
        o, x, y = out.ap, a.ap, b.ap
        s_ = self._ap(scalar)
        return self.sc.op(eng, lambda e: e.scalar_tensor_tensor(out=o, in0=x, scalar=s_, in1=y, op0=op0, op1=op1),
                          self._keys(a, scalar, b), self._keys(out), cost=self._ecost(eng, self._n(o)))

    def cp(self, eng, out, a):
        o, x = out.ap, a.ap
        c = self._ecost(eng, self._n(o))
        if eng == "act":
            return self.sc.op("act", lambda e: e.activation(out=o, in_=x, func=AF.Copy), self._keys(a), self._keys(out),
                              cost=c)
        return self.sc.op(eng, lambda e: e.tensor_copy(out=o, in_=x), self._keys(a), self._keys(out), cost=c)

    def memset(self, eng, out, val):
        o = out.ap
        return self.sc.op(eng, lambda e: e.memset(o, val), [], self._keys(out), cost=self._ecost(eng, self._n(o)))

    def gen(self, eng, name, reads, writes, **kw):
        kw2 = {k: self._ap(v_) for k, v_ in kw.items()}
        n = self._n(kw2["out"]) if "out" in kw2 else 64
        mult = 2.0 if name == "tensor_tensor_scan" else 1.0
        return self.sc.op(eng, lambda e: getattr(e, name)(**kw2), self._keys(*reads), self._keys(*writes),
                          cost=self._ecost(eng, n, mult))


class T:
    __slots__ = ("ap", "key")

    def __init__(self, ap, key):
        self.ap = ap
        self.key = key

    def __getitem__(self, idx):
        return T(self.ap[idx], self.key)

    def k(self, *suffix):
        return T(self.ap, (self.key,) + tuple(suffix))

    def re(self, pattern, **kw):
        return T(self.ap.rearrange(pattern, **kw), self.key)

    def bc(self, shape):
        ap = self.ap
        while len(ap.shape) < len(shape):
            ap = ap.unsqueeze(len(ap.shape))
        return T(ap.broadcast_to(list(shape)), self.key)

    def cast(self, dt_):
        return T(self.ap.bitcast(dt_), self.key)

    def ubc(self, axis, shape):
        return T(self.ap.unsqueeze(axis).broadcast_to(list(shape)), self.key)


class Prog(Builder):
    def declare_io(self):
        self.x = self.din("x", [S, D])
        self.pos = self.din("positions", [NT, 128], I32)
        for n, shp in (("ln_mix_g", [2, D]), ("ln_mix_b", [2, D]), ("ln_ffn_g", [2, D]), ("ln_ffn_b", [2, D]),
                       ("w_ff1", [2, D, DFF]), ("w_ff2", [2, DFF, D]),
                       ("ssm_lam_re", [64, 64]), ("ssm_lam_im", [64, 64]), ("ssm_log_dt", [1, 64]),
                       ("ssm_b_re", [64, 64, 16]), ("ssm_b_im", [64, 64, 16]),
                       ("ssm_c_re", [64, 16, 64]), ("ssm_c_im", [64, 16, 64]), ("ssm_d", [D]),
                       ("ssm_w_glu", [D, 2 * D]), ("ssm_w_out", [D, D]),
                       ("kv_w_a", [D, 320]), ("kv_norm_g", [1, 256]), ("kv_w_b", [256, 2048]),
                       ("q_w_a", [D, 384]), ("q_norm_g", [1, 384]), ("q_w_b", [384, 1536]),
                       ("attn_w_o", [D, D])):
            self.din(n, shp)
        self.out = self.dout("out", [S, D])

    def consts(self):
        self.identf = self.alloc([128], F32, "identf")
        self.identb = self.alloc([128], BF16, "identb")
        self.memset("pool", self.identf, 0.0)
        self.gen("pool", "affine_select", [self.identf], [self.identf], out=self.identf, in_=self.identf,
                 pattern=[[-1, 128]], compare_op=ALU.not_equal, fill=1.0, base=0, channel_multiplier=1)
        self.cp("pool", self.identb, self.identf)

    def phase_a0(self):
        xs = self.xs
        for tt in range(NT):
            slot = xs[tt % 2]
            self.dma(slot, self.x[tt * 128:(tt + 1) * 128, :])
            for half in range(2):
                bank = self.bank(4 + (tt % 2) * 2 + half)
                for j in range(4):
                    kt = half * 4 + j
                    self.tr(bank[:, j * 128:(j + 1) * 128], slot[:, kt * 128:(kt + 1) * 128], self.identf)
                o = self.UT[:, half * 4:half * 4 + 4, :, tt * 16:(tt + 1) * 16].k(half, tt).re("p a s j -> p a j s")
                i = bank.re("p (a j s) -> p a j s", a=4, s=8)
                self.cp("act" if half == 0 else "dve", o, i)
        self.sc.barrier()
        self.ar.reset(self.a1_mark)

    def phase_a1(self):
        dr = self.dram
        A = self.alloc
        self.UT = A([8, 8, 512], BF16, "UT")
        self.ut_top = self.ar.mark()
        self.ZupW = [A([8, 8, 128], BF16, "zupw_re"), A([8, 8, 128], BF16, "zupw_im")]
        self.CarW = [A([32, 8, 32], BF16, "carw_re"), A([32, 8, 32], BF16, "carw_nim")]
        self.BD = A([8, 8, 128], BF16, "bd")
        self.Rch = A([32], F32, "rch")
        self.f8 = A([32], F32, "f8")
        self.Dsk = A([8], F32, "dsk")
        m = self.ar.mark()
        sm = lambda n: A([32], F32, n)
        CIN = [A([8, 128], F32, "cin_re"), A([8, 128], F32, "cin_im")]
        for ci, nm in enumerate(("ssm_c_re", "ssm_c_im")):
            self.memset("pool", CIN[ci], 0.0)
            v = dr[nm].rearrange("(kt qq g) c p -> qq g c kt p", qq=4, g=2)
            for qq in range(4):
                for g2 in range(2):
                    p0 = qq * 32 + g2 * 16
                    self.dma(CIN[ci][p0:p0 + 16, :, g2 * 64:(g2 + 1) * 64], v[qq, g2])
        ldt = sm("ldt")
        for g2 in range(2):
            src = dr["ssm_log_dt"][0, g2::2].partition_broadcast(64)
            self.dma(ldt[g2 * 64:(g2 + 1) * 64, :], src, allow_slow_non_contiguous=True)
        self.dma(self.Dsk, dr["ssm_d"].rearrange("(k p) -> p k", p=128), allow_slow_non_contiguous=True)
        Bbr, Bbi = A([32, 16], F32, "Bbr"), A([32, 16], F32, "Bbi")
        BbBD = [A([32, 32], F32, "bbbd_re"), A([32, 32], F32, "bbbd_im")]
        lr, li = sm("lr"), sm("li")
        m0, m1 = A([1], F32, "m0"), A([1], F32, "m1")
        self.memset("pool", m0, 0.0)
        self.memset("pool", m0[0:64], 1.0)
        self.memset("pool", m1, 0.0)
        self.memset("pool", m1[64:128], 1.0)
        bdm = A([128], F32, "bdm")
        self.memset("pool", bdm, 1.0)
        for i in range(4):
            blk = bdm[:, 32 * i:32 * i + 32]
            self.gen("pool", "affine_select", [bdm], [bdm], out=blk, in_=blk, pattern=[[0, 32]],
                     compare_op=ALU.is_ge, fill=0.0, base=-32 * i, channel_multiplier=1)
            self.gen("pool", "affine_select", [bdm], [bdm], out=blk, in_=blk, pattern=[[0, 32]],
                     compare_op=ALU.is_ge, fill=0.0, base=32 * i + 31, channel_multiplier=-1)
        cre, cim = sm("cre"), sm("cim")
        AR = [sm(f"ar{k}") for k in range(9)]
        AI = [sm(f"ai{k}") for k in range(9)]
        self.xs = [A([D], F32, "xs") for _ in range(2)]
        m_short = self.ar.mark()
        Lin = A([256], F32, "Lin")
        self.memset("pool", Lin, 0.0)
        self.dma(Lin[0:32, 0:128], dr["ssm_lam_re"].rearrange("(q g) p -> q (g p)", g=2))
        self.dma(Lin[0:32, 128:256], dr["ssm_lam_im"].rearrange("(q g) p -> q (g p)", g=2))
        Br, Bi = A([32, 16], F32, "Br"), A([32, 16], F32, "Bi")
        for g2 in range(2):
            self.dma(Br[g2 * 64:(g2 + 1) * 64], dr["ssm_b_re"].rearrange("(q g) p c -> g p q c", g=2)[g2])
            self.dma(Bi[g2 * 64:(g2 + 1) * 64], dr["ssm_b_im"].rearrange("(q g) p c -> g p q c", g=2)[g2])
        bk = self.bank(0)
        self.tr(bk[:, 0:32], Lin[0:32, 0:128], self.identf[0:32, 0:32])
        self.tr(bk[:, 32:64], Lin[0:32, 128:256], self.identf[0:32, 0:32])
        self.cp("dve", lr, bk[:, 0:32])
        self.cp("dve", li, bk[:, 32:64])
        dt, xr, mag, ang, trn, trc = sm("dt"), sm("xr"), sm("mag"), sm("ang"), sm("trn"), sm("trc")
        ti = A([32], I32, "ti")
        rs, rc, sn, cs = sm("rs"), sm("rc"), sm("sn"), sm("cs")
        self.act(dt, ldt, AF.Exp)
        self.tt("dve", xr, lr, dt, ALU.mult)
        self.act(mag, xr, AF.Exp)
        self.act(self.Rch, xr, AF.Exp, scale=8.0)
        self.tt("dve", ang, li, dt, ALU.mult)
        self.ts("dve", trn, ang, 1.0 / TWO_PI, ALU.mult)
        self.ts("dve", trc, trn, 0.25, ALU.add)
        self.cp("dve", ti, trn)
        self.tt("dve", rs, trn, ti, ALU.subtract)
        ti2 = A([32], I32, "ti2")
        self.cp("dve", ti2, trc)
        self.tt("dve", rc, trc, ti2, ALU.subtract)
        self.act(sn, rs, AF.Sin, scale=TWO_PI)
        self.act(cs, rc, AF.Sin, scale=TWO_PI)
        t8 = sm("t8")
        ti3 = A([32], I32, "ti3")
        self.ts("dve", t8, rs, 8.0, ALU.mult)
        self.cp("dve", ti3, t8)
        self.tt("dve", self.f8, t8, ti3, ALU.subtract)
        are, aim = sm("are"), sm("aim")
        self.tt("dve", are, mag, cs, ALU.mult)
        self.tt("dve", aim, mag, sn, ALU.mult)
        den, t1, t2, inv, am1 = sm("den"), sm("t1"), sm("t2"), sm("inv"), sm("am1")
        self.tt("dve", t1, lr, lr, ALU.mult)
        self.tt("dve", t2, li, li, ALU.mult)
        self.tt("dve", den, t1, t2, ALU.add)
        self.gen("dve", "reciprocal", [den], [inv], out=inv, in_=den)
        self.ts("dve", am1, are, -1.0, ALU.add)
        t3, t4 = sm("t3"), sm("t4")
        self.tt("dve", t3, am1, lr, ALU.mult)
        self.tt("dve", t4, aim, li, ALU.mult)
        self.tt("dve", t1, t3, t4, ALU.add)
        self.tt("dve", cre, t1, inv, ALU.mult)
        t5, t6 = sm("t5"), sm("t6")
        self.tt("dve", t5, aim, lr, ALU.mult)
        self.tt("dve", t6, am1, li, ALU.mult)
        self.tt("dve", t2, t5, t6, ALU.subtract)
        self.tt("dve", cim, t2, inv, ALU.mult)
        self.memset("pool", AR[0], 1.0)
        self.memset("pool", AI[0], 0.0)
        self.cp("dve", AR[1], are)
        self.cp("dve", AI[1], aim)
        u1, u2, u3, u4 = sm("u1"), sm("u2"), sm("u3"), sm("u4")
        for k in range(1, 8):
            self.tt("dve", u1, AR[k], are, ALU.mult)
            self.tt("dve", u2, AI[k], aim, ALU.mult)
            self.tt("dve", AR[k + 1], u1, u2, ALU.subtract)
            self.tt("dve", u3, AR[k], aim, ALU.mult)
            self.tt("dve", u4, AI[k], are, ALU.mult)
            self.tt("dve", AI[k + 1], u3, u4, ALU.add)
        w1, w2 = A([32, 16], F32, "w1"), A([32, 16], F32, "w2")
        s16 = [128, 32, 16]
        self.tt("dve", w1, Br, cre.bc(s16), ALU.mult)
        self.tt("dve", w2, Bi, cim.bc(s16), ALU.mult)
        self.tt("dve", Bbr, w1, w2, ALU.subtract)
        w3, w4 = A([32, 16], F32, "w3"), A([32, 16], F32, "w4")
        self.tt("dve", w3, Bi, cre.bc(s16), ALU.mult)
        self.tt("dve", w4, Br, cim.bc(s16), ALU.mult)
        self.tt("dve", Bbi, w3, w4, ALU.add)
        for src, dst in ((Bbr, BbBD[0]), (Bbi, BbBD[1])):
            self.ts("pool", dst[:, :, 0:16], src, m0, ALU.mult)
            self.ts("pool", dst[:, :, 16:32], src, m1, ALU.mult)
        self.sc.barrier()
        self.ar.reset(m_short)
        s32 = [128, 32, 32]
        TP = [A([32, 32], F32, f"tp{i}") for i in range(4)]
        bi_ = 0
        for s_ in range(8):
            k = 7 - s_
            Are, Aim, e2, e4 = TP
            self.tt("dve", Are, BbBD[0], AR[k].bc(s32), ALU.mult)
            self.tt("pool", e2, BbBD[1], AI[k].bc(s32), ALU.mult)
            self.tt("dve", Are, Are, e2, ALU.subtract)
            self.tt("dve", Aim, BbBD[1], AR[k].bc(s32), ALU.mult)
            self.tt("pool", e4, BbBD[0], AI[k].bc(s32), ALU.mult)
            self.tt("dve", Aim, Aim, e4, ALU.add)
            for ri, src_ in enumerate((Are, Aim)):
                for half in range(2):
                    bk = self.bank(bi_ % 4)
                    bi_ += 1
                    for j in range(4):
                        kt = half * 4 + j
                        self.tr(bk[:, j * 128:(j + 1) * 128],
                                src_[:, kt * 4:(kt + 1) * 4, :].re("p a b -> p (a b)"), self.identf)
                    self.cp("act", self.ZupW[ri][:, half * 4:half * 4 + 4, s_, :], bk.re("p (a b) -> p a b", a=4))
        CBD = [A([32, 32], F32, "cbd_re"), A([32, 32], F32, "cbd_im")]
        for ci in range(2):
            for half in range(2):
                bk = self.bank(4 + (ci * 2 + half) % 4)
                for j in range(4):
                    kt = half * 4 + j
                    self.tr(bk[:, j * 128:(j + 1) * 128], CIN[ci][:, kt, :], self.identf)
                self.cp("dve", CBD[ci][:, half * 16:half * 16 + 16, :], bk.re("p (a b) -> p a b", a=16))
        for k in range(9):
            CAre, nCAim, f2, f4 = TP
            self.tt("dve", CAre, CBD[0], AR[k].bc(s32), ALU.mult)
            self.tt("pool", f2, CBD[1], AI[k].bc(s32), ALU.mult)
            self.tt("dve", CAre, CAre, f2, ALU.subtract)
            self.tt("dve", nCAim, CBD[0], AI[k].bc(s32), ALU.mult)
            self.tt("pool", f4, CBD[1], AR[k].bc(s32), ALU.mult)
            self.stt("dve", nCAim, nCAim, -1.0, f4, ALU.mult, ALU.subtract)
            if k >= 1:
                self.cp("act", self.CarW[0][:, :, k - 1, :], CAre)
                self.cp("act", self.CarW[1][:, :, k - 1, :], nCAim)
            if k <= 7:
                for half in range(2):
                    bk = self.bank(half * 2 + (k % 2))
                    for j in range(4):
                        kt = half * 4 + j
                        o = bk[:, j * 128:(j + 1) * 128]
                        fl = lambda t_: t_[:, kt * 4:(kt + 1) * 4, :].re("p a b -> p (a b)")
                        self.mm(o, fl(BbBD[0]), fl(CAre), True, False)
                        self.mm(o, fl(BbBD[1]), fl(nCAim), False, True)
                    self.tt("dve", self.BD[:, half * 4:half * 4 + 4, k, :], bk.re("p (a b) -> p a b", a=4),
                            T(bdm.ap.unsqueeze(1).broadcast_to([128, 4, 128]), bdm.key), ALU.mult)
        self.a1_mark = m

    def phase_a2(self):
        A = self.alloc
        UT = self.UT
        m = self.ar.mark()
        iota_f = A([512], F32, "iota_f")
        mi = self.ar.mark()
        iota_i = A([512], I32, "iota_i")
        self.gen("pool", "iota", [], [iota_i], out=iota_i, pattern=[[1, 512]], base=0, channel_multiplier=0)
        self.cp("pool", iota_f, iota_i)
        self.ar.reset(mi)
        ph = A([512], F32, "ph")
        pi = A([512], I32, "pi")
        ph2 = A([512], F32, "ph2")
        pi2 = A([512], I32, "pi2")
        halfpi = A([1], F32, "halfpi")
        self.memset("pool", halfpi, math.pi / 2.0)
        NSL = 2
        cos_t = [A([512], F32, "cos") for _ in range(NSL)]
        sin_t = [A([512], F32, "sin") for _ in range(NSL)]
        T1 = [A([512], F32, "t1") for _ in range(NSL)]
        T2 = [A([512], F32, "t2") for _ in range(NSL)]
        T3 = [A([512], F32, "t3") for _ in range(NSL)]
        GR = [A([512], F32, "gr") for _ in range(NSL)]
        GI = [A([512], F32, "gi") for _ in range(NSL)]
        NH = 8
        Hre = [A([512], BF16, "hre") for _ in range(NH)]
        Him = [A([512], BF16, "him") for _ in range(NH)]
        evt = [A([512], F32, "evt") for _ in range(2)]
        for h in Hre + Him:
            self.memset("pool", h[:, 0:1], 0.0)
        obi = 0
        n1 = 511
        for kt in range(8):
            for pl in range(4):
                q = kt * 4 + pl
                sl_ = q % NSL
                hs = q % NH
                zr, zi = self.bank(sl_), self.bank(2 + sl_)
                for ri, zb in ((0, zr), (1, zi)):
                    for s_ in range(8):
                        self.mm(zb, self.ZupW[ri][32 * pl:32 * pl + 32, kt, s_, :],
                                UT[32 * pl:32 * pl + 32, kt, s_, :].k(kt, s_),
                                s_ == 0, s_ == 7, tile_position=(32 * pl, 0))
                f8q = self.f8[:, q:q + 1]
                c_, s__ = cos_t[sl_], sin_t[sl_]
                self.act(ph, iota_f, AF.Identity, scale=f8q)
                self.cp("dve", pi, ph)
                self.tt("dve", ph, ph, pi, ALU.subtract)
                self.act(s__, ph, AF.Sin, scale=TWO_PI)
                self.act(ph2, ph, AF.Abs)
                self.act(c_, ph2, AF.Sin, scale=-TWO_PI, bias=halfpi)
                t1, t2, t3, gr, gi = T1[sl_], T2[sl_], T3[sl_], GR[sl_], GI[sl_]
                self.tt("dve", t1, zr, c_, ALU.mult)
                self.tt("dve", t2, zi, s__, ALU.mult)
                self.tt("pool", t1, t1, t2, ALU.add)
                self.tt("dve", t2, zi, c_, ALU.mult)
                self.tt("dve", t3, zr, s__, ALU.mult)
                self.tt("pool", t2, t2, t3, ALU.subtract)
                Rb = self.Rch[:, q:q + 1].bc([128, 512])
                self.gen("dve", "tensor_tensor_scan", [Rb, t1], [gr], out=gr, data0=Rb, data1=t1, initial=0.0,
                         op0=ALU.mult, op1=ALU.add)
                self.gen("dve", "tensor_tensor_scan", [Rb, t2], [gi], out=gi, data0=Rb, data1=t2, initial=0.0,
                         op0=ALU.mult, op1=ALU.add)
                self.tt("dve", t1[:, 0:n1], gr[:, 0:n1], c_[:, 0:n1], ALU.mult)
                self.tt("pool", t3[:, 0:n1], gi[:, 0:n1], s__[:, 0:n1], ALU.mult)
                self.tt("pool", Hre[hs][:, 1:512], t1[:, 0:n1], t3[:, 0:n1], ALU.subtract)
                self.tt("dve", t2[:, 0:n1], gi[:, 0:n1], c_[:, 0:n1], ALU.mult)
                self.tt("pool", t3[:, 0:n1], gr[:, 0:n1], s__[:, 0:n1], ALU.mult)
                self.tt("pool", Him[hs][:, 1:512], t2[:, 0:n1], t3[:, 0:n1], ALU.add)
            for t in range(7, -1, -1):
                ob = self.bank(4 + (obi % 4))
                obi += 1
                for s_ in range(t + 1):
                    self.mm(ob, self.BD[:, kt, t - s_, :], UT[:, kt, s_, :].k(kt, s_), s_ == 0, False)
                for pl in range(4):
                    q = kt * 4 + pl
                    hs = q % NH
                    o_ = ob[32 * pl:32 * pl + 32, :]
                    self.mm(o_, self.CarW[0][:, q, t, :], Hre[hs], False, False, tile_position=(0, 32 * pl))
                    self.mm(o_, self.CarW[1][:, q, t, :], Him[hs], False, True, tile_position=(0, 32 * pl))
                ev = evt[t % 2]
                ut = UT[:, kt, t, :].k(kt, t)
                self.stt("dve", ev, ut, self.Dsk[:, kt:kt + 1], ob, ALU.mult, ALU.add)
                self.act(ut, ev, AF.Gelu)
        self.sc.barrier()
        self.ar.reset(m)

    def load_w(self, dst, src, nkt, split=None):
        v = src.rearrange("(k p) n -> p k n", p=128)
        n = dst.ap.shape[2]
        cw = min(n, 2048)
        if split is None:
            stg = [self.alloc([cw], F32, "wstage") for _ in range(2)]
        else:
            stg = [s_[:, 0:cw] for s_ in split]
        engs = ("pool", "act", "dve")
        i = 0
        for kt in range(nkt):
            for c0 in range(0, n, cw):
                s_ = stg[i % 2]
                self.dma(s_, v[:, kt, c0:c0 + cw])
                self.cp(engs[i % 3], dst[:, kt, c0:c0 + cw].k("w", kt), s_)
                i += 1
    def load_bcast(self, dst, row_ap):
        self.dma(dst, row_ap.partition_broadcast(128))

    def layernorm(self, r, out, gb, bb, scr):
        st, mv, sd = scr["st"], scr["mv"], scr["sd"]
        for c in range(2):
            self.gen("dve", "bn_stats", [r], [st], out=st[:, c, :], in_=r[:, c * 512:(c + 1) * 512])
        self.gen("dve", "bn_aggr", [st], [mv], out=mv, in_=st.re("p a b -> p (a b)"))
        self.act(sd, mv[:, 1:2], AF.Sqrt, scale=1.0, bias=scr["eps"])
        self.gen("dve", "reciprocal", [sd], [sd], out=sd, in_=sd)
        self.ts("dve", out, r, mv[:, 0:1], ALU.subtract, sd, ALU.mult)
        self.tt("pool", out, out, gb, ALU.mult)
        self.tt("pool", out, out, bb, ALU.add)

    def ln_scratch(self):
        A = self.alloc
        eps = A([1], F32, "eps")
        self.memset("pool", eps, LN_EPS)
        return [{"st": A([2, 6], F32, "st"), "mv": A([2], F32, "mv"), "sd": A([1], F32, "sd"), "eps": eps}
                for _ in range(2)]

    def to_feature_major(self, h, hT, banks):
        for half in range(2):
            bk = banks[half]
            for j in range(4):
                kt = half * 4 + j
                self.tr(bk[:, j * 128:(j + 1) * 128], h[:, kt * 128:(kt + 1) * 128], self.identf)
            self.cp("act", hT[:, half * 4:half * 4 + 4, :], bk.re("p (a b) -> p a b", a=4))

    def phase_b1(self):
        A = self.alloc
        dr = self.dram
        UT = self.UT
        m = self.ar.mark()
        self.H1 = self.dscr("H1", [S, D], F32)
        self.H1T = self.dscr("H1T", [128, 8, S], BF16)
        Wg = A([8, 2 * D], BF16, "wglu")
        Wo = A([8, D], BF16, "wout")
        self.load_w(Wg, dr["ssm_w_glu"], 8)
        self.load_w(Wo, dr["ssm_w_out"], 8)
        gb, bb = A([D], F32, "gb"), A([D], F32, "bb")
        self.load_bcast(gb, dr["ln_mix_g"][0])
        self.load_bcast(bb, dr["ln_mix_b"][0])
        lsc = self.ln_scratch()
        zT = [A([8, 512], BF16, "zT") for _ in range(2)]
        sg = [A([512], F32, "sg") for _ in range(2)]
        xt = [A([D], F32, "xt") for _ in range(2)]
        rr = [A([D], F32, "rr") for _ in range(2)]
        h1 = [A([D], F32, "h1") for _ in range(2)]
        h1T = [A([8, 512], BF16, "h1T") for _ in range(2)]
        xperm = self.x.rearrange("(c j t) d -> c t j d", j=64, t=8)
        h1perm = self.H1.rearrange("(c j t) d -> c t j d", j=64, t=8)
        n = 0
        for tb in range(S // 512):
            z = zT[tb % 2]
            hTb = h1T[tb % 2]
            for mt in range(8):
                bv, bg = self.bank(2 * (mt % 2)), self.bank(2 * (mt % 2) + 1)
                for which, bk in ((0, bv), (1, bg)):
                    c0 = which * D + mt * 128
                    for kt in range(8):
                        self.mm(bk, Wg[:, kt, c0:c0 + 128].k("w", kt),
                                UT[:, kt, :, tb * 64:(tb + 1) * 64], kt == 0, kt == 7)
                s_ = sg[mt % 2]
                self.act(s_, bg, AF.Sigmoid)
                self.tt("dve", z[:, mt, :], bv, s_, ALU.mult)
            for sub in range(4):
                tt_ = tb * 4 + sub
                x_, r_, h_ = xt[n % 2], rr[n % 2], h1[n % 2]
                sc_ = lsc[n % 2]
                n += 1
                for tl in range(2):
                    self.dma(x_[tl * 64:(tl + 1) * 64, :], xperm[tb, 2 * sub + tl])
                for nh in range(2):
                    bk = self.bank(4 + nh)
                    for kt in range(8):
                        self.mm(bk, z[:, kt, sub * 128:(sub + 1) * 128], Wo[:, kt, nh * 512:(nh + 1) * 512].k("w", kt),
                                kt == 0, kt == 7)
                    self.stt("dve", r_[:, nh * 512:(nh + 1) * 512], x_[:, nh * 512:(nh + 1) * 512], ALPHA, bk,
                             ALU.mult, ALU.add)
                self.layernorm(r_, h_, gb, bb, sc_)
                for tl in range(2):
                    self.dma(h1perm[tb, 2 * sub + tl], h_[tl * 64:(tl + 1) * 64, :])
                for half in range(2):
                    bk = self.bank(6 + half)
                    for j in range(4):
                        kt = half * 4 + j
                        self.tr(bk[:, j * 128:(j + 1) * 128], h_[:, kt * 128:(kt + 1) * 128], self.identf)
                    o = hTb[:, half * 4:half * 4 + 4, :].re("p a (j t) -> p a t j", t=8)[:, :, 2 * sub:2 * sub + 2, :]
                    self.cp("act", o, bk.re("p (a t j) -> p a t j", a=4, t=2))
            self.dma(self.H1T[:, :, tb * 512:(tb + 1) * 512], hTb)
        self.sc.barrier()
        self.ar.reset(m)

    def phase_ffn(self, layer, Hin, HinT, Hout, HoutT):
        A = self.alloc
        dr = self.dram
        m = self.ar.mark()
        W1 = A([8, DFF], BF16, "w1")
        W2 = A([32, D], BF16, "w2")
        stg = [A([1024], F32, "wstage") for _ in range(4)]
        v1 = dr["w_ff1"][layer].rearrange("(k p) n -> p k n", p=128)
        v2 = dr["w_ff2"][layer].rearrange("(k p) n -> p k n", p=128)
        engs = ("pool", "act", "dve")
        li = 0
        for cb in range(8):
            for kg in range(4):
                s_ = stg[li % 4]
                s2 = s_.re("p (a b) -> p a b", a=2)
                self.dma(s2, v1[:, kg * 2:(kg + 1) * 2, cb * 512:(cb + 1) * 512])
                self.cp(engs[li % 3], W1[:, kg * 2:(kg + 1) * 2, cb * 512:(cb + 1) * 512].k("w1", cb), s2)
                li += 1
            for r2 in range(4):
                s_ = stg[li % 4]
                f0 = cb * 4 + r2
                self.dma(s_, v2[:, f0, :])
                self.cp(engs[li % 3], W2[:, f0, :].k("w2", f0), s_)
                li += 1
        gb, bb = A([D], F32, "gb"), A([D], F32, "bb")
        self.load_bcast(gb, dr["ln_ffn_g"][layer])
        self.load_bcast(bb, dr["ln_ffn_b"][layer])
        lsc = self.ln_scratch()
        hT = [A([8, 256], BF16, "hT") for _ in range(2)]
        hin = [A([D], F32, "hin") for _ in range(4)]
        rr = [A([D], F32, "rr") for _ in range(2)]
        ho = [A([D], F32, "ho") for _ in range(2)]
        hoT = [A([8, 128], BF16, "hoT") for _ in range(2)]
        rl = [A([256], F32, "rl") for _ in range(3)]
        aT = [A([256], BF16, "aT") for _ in range(4)]
        n = 0
        fb = 0
        for blk in range(S // 256):
            t0 = blk * 256
            h_T = hT[blk % 2]
            hs_ = [hin[(2 * blk) % 4], hin[(2 * blk + 1) % 4]]
            self.dma(h_T, HinT[:, :, t0:t0 + 256])
            for sub in range(2):
                self.dma(hs_[sub], Hin[t0 + sub * 128:t0 + (sub + 1) * 128, :])
            acc = [[self.bank(0), self.bank(1)], [self.bank(2), self.bank(3)]]

            def ff2(ft, a_):
                for sub in range(2):
                    for nh in range(2):
                        self.mm(acc[sub][nh], a_[:, sub * 128:(sub + 1) * 128],
                                W2[:, ft, nh * 512:(nh + 1) * 512].k("w2", ft), ft == 0, ft == 31)

            prev = None
            for ft in range(32):
                nfb = 3 if HoutT is None else 2
                bk = self.bank(4 + fb % nfb)
                r_ = rl[fb % 3]
                a_ = aT[fb % 4]
                fb += 1
                for kt in range(8):
                    self.mm(bk[:, 0:256], W1[:, kt, ft * 128:(ft + 1) * 128].k("w1", ft // 4), h_T[:, kt, :], kt == 0, kt == 7)
                self.act(r_, bk[:, 0:256], AF.Relu)
                self.tt("pool", a_, r_, r_, ALU.mult)
                if prev is not None:
                    ff2(*prev)
                prev = (ft, a_)
            ff2(*prev)
            for sub in range(2):
                tt_ = blk * 2 + sub
                r2, h_o, h_oT = rr[n % 2], ho[n % 2], hoT[n % 2]
                sc_ = lsc[n % 2]
                n += 1
                for nh in range(2):
                    self.stt("dve", r2[:, nh * 512:(nh + 1) * 512], hs_[sub][:, nh * 512:(nh + 1) * 512], ALPHA,
                             acc[sub][nh], ALU.mult, ALU.add)
                self.layernorm(r2, h_o, gb, bb, sc_)
                self.dma(Hout[tt_ * 128:(tt_ + 1) * 128, :], h_o)
                if HoutT is not None:
                    self.to_feature_major(h_o, h_oT, (self.bank(6), self.bank(7)))
                    self.dma(HoutT[:, :, tt_ * 128:(tt_ + 1) * 128], h_oT)
        self.sc.barrier()
        self.ar.reset(m)

    def rmsnorm_tm(self, out_bf, bank_ap, n, g_b, st, mv, rs, eps):
        self.gen("dve", "bn_stats", [bank_ap], [st], out=st, in_=bank_ap)
        self.gen("dve", "bn_aggr", [st], [mv], out=mv, in_=st)
        self.stt("dve", rs, mv[:, 0:1], mv[:, 0:1], mv[:, 1:2], ALU.mult, ALU.add)
        self.act(rs, rs, AF.Sqrt, scale=1.0, bias=eps)
        self.gen("dve", "reciprocal", [rs], [rs], out=rs, in_=rs)
        self.stt("dve", out_bf, bank_ap, rs, g_b, ALU.mult, ALU.mult)

    def phase_c0(self):
        A = self.alloc
        dr = self.dram
        self.Wkvb = A([2, 2048], BF16, "wkvb")
        self.Wqb = A([3, 1536], BF16, "wqb")
        self.load_w(self.Wkvb, dr["kv_w_b"], 2)
        self.load_w(self.Wqb, dr["q_w_b"], 3)
        import os
        if os.environ.get("C0_PAD"):
            self.alloc([int(os.environ["C0_PAD"])], F32, "pad")
        self.cosT = A([NT, 32], F32, "cosT")
        self.sinT = A([NT, 32], F32, "sinT")
        self.ckvT = A([2, S], BF16, "ckvT")
        self.kropeT = A([S], BF16, "kropeT")
        self.cqT = A([3, S], BF16, "cqT")
        self.c_top = self.ar.mark()
        Wkva = A([8, 320], BF16, "wkva")
        Wqa = A([8, 384], BF16, "wqa")
        self.load_w(Wkva, dr["kv_w_a"], 8)
        self.load_w(Wqa, dr["q_w_a"], 8)
        gkv, gq = A([256], F32, "gkv"), A([384], F32, "gq")
        self.load_bcast(gkv, dr["kv_norm_g"][0])
        self.load_bcast(gq, dr["q_norm_g"][0])
        eps = A([1], F32, "rmseps")
        self.memset("pool", eps, RMS_EPS)
        import os
        SK = os.environ.get("C0_SKIP", "").split(",")
        C0NT = int(os.environ.get("C0_NT", str(NT)))
        m2 = self.ar.mark()
        pin = A([128], I32, "pin")
        pinf = A([128], F32, "pinf")
        posf = A([NT], F32, "posf")
        invf = A([32], F32, "invf")
        self.memset("pool", pinf, 0.0)
        self.dma(pin[0:NT, :], self.pos)
        self.cp("dve", pinf[0:NT, :], pin[0:NT, :])
        bk = self.bank(0)
        if "ptr" not in SK:
            self.tr(bk[:, 0:NT], pinf[0:NT, :], self.identf[0:NT, 0:NT])
            self.cp("dve", posf, bk[:, 0:NT])
        iv = (np.float32(10000.0) ** (-(np.arange(32, dtype=np.float32) / np.float32(32.0)))).astype(np.float32)
        for i_ in range(32):
            self.memset("pool", invf[:, i_:i_ + 1], float(iv[i_]))
        s3 = [128, NT, 32]
        if "tab" in SK:
            self.sc.barrier()
            self.ar.reset(m2)
            return
        ang = A([NT, 32], F32, "ang")
        angc = A([NT, 32], F32, "angc")
        ai = A([NT, 32], I32, "ai")
        self.tt("dve", ang, posf.bc(s3), invf.ubc(1, s3), ALU.mult)
        self.ts("dve", ang, ang, 1.0 / TWO_PI, ALU.mult)
        self.ts("dve", angc, ang, 0.25, ALU.add)
        self.cp("dve", ai, ang)
        self.tt("dve", ang, ang, ai, ALU.subtract)
        self.act(self.sinT, ang, AF.Sin, scale=TWO_PI)
        self.cp("dve", ai, angc)
        self.tt("dve", angc, angc, ai, ALU.subtract)
        self.act(self.cosT, angc, AF.Sin, scale=TWO_PI)
        self.sc.barrier()
        if not _os.environ.get("NO_M2_RESET"):
            self.ar.reset(m2)
        h2T = [A([8, 128], BF16, "h2T") for _ in range(2)]
        stk = [A([6], F32, "stk") for _ in range(2)]
        stq = [A([6], F32, "stq") for _ in range(2)]
        mvk = [A([2], F32, "mvk") for _ in range(2)]
        mvq = [A([2], F32, "mvq") for _ in range(2)]
        rk = [A([1], F32, "rk") for _ in range(2)]
        rq = [A([1], F32, "rq") for _ in range(2)]
        ckv_tm = [A([256], F32, "ckv_tm") for _ in range(2)]
        kr_tm = [A([128], F32, "kr_tm") for _ in range(2)]
        cq_tm = [A([384], F32, "cq_tm") for _ in range(2)]
        ra = [[A([32], F32, "ra") for _ in range(4)] for _ in range(2)]
        for tt_ in range(C0NT):
            sl_ = tt_ % 2
            tok = slice(tt_ * 128, (tt_ + 1) * 128)
            hT = h2T[sl_]
            self.dma(hT, self.H2T[:, :, tok])
            bA, bB, bC = self.bank(sl_), self.bank(2 + sl_), self.bank(4 + sl_)
            for kt in range(8):
                self.mm(bA[:, 0:320], hT[:, kt, :], Wkva[:, kt, :].k("w", kt), kt == 0, kt == 7)
            for kt in range(8):
                self.mm(bB[:, 0:384], hT[:, kt, :], Wqa[:, kt, :].k("w", kt), kt == 0, kt == 7)
            if "rms" not in SK:
                self.rmsnorm_tm(ckv_tm[sl_], bA[:, 0:256], 256, gkv, stk[sl_], mvk[sl_], rk[sl_], eps)
                self.rmsnorm_tm(cq_tm[sl_], bB[:, 0:384], 384, gq, stq[sl_], mvq[sl_], rq[sl_], eps)
            x1, x2 = bA[:, 256:288], bA[:, 288:320]
            c_, s_ = self.cosT[:, tt_, :], self.sinT[:, tt_, :]
            a1, a2, a3, a4 = ra[sl_]
            if "rope" not in SK:
                self.tt("dve", a1, x1, c_, ALU.mult)
                self.tt("dve", a2, x2, s_, ALU.mult)
                self.tt("dve", a3, x1, s_, ALU.mult)
                self.tt("dve", a4, x2, c_, ALU.mult)
            kr3 = kr_tm[sl_].re("p (r c) -> p r c", r=2)
            s2 = [128, 2, 32]
            for r_ in range(2):
                self.tt(KR_ENG, kr3[:, r_, 0:32], a1, a2, ALU.subtract)
                self.tt(KR_ENG, kr3[:, r_, 32:64], a3, a4, ALU.add)
            bC2 = self.bank(6 + sl_)
            srcs = [ckv_tm[sl_][:, 0:128], ckv_tm[sl_][:, 128:256], kr_tm[sl_], cq_tm[sl_][:, 0:128],
                    cq_tm[sl_][:, 128:256], cq_tm[sl_][:, 256:384]]
            for j, s__ in enumerate(srcs):
                dst = bC[:, j * 128:(j + 1) * 128] if j < 4 else bC2[:, (j - 4) * 128:(j - 3) * 128]
                self.tr(dst, s__, self.identf)
            if "ev1" not in SK:
                self.cp("act", self.ckvT[:, :, tok], bC[:, 0:256].re("p (a b) -> p a b", a=2))
            if "ev2" not in SK:
                self.cp("act", self.kropeT[:, tok], bC[:, 256:384])
            if "ev3" not in SK:
                self.cp("dve", self.cqT[:, 0, tok], bC[:, 384:512])
            if "ev4" not in SK:
                self.cp("dve", self.cqT[:, 1:3, tok], bC2[:, 0:256].re("p (a b) -> p a b", a=2))
        self.sc.barrier()
        self.ar.reset(self.c_top)

    def phase_c(self):
        A = self.alloc
        self.OT = self.dscr("OT", [128, 8, S], BF16)
        onesf = A([128], F32, "onesf")
        self.memset("pool", onesf, 1.0)
        accD = [A([512], F32, "accD") for _ in range(2)]
        accP = [A([512], F32, "accP") for _ in range(2)]
        knT = A([4, S], BF16, "knT")
        vtm = A([NT, 4, 128], BF16, "vtm")
        QN = [A([4, 512], BF16, "QN") for _ in range(2)]
        QR = [A([2, 512], BF16, "QR") for _ in range(2)]
        qr_tm = [A([256], F32, "qr_tm") for _ in range(2)]
        ra = [[A([4, 32], F32, "qra") for _ in range(4)] for _ in range(2)]
        PT = [A([512], BF16, "PT") for _ in range(4)]
        rd = [A([512], F32, "rd") for _ in range(1)]
        oTb = [A([4, 512], BF16, "oTb") for _ in range(2)]
        Wkvb4 = self.Wkvb.re("p k (h two d) -> p k h two d", two=2, d=128)
        Wqb3 = self.Wqb.re("p k (h e) -> p k h e", e=192)
        ev = 0
        for hh in range(2):
            for tb in range(S // 512):
                for hl in range(4):
                    h = 4 * hh + hl
                    bk = self.bank(ev % 3)
                    for j in range(2):
                        self.mm(bk, self.Wkvb[:, j, h * 256:h * 256 + 128].k("w", j),
                                self.ckvT[:, j, tb * 512:(tb + 1) * 512], j == 0, j == 1)
                    self.cp("act" if ev % 2 == 0 else "dve", knT[:, hl, tb * 512:(tb + 1) * 512], bk)
                    ev += 1
            for tt_ in range(NT):
                bk = self.bank(ev % 3)
                for j in range(2):
                    self.mm(bk, self.ckvT[:, j, tt_ * 128:(tt_ + 1) * 128],
                            Wkvb4[:, j, 4 * hh:4 * hh + 4, 1, :].k("w", j), j == 0, j == 1)
                self.cp("act" if ev % 2 == 0 else "dve", vtm[:, tt_, :, :], bk.re("p (a b) -> p a b", a=4))
                ev += 1
            sbi = 0
            pti = 0
            for qb in range(S // 512):
                qn, qr = QN[qb % 2], QR[qb % 2]
                qs = slice(qb * 512, (qb + 1) * 512)
                for hl in range(4):
                    h = 4 * hh + hl
                    bk = self.bank(7)
                    for j in range(3):
                        self.mm(bk, self.Wqb[:, j, h * 192:h * 192 + 128].k("w", j), self.cqT[:, j, qs], j == 0, j == 2)
                    self.cp("dve", qn[:, hl, :], bk)
                for sub in range(4):
                    tt_ = qb * 4 + sub
                    tok = slice(tt_ * 128, (tt_ + 1) * 128)
                    bk = self.bank(7)
                    for j in range(3):
                        self.mm(bk[:, 0:256], self.cqT[:, j, tok], Wqb3[:, j, 4 * hh:4 * hh + 4, 128:192].k("w", j),
                                j == 0, j == 2)
                    b4 = bk[:, 0:256].re("p (h r c) -> p h r c", h=4, r=2)
                    x1, x2 = b4[:, :, 0, :], b4[:, :, 1, :]
                    s4 = [128, 4, 32]
                    c_, s_ = self.cosT[:, tt_, :].ubc(1, s4), self.sinT[:, tt_, :].ubc(1, s4)
                    a1, a2, a3, a4 = ra[sub % 2]
                    self.tt("dve", a1, x1, c_, ALU.mult)
                    self.tt("dve", a2, x2, s_, ALU.mult)
                    self.tt("dve", a3, x1, s_, ALU.mult)
                    self.tt("dve", a4, x2, c_, ALU.mult)
                    q3 = qr_tm[sub % 2].re("p (h e) -> p h e", h=4)
                    self.tt("pool", q3[:, :, 0:32], a1, a2, ALU.subtract)
                    self.tt("pool", q3[:, :, 32:64], a3, a4, ALU.add)
                    for pr in range(2):
                        self.tr(bk[:, 256 + pr * 128:256 + (pr + 1) * 128], qr_tm[sub % 2][:, pr * 128:(pr + 1) * 128],
                                self.identf)
                    self.cp("act", qr[:, :, sub * 128:(sub + 1) * 128],
                            bk[:, 256:512].re("p (a b) -> p a b", a=2))
                ot = oTb[qb % 2]
                for hl in range(4):
                    pr, e = hl // 2, hl % 2
                    bO, bD = self.bank(3 + 2 * (hl % 2)), self.bank(4 + 2 * (hl % 2))
                    nk = 4 * qb + 4

                    def c0_of(kt):
                        i = kt - 4 * qb
                        return 128 * i if i > 0 else 0

                    def scores(kt):
                        sb = self.bank(kt % 3)
                        c0 = c0_of(kt)
                        ks = slice(kt * 128, (kt + 1) * 128)
                        self.mm(sb[:, c0:512], knT[:, hl, ks], qn[:, hl, c0:512], True, False)
                        self.mm(sb[:, c0:512], self.kropeT[64 * e:64 * e + 64, ks], qr[64 * e:64 * e + 64, pr, c0:512],
                                False, True, tile_position=(64 * e, 0))

                    aD, aP = accD[hl % 2], accP[hl % 2]
                    self.memset("pool", aD, 0.0)
                    self.memset("pool", aP, 0.0)
                    scores(0)
                    for kt in range(nk):
                        if kt + 1 < nk:
                            scores(kt + 1)
                        sb = self.bank(kt % 3)
                        c0 = c0_of(kt)
                        pt = PT[pti % 4]
                        pti += 1
                        self.act(pt[:, c0:512], sb[:, c0:512], AF.Exp, scale=SM_SCALE)
                        if kt >= 4 * qb:
                            dg = pt[:, c0:c0 + 128]
                            self.gen("pool", "affine_select", [pt], [pt], out=dg, in_=dg, pattern=[[1, 128]],
                                     compare_op=ALU.is_ge, fill=0.0, base=0, channel_multiplier=-1)
                        self.mm(bO[:, c0:512], vtm[:, kt, hl, :], pt[:, c0:512], kt == 0, kt == nk - 1)
                        if kt % 3 == 2:
                            self.tt("pool", aP[:, c0:512], aP[:, c0:512], pt[:, c0:512], ALU.add)
                        else:
                            self.tt("dve", aD[:, c0:512], aD[:, c0:512], pt[:, c0:512], ALU.add)
                    self.mm(bD, onesf, aD, True, False)
                    self.mm(bD, onesf, aP, False, True)
                    r_ = rd[0]
                    self.gen("dve", "reciprocal", [bD], [r_], out=r_, in_=bD)
                    self.tt("dve", ot[:, hl, :], bO, r_, ALU.mult)
                self.dma(self.OT[:, 4 * hh:4 * hh + 4, qs], ot)
        self.sc.barrier()
        if self.stop_after in ("c", "c2"):
            self.dbg_dump("dbg_ckvT", self.ckvT, [128, 2, S], BF16)
            self.dbg_dump("dbg_kropeT", self.kropeT, [128, S], BF16)
            self.dbg_dump("dbg_cqT", self.cqT, [128, 3, S], BF16)
            self.dbg_dump("dbg_knT", knT, [128, 4, S], BF16)
            self.dbg_dump("dbg_vtm", vtm, [128, NT, 4, 128], BF16)
            self.dbg_dump("dbg_cosT", self.cosT, [128, NT, 32], F32)
            self.sc.barrier()
        self.ar.off = 0
        self.consts()
        self.sc.barrier()

    def phase_c2(self):
        A = self.alloc
        dr = self.dram
        m = self.ar.mark()
        self.H3 = self.H1
        self.H3T = self.H1T
        Wo = A([8, D], BF16, "wo")
        self.load_w(Wo, dr["attn_w_o"], 8)
        gb, bb = A([D], F32, "gb"), A([D], F32, "bb")
        self.load_bcast(gb, dr["ln_mix_g"][1])
        self.load_bcast(bb, dr["ln_mix_b"][1])
        lsc = self.ln_scratch()
        oT = [A([8, 128], BF16, "oT") for _ in range(3)]
        h2 = [A([D], F32, "h2") for _ in range(3)]
        h3 = [A([D], F32, "h3") for _ in range(2)]
        h3T = [A([8, 128], BF16, "h3T") for _ in range(2)]
        for tt_ in range(NT):
            tok = slice(tt_ * 128, (tt_ + 1) * 128)
            o_, r_, h_, hT_ = oT[tt_ % 3], h2[tt_ % 3], h3[tt_ % 2], h3T[tt_ % 2]
            self.dma(o_, self.OT[:, :, tok])
            self.dma(r_, self.H2[tok, :])
            for nh in range(2):
                bk = self.bank(2 * (tt_ % 2) + nh)
                for kt in range(8):
                    self.mm(bk, o_[:, kt, :], Wo[:, kt, nh * 512:(nh + 1) * 512].k("w", kt), kt == 0, kt == 7)
                self.stt("dve", r_[:, nh * 512:(nh + 1) * 512], r_[:, nh * 512:(nh + 1) * 512], ALPHA, bk,
                         ALU.mult, ALU.add)
            self.layernorm(r_, h_, gb, bb, lsc[tt_ % 2])
            self.dma(self.H3[tok, :], h_)
            self.to_feature_major(h_, hT_, (self.bank(4 + 2 * (tt_ % 2)), self.bank(5 + 2 * (tt_ % 2))))
            self.dma(self.H3T[:, :, tok], hT_)
        self.sc.barrier()
        self.ar.reset(m)

    def finish(self):
        nc = self.nc
        with self.es:
            with nc.Block() as block:
                self.sc.emit(nc, block, self.es)
        return nc

    def dbg_dump(self, name, src, shape, dtype):
        t = self.dout(name, shape, dtype)
        self.dma(t, src)
        return t

    def dbg_copyT(self, name, src_dram):
        t = self.dout(name, [128, 8, S], BF16)
        buf = [self.alloc([8, 512], BF16, "dbgbufT") for _ in range(2)]
        for i in range(S // 512):
            self.dma(buf[i % 2], src_dram[:, :, i * 512:(i + 1) * 512])
            self.dma(t[:, :, i * 512:(i + 1) * 512], buf[i % 2])
        return t

    def dbg_copy(self, name, src_dram, shape, dtype):
        t = self.dout(name, shape, dtype)
        buf = [self.alloc([D], F32, "dbgbuf") for _ in range(2)]
        for i in range(shape[0] // 128):
            self.dma(buf[i % 2], src_dram[i * 128:(i + 1) * 128, :])
            self.dma(t[i * 128:(i + 1) * 128, :], buf[i % 2])
        return t


def build(stop_after=None, debug=()):
    b = Prog(stop_after, debug)
    b.declare_io()
    b.consts()
    b.phase_a1()
    if stop_after == "a1":
        b.dbg_dump("dbg_zupw_re", b.ZupW[0], [128, 8, 8, 128], BF16)
        b.dbg_dump("dbg_zupw_im", b.ZupW[1], [128, 8, 8, 128], BF16)
        b.dbg_dump("dbg_carw_re", b.CarW[0], [128, 32, 8, 32], BF16)
        b.dbg_dump("dbg_carw_nim", b.CarW[1], [128, 32, 8, 32], BF16)
        b.dbg_dump("dbg_bd", b.BD, [128, 8, 8, 128], BF16)
        b.dbg_dump("dbg_rch", b.Rch, [128, 32], F32)
        b.dbg_dump("dbg_f8", b.f8, [128, 32], F32)
        b.sc.barrier()
        return b.finish()
    b.phase_a0()
    if stop_after == "a0":
        b.dbg_dump("dbg_UT", b.UT, [128, 8, 8, 512], BF16)
        b.sc.barrier()
        return b.finish()
    b.phase_a2()
    if stop_after == "a2":
        b.dbg_dump("dbg_UT", b.UT, [128, 8, 8, 512], BF16)
        b.sc.barrier()
        return b.finish()
    b.ar.reset(b.ut_top)
    b.phase_b1()
    b.ar.off = 0
    b.consts()
    b.sc.barrier()
    if stop_after == "b1":
        b.dbg_copy("dbg_H1", b.H1, [S, D], F32)
        b.sc.barrier()
        return b.finish()
    b.H2 = b.dscr("H2", [S, D], F32)
    b.H2T = b.dscr("H2T", [128, 8, S], BF16)
    b.phase_ffn(0, b.H1, b.H1T, b.H2, b.H2T)
    if stop_after == "b2":
        b.dbg_copy("dbg_H2", b.H2, [S, D], F32)
        b.sc.barrier()
        return b.finish()
    b.phase_c0()
    if stop_after == "c0":
        b.dbg_dump("dbg_ckvT", b.ckvT, [128, 2, S], BF16)
        b.dbg_dump("dbg_kropeT", b.kropeT, [128, S], BF16)
        b.dbg_dump("dbg_cqT", b.cqT, [128, 3, S], BF16)
        b.dbg_dump("dbg_cosT", b.cosT, [128, NT, 32], F32)
        b.dbg_dump("dbg_sinT", b.sinT, [128, NT, 32], F32)
        b.sc.barrier()
        return b.finish()
    b.phase_c()
    if stop_after == "c":
        b.dbg_copyT("dbg_OT", b.OT)
        b.sc.barrier()
        return b.finish()
    b.phase_c2()
    if stop_after == "c2":
        b.dbg_copy("dbg_H3", b.H3, [S, D], F32)
        b.sc.barrier()
        return b.finish()
    b.phase_ffn(1, b.H3, b.H3T, b.out, None)
    return b.finish()


WEIGHT_NAMES = ["ln_mix_g", "ln_mix_b", "ln_ffn_g", "ln_ffn_b", "w_ff1", "w_ff2", "ssm_lam_re", "ssm_lam_im",
                "ssm_log_dt", "ssm_b_re", "ssm_b_im", "ssm_c_re", "ssm_c_im", "ssm_d", "ssm_w_glu", "ssm_w_out",
                "kv_w_a", "kv_norm_g", "kv_w_b", "q_w_a", "q_norm_g", "q_w_b", "attn_w_o"]


def make_in_maps(inputs, n_cores=8):
    f = lambda a: np.ascontiguousarray(np.asarray(a))
    shared = {
        "ln_mix_g": f(inputs["ln_mix_g"]), "ln_mix_b": f(inputs["ln_mix_b"]),
        "ln_ffn_g": f(inputs["ln_ffn_g"]), "ln_ffn_b": f(inputs["ln_ffn_b"]),
        "w_ff1": f(inputs["w_ff1"]), "w_ff2": f(inputs["w_ff2"]),
        "ssm_lam_re": f(inputs["ssm_lam_re"])[0], "ssm_lam_im": f(inputs["ssm_lam_im"])[0],
        "ssm_log_dt": f(inputs["ssm_log_dt"]).reshape(1, 64),
        "ssm_b_re": f(inputs["ssm_b_re"])[0], "ssm_b_im": f(inputs["ssm_b_im"])[0],
        "ssm_c_re": f(inputs["ssm_c_re"])[0], "ssm_c_im": f(inputs["ssm_c_im"])[0],
        "ssm_d": f(inputs["ssm_d"])[0], "ssm_w_glu": f(inputs["ssm_w_glu"])[0],
        "ssm_w_out": f(inputs["ssm_w_out"])[0],
        "kv_w_a": f(inputs["kv_w_a"]), "kv_norm_g": f(inputs["kv_norm_g"]).reshape(1, 256),
        "kv_w_b": f(inputs["kv_w_b"]),
        "q_w_a": f(inputs["q_w_a"])[0], "q_norm_g": f(inputs["q_norm_g"]).reshape(1, 384),
        "q_w_b": f(inputs["q_w_b"])[0], "attn_w_o": f(inputs["attn_w_o"])[0],
    }
    x = f(inputs["x"])
    pos = f(inputs["positions"]).astype(np.int32)
    maps = []
    for c in range(n_cores):
        m = dict(shared)
        m["x"] = x[c]
        m["positions"] = pos[c].reshape(NT, 128)
        maps.append(m)
    return maps


def kernel(**inputs):
    nc = build()
    maps = make_in_maps(inputs)
    res = run_bass_kernel_spmd(nc, maps, core_ids=list(range(8)))
    return np.stack([np.asarray(r["out"]) for r in res.results], axis=0).astype(np.float32)
```
